# Optimizing a Trainium2 kernel written in Bass

```python
import math
import jax
import jax.numpy as jnp
from jax import lax
import numpy as np

D_MODEL = 1024
BATCH = 32
SEQ = 2048
DEPTH = 1
DEC_BATCH = 32
DEC_SEQ = 64
PAST_LEN = 1024

CHUNK = 64
Q_BLOCK = 128
ROPE_THETA = 10000.0
EPS = 1e-5
F32 = jnp.float32
N_HEADS_A = 8
HEAD_DIM_A = D_MODEL // (2 * N_HEADS_A)
V_DIM_A = 2 * HEAD_DIM_A
ATTN_WIDTH = N_HEADS_A * V_DIM_A
SCALE_A = HEAD_DIM_A ** -0.5
EXPAND = 2
D_INNER = EXPAND * D_MODEL
HEAD_DIM_S = 64
N_HEADS_S = D_INNER // HEAD_DIM_S
N_GROUPS_S = 4
HEADS_PER_GROUP = N_HEADS_S // N_GROUPS_S
D_STATE = 128
CONV_W = 4
CONV_DIM = D_INNER + 2 * N_GROUPS_S * D_STATE
D_FF = 256 * (-(-(8 * D_MODEL) // (3 * 256)))
OFF_K = 2 * N_HEADS_A * HEAD_DIM_A
OFF_V = OFF_K + 2 * N_HEADS_A * HEAD_DIM_A
OFF_Z = OFF_V + ATTN_WIDTH
OFF_XBC = OFF_Z + D_INNER
OFF_DT = OFF_XBC + CONV_DIM
OFF_GATE = OFF_DT + N_HEADS_S
IN_COLS = OFF_GATE + 2 * D_MODEL

kernel_name = "diffattn_ssd_gated_hybrid_stream_step"


def rmsnorm(x, g):
    xf = x.astype(F32)
    xf = xf * lax.rsqrt(jnp.mean(xf * xf, axis=-1, keepdims=True) + EPS)
    return (xf * g.astype(F32)).astype(x.dtype)


def rope(x, pos):
    half = x.shape[-1] // 2
    inv = ROPE_THETA ** (-jnp.arange(half, dtype=F32) / half)
    ang = pos.astype(F32)[:, None] * inv[None, :]
    cos = jnp.cos(ang)[None, :, None, :]
    sin = jnp.sin(ang)[None, :, None, :]
    xf = x.astype(F32)
    x1, x2 = xf[..., :half], xf[..., half:]
    return jnp.concatenate([x1 * cos - x2 * sin, x2 * cos + x1 * sin], axis=-1).astype(x.dtype)


def diff_combine(probs, v, lam):
    b, _, tq, tk = probs.shape
    p = probs.reshape(b, N_HEADS_A, 2, tq, tk)
    w = p[:, :, 0] - lam * p[:, :, 1]
    return jnp.einsum("bhqk,bkhe->bqhe", w, v.astype(F32))


def diff_attn_prompt(q, k, v, lam):
    b, t = q.shape[:2]
    nb = t // Q_BLOCK
    qb = q.reshape(b, nb, Q_BLOCK, 2 * N_HEADS_A, HEAD_DIM_A).swapaxes(0, 1)
    key_chunk = jnp.arange(t) // CHUNK

    def one_block(args):
        qi, blk = args
        q_chunk = (blk * Q_BLOCK + jnp.arange(Q_BLOCK)) // CHUNK
        mask = key_chunk[None, :] <= q_chunk[:, None]
        s = jnp.einsum("bqhd,bkhd->bhqk", qi, k, preferred_element_type=F32) * SCALE_A
        s = jnp.where(mask[None, None], s, -jnp.inf)
        return diff_combine(jax.nn.softmax(s, axis=-1), v, lam)

    out = lax.map(one_block, (qb, jnp.arange(nb)))
    return out.swapaxes(0, 1).reshape(b, t, N_HEADS_A, V_DIM_A)


def diff_attn_full(q, k, v, lam):
    s = jnp.einsum("bqhd,bkhd->bhqk", q, k, preferred_element_type=F32) * SCALE_A
    return diff_combine(jax.nn.softmax(s, axis=-1), v, lam)


def causal_conv(u, buf, w, bias):
    t = u.shape[1]
    full = jnp.concatenate([buf.astype(u.dtype), u], axis=1)
    out = bias.astype(u.dtype) + full[:, 0:t] * w[0]
    for j in range(1, CONV_W):
        out = out + full[:, j:j + t] * w[j]
    return out, full[:, -(CONV_W - 1):].astype(buf.dtype)


def ssd_scan(xdt, dA, bm, cm, s0, chunk):
    b, t = xdt.shape[:2]
    nc = t // chunk

    def to_chunks(a):
        a = a.astype(F32)
        return a.reshape((b, nc, chunk) + a.shape[2:]).swapaxes(0, 1)

    xs = (to_chunks(xdt.reshape(b, t, N_GROUPS_S, HEADS_PER_GROUP, HEAD_DIM_S)),
          to_chunks(dA.reshape(b, t, N_GROUPS_S, HEADS_PER_GROUP)),
          to_chunks(bm), to_chunks(cm))
    causal = jnp.tril(jnp.ones((chunk, chunk), dtype=bool))[None, :, :, None, None]

    def step(state, inp):
        xc, ac, bc, cc = inp
        cum = jnp.cumsum(ac, axis=1)
        seg = cum[:, :, None] - cum[:, None, :]
        decay = jnp.exp(jnp.where(causal, seg, -jnp.inf))
        cb = jnp.einsum("blgn,bsgn->blsg", cc, bc)
        y_diag = jnp.einsum("blsg,blsgk,bsgkp->blgkp", cb, decay, xc)
        y_off = jnp.einsum("blgn,bgkpn->blgkp", cc, state) * jnp.exp(cum)[..., None]
        to_end = jnp.exp(cum[:, -1:] - cum)
        new_state = (state * jnp.exp(cum[:, -1])[..., None, None]
                     + jnp.einsum("bsgn,bsgk,bsgkp->bgkpn", bc, to_end, xc))
        return new_state, y_diag + y_off

    init = s0.astype(F32).reshape(b, N_GROUPS_S, HEADS_PER_GROUP, HEAD_DIM_S, D_STATE)
    s_fin, ys = lax.scan(step, init, xs)
    y = ys.swapaxes(0, 1).reshape(b, t, N_HEADS_S, HEAD_DIM_S)
    return y, s_fin.reshape(b, N_HEADS_S, HEAD_DIM_S, D_STATE)


def gated_group_rmsnorm(y, z, g):
    b, t, _ = y.shape
    u = (y.astype(F32) * jax.nn.silu(z.astype(F32))).reshape(b, t, N_GROUPS_S, D_INNER // N_GROUPS_S)
    u = u * lax.rsqrt(jnp.mean(u * u, axis=-1, keepdims=True) + EPS)
    return u.reshape(b, t, D_INNER) * g.astype(F32)


def trunk_layer(x, c, pos, k_past, v_past, conv_buf, ssm_state, lp, lam_init):
    (w_ada, b_ada, g_mix, g_ffn, w_in, lq1, lk1, lq2, lk2, g_subln, conv_w, conv_b,
     dt_bias, a_log, d_skip, g_ssd, w_pa, w_pb, w_out, w_gu, w_down) = lp
    b, t, _ = x.shape
    dt_x = x.dtype
    mod = jnp.einsum("bd,de->be", jax.nn.silu(c), w_ada) + b_ada
    sh_m, sc_m, gt_m, sh_f, sc_f, gt_f = jnp.split(mod[:, None, :], 6, axis=-1)

    h = rmsnorm(x, g_mix) * (1 + sc_m) + sh_m
    proj = jnp.einsum("btd,de->bte", h, w_in)
    q, k, v, z, xbc, dt_raw, gates = jnp.split(
        proj, [OFF_K, OFF_V, OFF_Z, OFF_XBC, OFF_DT, OFF_GATE], axis=-1)

    q = rope(q.reshape(b, t, 2 * N_HEADS_A, HEAD_DIM_A), pos)
    k = rope(k.reshape(b, t, 2 * N_HEADS_A, HEAD_DIM_A), pos)
    v = v.reshape(b, t, N_HEADS_A, V_DIM_A)
    lam = (jnp.exp(jnp.sum(lq1.astype(F32) * lk1.astype(F32)))
           - jnp.exp(jnp.sum(lq2.astype(F32) * lk2.astype(F32))) + lam_init)
    if k_past is None:
        o = diff_attn_prompt(q, k, v, lam)
    else:
        o = diff_attn_full(q, jnp.concatenate([k_past.astype(k.dtype), k], axis=1),
                           jnp.concatenate([v_past.astype(v.dtype), v], axis=1), lam)
    o = rmsnorm(o, g_subln) * (1.0 - lam_init)
    y_a = jnp.einsum("bte,ed->btd", o.reshape(b, t, ATTN_WIDTH).astype(dt_x), w_pa)

    xbc_c, conv_new = causal_conv(xbc, conv_buf, conv_w, conv_b)
    xbc_c = jax.nn.silu(xbc_c)
    xs, bm, cm = jnp.split(xbc_c, [D_INNER, D_INNER + N_GROUPS_S * D_STATE], axis=-1)
    xs = xs.reshape(b, t, N_HEADS_S, HEAD_DIM_S)
    bm = bm.reshape(b, t, N_GROUPS_S, D_STATE)
    cm = cm.reshape(b, t, N_GROUPS_S, D_STATE)
    dt = jax.nn.softplus(dt_raw.astype(F32) + dt_bias.astype(F32))
    a = -jnp.exp(a_log.astype(F32))
    y_s, ssm_new = ssd_scan(xs.astype(F32) * dt[..., None], dt * a, bm, cm, ssm_state, min(CHUNK, t))
    y_s = y_s + d_skip.astype(F32)[:, None] * xs.astype(F32)
    y_s = gated_group_rmsnorm(y_s.reshape(b, t, D_INNER), z, g_ssd).astype(dt_x)
    y_b = jnp.einsum("bte,ed->btd", y_s, w_pb)

    g_a, g_b = jnp.split(gates, 2, axis=-1)
    merged = jax.nn.sigmoid(g_a) * y_a + jax.nn.sigmoid(g_b) * y_b
    x = x + gt_m * jnp.einsum("btd,de->bte", merged, w_out)

    h2 = rmsnorm(x, g_ffn) * (1 + sc_f) + sh_f
    gate_ff, up_ff = jnp.split(jnp.einsum("btd,df->btf", h2, w_gu), 2, axis=-1)
    x = x + gt_f * jnp.einsum("btf,fd->btd", jax.nn.silu(gate_ff) * up_ff, w_down)
    return x.astype(dt_x), (k, v, conv_new, ssm_new.astype(ssm_state.dtype))


def setup_inputs(seed: int = 0) -> dict:
    key = jax.random.key(seed)
    ks = jax.random.split(key, 32)
    nrm = lambda k, shape, s: jax.random.normal(k, shape, F32) * s
    gain = lambda k, shape: 1.0 + 0.01 * jax.random.normal(k, shape, F32)
    u = jax.random.uniform(ks[20], (DEPTH, N_HEADS_S), F32)
    dt0 = jnp.exp(u * (math.log(0.1) - math.log(1e-3)) + math.log(1e-3))
    return {
        "x_prompt": nrm(ks[0], (BATCH, SEQ, D_MODEL), 1.0),
        "x_sample": nrm(ks[1], (DEC_BATCH, DEC_SEQ, D_MODEL), 1.0),
        "cache_k": nrm(ks[2], (DEPTH, DEC_BATCH, PAST_LEN, 2 * N_HEADS_A, HEAD_DIM_A), 1.0),
        "cache_v": nrm(ks[3], (DEPTH, DEC_BATCH, PAST_LEN, N_HEADS_A, V_DIM_A), 1.0),
        "state_conv": nrm(ks[4], (DEPTH, DEC_BATCH, CONV_W - 1, CONV_DIM), 1.0),
        "state_ssm": nrm(ks[5], (DEPTH, DEC_BATCH, N_HEADS_S, HEAD_DIM_S, D_STATE), 0.1),
        "c_prompt": nrm(ks[6], (BATCH, D_MODEL), 1.0),
        "c_sample": nrm(ks[7], (DEC_BATCH, D_MODEL), 1.0),
        "w_ada": nrm(ks[8], (DEPTH, D_MODEL, 6 * D_MODEL), 0.5 * D_MODEL ** -0.5),
        "b_ada": nrm(ks[9], (DEPTH, 6 * D_MODEL), 0.01),
        "g_mix": gain(ks[10], (DEPTH, D_MODEL)),
        "g_ffn": gain(ks[11], (DEPTH, D_MODEL)),
        "w_in": nrm(ks[12], (DEPTH, D_MODEL, IN_COLS), D_MODEL ** -0.5),
        "lambda_q1": nrm(ks[13], (DEPTH, HEAD_DIM_A), 0.1),
        "lambda_k1": nrm(ks[14], (DEPTH, HEAD_DIM_A), 0.1),
        "lambda_q2": nrm(ks[15], (DEPTH, HEAD_DIM_A), 0.1),
        "lambda_k2": nrm(ks[16], (DEPTH, HEAD_DIM_A), 0.1),
        "g_subln": gain(ks[17], (DEPTH, V_DIM_A)),
        "conv_w": nrm(ks[18], (DEPTH, CONV_W, CONV_DIM), CONV_W ** -0.5),
        "conv_b": nrm(ks[19], (DEPTH, CONV_DIM), 0.01),
        "dt_bias": dt0 + jnp.log(-jnp.expm1(-dt0)),
        "a_log": jnp.log(jax.random.uniform(ks[21], (DEPTH, N_HEADS_S), F32, 1.0, 16.0)),
        "d_skip": gain(ks[22], (DEPTH, N_HEADS_S)),
        "g_ssd": gain(ks[23], (DEPTH, D_INNER)),
        "w_pa": nrm(ks[24], (DEPTH, ATTN_WIDTH, D_MODEL), ATTN_WIDTH ** -0.5),
        "w_pb": nrm(ks[25], (DEPTH, D_INNER, D_MODEL), D_INNER ** -0.5),
        "w_out": nrm(ks[26], (DEPTH, D_MODEL, D_MODEL), D_MODEL ** -0.5),
        "w_gu": nrm(ks[27], (DEPTH, D_MODEL, 2 * D_FF), D_MODEL ** -0.5),
        "w_down": nrm(ks[28], (DEPTH, D_FF, D_MODEL), D_FF ** -0.5),
        "g_final": gain(ks[29], (D_MODEL,)),
    }


def reference(x_prompt, x_sample, cache_k, cache_v, state_conv, state_ssm, c_prompt, c_sample,
              w_ada, b_ada, g_mix, g_ffn, w_in, lambda_q1, lambda_k1, lambda_q2, lambda_k2, g_subln,
              conv_w, conv_b, dt_bias, a_log, d_skip, g_ssd, w_pa, w_pb, w_out, w_gu, w_down, g_final):
    bp, tp = x_prompt.shape[:2]
    ts = x_sample.shape[1]
    past = cache_k.shape[2]
    pos_p = jnp.arange(tp)
    pos_s = past + jnp.arange(ts)
    xp, xs = x_prompt, x_sample
    kp_l, vp_l, cp_l, sp_l, ks_l, vs_l, cs_l, ss_l = [], [], [], [], [], [], [], []
    for layer in range(DEPTH):
        lp = (w_ada[layer], b_ada[layer], g_mix[layer], g_ffn[layer], w_in[layer],
              lambda_q1[layer], lambda_k1[layer], lambda_q2[layer], lambda_k2[layer], g_subln[layer],
              conv_w[layer], conv_b[layer], dt_bias[layer], a_log[layer], d_skip[layer], g_ssd[layer],
              w_pa[layer], w_pb[layer], w_out[layer], w_gu[layer], w_down[layer])
        lam_init = 0.8 - 0.6 * math.exp(-0.3 * layer)
        zero_conv = jnp.zeros((bp, CONV_W - 1, CONV_DIM), x_prompt.dtype)
        zero_ssm = jnp.zeros((bp, N_HEADS_S, HEAD_DIM_S, D_STATE), x_prompt.dtype)
        xp, (k_p, v_p, c_p, s_p) = trunk_layer(xp, c_prompt, pos_p, None, None, zero_conv, zero_ssm, lp, lam_init)
        xs, (k_s, v_s, c_s, s_s) = trunk_layer(xs, c_sample, pos_s, cache_k[layer], cache_v[layer],
                                               state_conv[layer], state_ssm[layer], lp, lam_init)
        kp_l.append(k_p); vp_l.append(v_p); cp_l.append(c_p); sp_l.append(s_p)
        ks_l.append(k_s); vs_l.append(v_s); cs_l.append(c_s); ss_l.append(s_s)
    y_prompt = rmsnorm(xp, g_final)
    y_sample = rmsnorm(xs, g_final)
    new_k_prompt = jnp.stack(kp_l)
    new_v_prompt = jnp.stack(vp_l)
    new_conv_prompt = jnp.stack(cp_l)
    new_ssm_prompt = jnp.stack(sp_l)
    new_k_sample = jnp.stack(ks_l)
    new_v_sample = jnp.stack(vs_l)
    new_conv_sample = jnp.stack(cs_l)
    new_ssm_sample = jnp.stack(ss_l)
    return (y_prompt, y_sample, new_k_prompt, new_v_prompt, new_conv_prompt, new_ssm_prompt,
            new_k_sample, new_v_sample, new_conv_sample, new_ssm_sample)
```

```python
import math
from contextlib import ExitStack

import numpy as np
import ml_dtypes

import concourse.bass as bass
import concourse.mybir as mybir
from concourse.bass_utils import run_bass_kernel_spmd

F32 = mybir.dt.float32
BF16 = mybir.dt.bfloat16
AF = mybir.ActivationFunctionType
ALU = mybir.AluOpType

NCORES = 8
NSEQ = 4
D = 1024
TP = 2048
TS = 64
PAST = 1024
NKS = PAST + TS
OFF_K, OFF_V, OFF_Z, OFF_XBC, OFF_DT, OFF_GATE, IN_COLS = 1024, 2048, 3072, 5120, 8192, 8224, 10272
DFF = 2816
EPS = 1e-5
SCALE_A = 0.125
LAM_INIT = 0.8 - 0.6 * math.exp(-0.3 * 0)
NSLOT = 5
ENGS = ("sync", "scalar", "vector", "gpsimd", "tensor")
STOP_AFTER = 99
CONST_KEYS = frozenset(("ident_bf", "ident_f", "ones_bf", "ones_f", "uincl", "ustrict", "ustrict_bf", "causal_bf",
                        "cosP", "sinP", "cosS", "sinS", "b_ada_col", "gmix_col", "gffn_col", "gsub_col", "gssd_col",
                        "convw", "convb", "dtb_b", "alog_b", "dskip_b", "a_b", "gfin_b", "neglam", "gsubs", "mhalf",
                        "modT", "Gm", "Gf", "wdt", "lams", "scT", "cTs"))
SAME_ENG_ORDER_ONLY = True
SCHED = True
RAW_ONLY = False
RAW_KEEP = ()
LIM = {"np": NSEQ, "ns": NSEQ, "nb": 99, "setup_only": False}
DBG = {"kinds": "kvq", "vscr": True, "kscr": True, "vout": True, "pool_bm": True, "pool_x": True, "pool_res": True,
       "pool_merge": True, "pool_rstd": True, "pool_u": True, "pool_s": True, "pool_sq": False}


class Op:
    __slots__ = ("eng", "fn", "deps", "signal", "sig_val", "dma_slot", "dma_val", "idx", "cost", "odeps", "fin",
                 "nun", "users")

    def __init__(self, eng, fn):
        self.eng = eng
        self.fn = fn
        self.deps = []
        self.odeps = []
        self.cost = 0.3
        self.signal = False
        self.sig_val = 0
        self.dma_slot = None
        self.dma_val = 0


class Prog:
    def __init__(self):
        self.ops = {e: [] for e in ENGS}
        self.res = {}
        self.dma_cnt = {}
        self.all = []

    def add(self, eng, fn, reads=(), writes=(), dma=None, cost=0.3):
        op = Op(eng, fn)
        op.cost = cost
        op.idx = len(self.all)
        self.all.append(op)
        deps = []
        for k in reads:
            st = self.res.get(k)
            if st is not None:
                deps.extend(st[0])
                if isinstance(k, tuple) and k[0] == "ps":
                    deps.extend(r for r in st[1] if r.eng != eng)
        n_raw = len(deps)
        for k in writes:
            st = self.res.get(k)
            if st is not None and not (RAW_ONLY and not (isinstance(k, tuple) and k[0] in RAW_KEEP)):
                deps.extend(st[1] if st[1] else st[0])
        for k in reads:
            if k in CONST_KEYS:
                continue
            st = self.res.setdefault(k, [[], []])
            if dma is None:
                keep = []
                for r in st[1]:
                    if r.dma_slot is not None or r.eng != eng:
                        keep.append(r)
                    elif r is not op:
                        op.odeps.append(r)
                st[1] = keep
            st[1].append(op)
        for k in writes:
            st = self.res.setdefault(k, [[], []])
            if st[1]:
                st[0] = [op]
                st[1] = []
            else:
                if dma is None:
                    keep = []
                    for w in st[0]:
                        if w.dma_slot is not None or w.eng != eng:
                            keep.append(w)
                        elif w is not op:
                            op.odeps.append(w)
                    st[0] = keep
                st[0].append(op)
        seen = set()
        raw_ids = set()
        if SAME_ENG_ORDER_ONLY:
            for k in reads:
                st = self.res.get(k)
            raw_ids = set(id(d) for d in deps[:n_raw])
        for d in deps:
            if d is op or id(d) in seen:
                continue
            seen.add(id(d))
            if d.dma_slot is None and d.eng == "tensor" and eng == "tensor" and dma is None:
                op.odeps.append(d)
                continue
            if (SAME_ENG_ORDER_ONLY and dma is None and d.dma_slot is None and d.eng == eng
                    and eng in ("vector", "scalar") and id(d) not in raw_ids):
                op.odeps.append(d)
                continue
            op.deps.append(d)
            if d.dma_slot is None:
                d.signal = True
        if dma is not None:
            op.dma_slot = dma
            c = self.dma_cnt.get(dma, 0) + 1
            self.dma_cnt[dma] = c
            op.dma_val = 16 * c
        self.ops[eng].append(op)
        return op

    def schedule(self, window=120):
        import heapq
        for op in self.all:
            op.fin = None
            op.users = []
        inorder = ("sync",)
        prev = None
        for op in self.ops["gpsimd"]:
            if op.dma_slot is not None:
                if prev is not None:
                    op.odeps.append(prev)
                prev = op
        for op in self.all:
            ds = op.deps + op.odeps
            op.nun = len(ds)
            for d in ds:
                d.users.append(op)
        rem = {e: list(self.ops[e]) for e in ENGS}
        ptr = {e: 0 for e in ENGS}
        win = {e: [] for e in ENGS}
        nxt = {e: 0 for e in ENGS}
        etime = {e: 0.0 for e in ENGS}
        order = {e: [] for e in ENGS}
        LAT = 0.25

        def ready_time(op):
            t = 0.0
            for d in op.deps:
                lt = d.fin + (LAT if (d.eng != op.eng or d.dma_slot is not None) else 0.08)
                if lt > t:
                    t = lt
            for d in op.odeps:
                if d.fin - d.cost * 0.5 > t:
                    t = d.fin - d.cost * 0.5
            return t

        def refill(e):
            w = win[e]
            while len(w) < window and nxt[e] < len(rem[e]):
                w.append(rem[e][nxt[e]])
                nxt[e] += 1

        total = len(self.all)
        done = 0
        for e in ENGS:
            if e not in inorder:
                refill(e)
        while done < total:
            best = None
            for e in ENGS:
                if e in inorder:
                    if ptr[e] >= len(rem[e]):
                        continue
                    op = rem[e][ptr[e]]
                    if op.nun > 0:
                        continue
                    st = max(etime[e], ready_time(op))
                    cand = (st, op.idx, e, op)
                else:
                    cand = None
                    for op in win[e]:
                        if op.nun > 0:
                            continue
                        st = max(etime[e], ready_time(op))
                        c = (st, op.idx, e, op)
                        if cand is None or c < cand:
                            cand = c
                    if cand is None:
                        continue
                if best is None or cand < best:
                    best = cand
            assert best is not None, "scheduler stuck"
            st, _, e, op = best
            if op.dma_slot is not None:
                issue = 1.0 if e == "gpsimd" else 0.15
                etime[e] = st + issue
                op.fin = st + issue + 2.0 + op.cost
            else:
                etime[e] = st + op.cost
                op.fin = st + op.cost
            order[e].append(op)
            if e in inorder:
                ptr[e] += 1
            else:
                win[e].remove(op)
                refill(e)
            for u in op.users:
                u.nun -= 1
            done += 1
        self.ops = order
        self.makespan = max(etime.values())

    def emit(self, nc):
        for e in ENGS:
            c = 0
            for op in self.ops[e]:
                if op.dma_slot is None and op.signal:
                    c += 1
                    op.sig_val = c
        with ExitStack() as es:
            esem = {e: es.enter_context(nc.semaphore("s_" + e)) for e in ENGS}
            dsem = {}
            for i, slot in enumerate(self.dma_cnt):
                dsem[slot] = es.enter_context(nc.semaphore("d%d" % i))
            block = es.enter_context(nc.Block())
            prog = self

            def body(ename):
                def _f(eng):
                    seen = {}
                    for op in prog.ops[ename]:
                        for d in op.deps:
                            if d.dma_slot is not None:
                                sem, val = dsem[d.dma_slot], d.dma_val
                            else:
                                sem, val = esem[d.eng], d.sig_val
                            if seen.get(id(sem), 0) >= val:
                                continue
                            seen[id(sem)] = val
                            eng.wait_ge(sem, val)
                        inst = op.fn(eng)
                        if op.dma_slot is not None:
                            inst.then_inc(dsem[op.dma_slot], 16)
                        elif op.signal:
                            inst.then_inc(esem[ename], 1)
                    if ename == "sync":
                        for slot, c in prog.dma_cnt.items():
                            if seen.get(id(dsem[slot]), 0) < 16 * c:
                                eng.wait_ge(dsem[slot], 16 * c)
                return _f

            block.sync(body("sync"))
            block.scalar(body("scalar"))
            block.vector(body("vector"))
            block.gpsimd(body("gpsimd"))
            block.tensor(body("tensor"))


class WStream:
    def __init__(self, P, wsl, plan):
        self.P = P
        self.wsl = wsl
        self.plan = plan
        self.req = []
        self.issued = 0

    def _issue(self, i, src, kc, ncol, skey):
        s = i % NSLOT
        dst = self.wsl[:, s, 0:kc, 0:ncol]
        self.P.add("sync", lambda e: e.dma_start(out=dst, in_=src), reads=list(skey), writes=[("w", s)],
                   dma=("w", s), cost=kc * ncol * 256 / 150e3)

    def get(self, src, kc, ncol, skey):
        i = len(self.req)
        self.req.append((src, kc, ncol, skey))
        if self.plan is None:
            self._issue(i, src, kc, ncol, skey)
        else:
            lim = min(i + NSLOT - 1, len(self.plan) - 1)
            while self.issued <= lim:
                self._issue(self.issued, *self.plan[self.issued])
                self.issued += 1
        s = i % NSLOT
        return self.wsl[:, s], ("w", s)


def wkeys(wn, r0, nrows, c0, ncols):
    return [("wb", wn, r, c) for r in range(r0 // 1024, (r0 + nrows - 1) // 1024 + 1)
            for c in range(c0 // 2048, (c0 + ncols - 1) // 2048 + 1)]


class Builder:
    def __init__(self):
        self.nc = nc = bass.Bass("TRN2", target_bir_lowering=False)
        self.es = ExitStack()
        self.t = {}
        self.dr = {}
        di = lambda n, s, d=F32: self.dr.__setitem__(n, nc.dram_tensor(n, list(s), d, kind="ExternalInput").ap())
        do = lambda n, s, d=F32: self.dr.__setitem__(n, nc.dram_tensor(n, list(s), d, kind="ExternalOutput").ap())
        ds = lambda n, s, d=BF16: self.dr.__setitem__(n, nc.dram_tensor(n, list(s), d, kind="Internal").ap())
        di("xp", [NSEQ, TP, D]); di("xs", [NSEQ, TS, D])
        di("ck", [NSEQ, PAST, D]); di("cv", [NSEQ, PAST, D])
        di("scT", [NSEQ, 128, 24, 3]); di("ssmT", [NSEQ, 128, 2048])
        di("cT", [128, 8, 8])
        di("w_ada", [D, 6 * D]); di("b_ada_col", [128, 48]); di("gmix_col", [128, 8]); di("gffn_col", [128, 8])
        di("w_in", [D, IN_COLS]); di("lam4", [4 * 64]); di("gsub_col", [128, 1])
        di("convw_col", [128, 24, 4]); di("convb_col", [128, 24])
        di("dt_bias", [32]); di("a_log", [32]); di("d_skip", [32]); di("gssd_col", [128, 16])
        di("w_pa", [D, D]); di("w_pb", [2 * D, D]); di("w_out", [D, D]); di("w_gu", [D, 2 * DFF]); di("w_down", [DFF, D])
        di("g_final", [D])
        di("ident_bf", [128, 128], BF16); di("ident_f", [128, 128]); di("ones_bf", [128, 128], BF16)
        di("ones_f", [128, 128]); di("uincl", [128, 128]); di("ustrict", [128, 128]); di("causal_bf", [128, 128], BF16)
        di("cosP", [128, 16, 32]); di("sinP", [128, 16, 32]); di("cosS", [64, 32]); di("sinS", [64, 32])
        do("y_p", [NSEQ, TP, D]); do("y_s", [NSEQ, TS, D])
        do("k_p", [NSEQ, TP, D]); do("v_p", [NSEQ, TP, D])
        do("conv_p", [NSEQ, 128, 24, 3]); do("ssm_p", [NSEQ, 128, 2048])
        do("k_s", [NSEQ, TS, D]); do("v_s", [NSEQ, TS, D])
        do("conv_s", [NSEQ, 128, 24, 3]); do("ssm_s", [NSEQ, 128, 2048])
        ds("wb_ada", [D, 6 * D]); ds("wb_in", [D, IN_COLS]); ds("wb_pa", [D, D]); ds("wb_pb", [2 * D, D])
        ds("wb_out", [D, D]); ds("wb_gu", [D, 2 * DFF]); ds("wb_down", [DFF, D])
        ds("kscr", [2 * NSEQ, 128, 8, TP]); ds("vscr", [2 * NSEQ, 8, TP, 128])

        sb = lambda n, s, d=F32: self.t.__setitem__(n, self.es.enter_context(nc.sbuf_tensor("sb_" + n, list(s), d)))
        sb("ident_bf", [128, 128], BF16); sb("ident_f", [128, 128]); sb("ones_bf", [128, 128], BF16)
        sb("ones_f", [128, 128]); sb("uincl", [128, 128]); sb("ustrict", [128, 128]); sb("causal_bf", [128, 128], BF16)
        sb("ustrict_bf", [128, 128], BF16)
        sb("cosP", [128, 16, 32]); sb("sinP", [128, 16, 32]); sb("cosS", [64, 32]); sb("sinS", [64, 32])
        sb("b_ada_col", [128, 48]); sb("gmix_col", [128, 8]); sb("gffn_col", [128, 8]); sb("gsub_col", [128, 1])
        sb("convw", [128, 24, 4]); sb("convb", [128, 24]); sb("gssd_col", [128, 16])
        sb("dtb_b", [128, 32]); sb("alog_b", [128, 32]); sb("dskip_b", [128, 32]); sb("a_b", [128, 32])
        sb("gfin_b", [128, D]); sb("lams", [128, 4])
        sb("neglam", [128, 1]); sb("gsubs", [128, 1]); sb("mhalf", [128, 1])
        sb("cTs", [128, 8, 8]); sb("scT", [128, 8, 8], BF16)
        sb("modT", [128, 48, 8]); sb("Gm", [128, 8, 8]); sb("Gf", [128, 8, 8]); sb("onep", [128, 8, 8])
        sb("gb", [128, 128]); sb("gtm_b", [128, D]); sb("gtf_b", [128, D])
        sb("wdt", [128, 8, 32], BF16)
        sb("wsl", [128, NSLOT, 8, 512], BF16)
        sb("S", [128, 2048]); sb("Sb", [128, 2048], BF16); sb("tail", [128, 24, 3])
        sb("H", [128, 5, 8, 512], BF16)
        sb("TM", [128, 4, D]); sb("TB", [128, 4, D], BF16)
        sb("kTst", [128, 2, 4, 128], BF16)
        sb("kTj", [128, 2, TP], BF16); sb("Vj", [128, 2, 16, 128], BF16)
        sb("PT", [128, 4, 512], BF16)
        sb("Ft", [128, 6, 512]); sb("sq", [128, 512], BF16)
        sb("raw", [128, 2, 516])
        sb("xsT", [128, 4, 512], BF16); sb("xstok", [128, 4, 512], BF16); sb("Btok", [128, 4, 4, 128], BF16)
        sb("E", [128, 1024], BF16); sb("MT", [128, 1024], BF16); sb("cbm", [128, 128], BF16)
        sb("xdt", [128, 512], BF16); sb("xw", [128, 512], BF16); sb("xsd", [128, 512], BF16); sb("utok", [128, 512], BF16)
        sb("sm", [128, 4, 8, 32])
        sb("st", [128, 16])
        sb("jk", [128, 16], BF16)
        self.t["lamv"] = self.t["Ft"][:, 0, 0:256].rearrange("p (a b) -> p a b", b=64)
        self.t["lamt"] = self.t["Ft"][:, 1, 0:128].rearrange("p (a b) -> p a b", b=64)
        self.t["junk"] = self.t["E"]
        self.PS = [self.es.enter_context(nc.psum_tensor("ps%d" % i, [128, 512], F32)) for i in range(8)]
        self.st_i = 0

    def op(self, eng, meth, reads, writes, *a, **kw):
        o = kw.get("out", a[0] if a else None)
        n = 1
        for d in o.shape[1:]:
            n *= d
        if eng == "tensor":
            if meth == "transpose":
                cost = 0.06
            else:
                r = kw["rhs"]
                n = 1
                for d in r.shape[1:]:
                    n *= d
                cost = (max(n, 64) / 2.2 + 15) * 1e-3 * (4.0 if r.dtype == F32 else 1.0)
        elif eng == "scalar":
            cost = 0.2 + 0.6e-3 * n
        elif eng == "gpsimd":
            cost = 0.25 + 1.5e-3 * n
        else:
            cost = 0.1 + 1.1e-3 * n * (8.0 if meth == "reciprocal" else (1.7 if meth == "reciprocal_approx_fast" else 1.0))
        return self.P.add(eng, lambda e: getattr(e, meth)(*a, **kw), reads, writes, cost=cost)

    def dma(self, eng, out, in_, reads, writes, slot, **kw):
        n = 1
        for d in out.shape:
            n *= d
        nbytes = n * (2 if out.dtype == BF16 else 4)
        return self.P.add(eng, lambda e: e.dma_start(out=out, in_=in_, **kw), reads, writes, dma=slot,
                          cost=nbytes / 150e3)

    def mm(self, out, lhsT, rhs, start, stop, reads, writes):
        return self.op("tensor", "matmul", reads, writes, out, lhsT=lhsT, rhs=rhs, start=start, stop=stop)

    def tr(self, out, in_, ident, reads, writes):
        return self.op("tensor", "transpose", reads, writes, out=out, in_=in_, identity=ident)

    def act(self, out, in_, func, reads, writes, **kw):
        return self.op("scalar", "activation", reads, writes, out=out, in_=in_, func=func, **kw)

    def tt(self, out, in0, in1, op, reads, writes, eng="vector"):
        return self.op(eng, "tensor_tensor", reads, writes, out=out, in0=in0, in1=in1, op=op)

    def ts(self, out, in0, s1, s2, op0, op1, reads, writes, eng="vector"):
        if op1 is None:
            return self.op(eng, "tensor_scalar", reads, writes, out=out, in0=in0, scalar1=s1, scalar2=None, op0=op0)
        return self.op(eng, "tensor_scalar", reads, writes, out=out, in0=in0, scalar1=s1, scalar2=s2, op0=op0, op1=op1)

    def stt(self, out, in0, scalar, in1, op0, op1, reads, writes):
        return self.op("vector", "scalar_tensor_tensor", reads, writes, out=out, in0=in0, scalar=scalar, in1=in1,
                       op0=op0, op1=op1)

    def stcol(self):
        i = self.st_i % 16
        self.st_i += 1
        return ("st", i), self.t["st"][:, i:i + 1]

    def jkcol(self):
        i = self.st_i % 16
        return ("jk", i), self.t["jk"][:, i:i + 1]

    def psb(self, i):
        return self.PS[i][:].bitcast(BF16)

    def record(self, P, plan):
        self.P = P
        self.ws = WStream(P, self.t["wsl"], plan)
        self.st_i = 0
        self.tmh_i = 0
        self.setup()
        if LIM["setup_only"]:
            return
        for s in range(LIM["np"]):
            self.sequence(s, prompt=True)
        for s in range(LIM["ns"]):
            self.sequence(s, prompt=False)

    def setup(self):
        t, dr = self.t, self.dr
        for n in ("ident_bf", "ident_f", "ones_bf", "ones_f", "uincl", "ustrict", "causal_bf", "cosP", "sinP",
                  "cosS", "sinS", "b_ada_col", "gmix_col", "gffn_col", "gsub_col", "gssd_col"):
            self.dma("sync", t[n][:], dr[n], [], [n], ("c", n))
        self.dma("sync", t["convw"][:], dr["convw_col"], [], ["convw"], ("c", "convw"))
        self.dma("sync", t["convb"][:], dr["convb_col"], [], ["convb"], ("c", "convb"))
        self.dma("sync", t["cTs"][:], dr["cT"], [], ["cTs"], ("c", "cTs"))
        self.dma("sync", t["dtb_b"][:], dr["dt_bias"].partition_broadcast(128), [], ["dtb_b"], ("c", "dtb"))
        self.dma("sync", t["alog_b"][:], dr["a_log"].partition_broadcast(128), [], ["alog_b"], ("c", "alog"))
        self.dma("sync", t["dskip_b"][:], dr["d_skip"].partition_broadcast(128), [], ["dskip_b"], ("c", "dskip"))
        self.dma("sync", t["gfin_b"][:], dr["g_final"].partition_broadcast(128), [], ["gfin_b"], ("c", "gfin"))
        self.dma("sync", t["Ft"][:, 0, 0:256], dr["lam4"].partition_broadcast(128), [],
                 [("F", 0)], ("c", "lamv"))
        shapes = {"ada": (D, 6 * D), "in": (D, IN_COLS), "pa": (D, D), "pb": (2 * D, D), "out": (D, D),
                  "gu": (D, 2 * DFF), "down": (DFF, D)}

        self.cast_i = 0

        def cast(wn, cols=None):
            rows, ncols = shapes[wn]
            src = dr["w_" + wn]
            dst = dr["wb_" + wn]
            for c0 in (range(0, ncols, 2048) if cols is None else cols):
                c1 = min(ncols, c0 + 2048)
                for r0 in range(0, rows, 1024):
                    r1 = min(rows, r0 + 1024)
                    k = self.cast_i
                    self.cast_i += 1
                    self.dma("gpsimd", dst[r0:r1, c0:c1], src[r0:r1, c0:c1], ([("cc", k - 2)] if k >= 2 else []),
                             [("wb", wn, r0 // 1024, c0 // 2048), ("cc", k)],
                             ("wb", wn, r0 // 1024, c0 // 2048))
        cast("ada")
        cast("in", [0, 2048])
        cast("pa")
        cast("in", [4096, 6144, 8192, 10240])
        cast("pb"); cast("out"); cast("gu"); cast("down")
        lamv = t["lamv"]
        self.tt(t["lamt"][:, 0, :], lamv[:, 0, :], lamv[:, 1, :], ALU.mult, [("F", 0)], [("F", 1)])
        self.tt(t["lamt"][:, 1, :], lamv[:, 2, :], lamv[:, 3, :], ALU.mult, [("F", 0)], [("F", 1)])
        self.op("vector", "tensor_reduce", [("F", 1)], ["lams"], out=t["lams"][:, 0:2], in_=t["lamt"],
                axis=mybir.AxisListType.X, op=ALU.add)
        self.act(t["lams"][:, 2:4], t["lams"][:, 0:2], AF.Exp, ["lams"], ["lams"])
        self.tt(t["neglam"][:], t["lams"][:, 3:4], t["lams"][:, 2:3], ALU.subtract, ["lams"], ["neglam"])
        self.ts(t["neglam"][:], t["neglam"][:], -LAM_INIT, None, ALU.add, None, ["neglam"], ["neglam"])
        self.ts(t["gsubs"][:], t["gsub_col"][:], 1.0 - LAM_INIT, None, ALU.mult, None, ["gsub_col"], ["gsubs"])
        self.op("vector", "memset", [], ["mhalf"], t["mhalf"][:], -0.5)
        self.op("vector", "tensor_copy", ["ustrict"], ["ustrict_bf"], out=t["ustrict_bf"][:], in_=t["ustrict"][:])
        self.ts(t["convw"][:], t["convw"][:], 0.5, None, ALU.mult, None, ["convw"], ["convw"])
        self.ts(t["convb"][:], t["convb"][:], 0.5, None, ALU.mult, None, ["convb"], ["convb"])
        self.act(t["a_b"][:], t["alog_b"][:], AF.Exp, ["alog_b"], ["a_b"])
        self.ts(t["a_b"][:], t["a_b"][:], -1.0, None, ALU.mult, None, ["a_b"], ["a_b"])
        self.dma("sync", t["wdt"][:], dr["wb_in"][:, OFF_DT:OFF_DT + 32].rearrange("(c p) n -> p c n", p=128),
                 wkeys("in", 0, 1024, OFF_DT, 32), ["wdt"], ("c", "wdt"))
        self.act(t["scT"][:], t["cTs"][:], AF.Silu, ["cTs"], ["scT"])
        ps = self.PS[0]
        for g in range(12):
            wsl, wk = self.ws.get(dr["wb_ada"][:, g * 512:(g + 1) * 512].rearrange("(c p) n -> p c n", p=128), 8, 512,
                                  wkeys("ada", 0, 1024, g * 512, 512))
            for m in range(4):
                c = g * 4 + m
                for k in range(8):
                    self.mm(ps[:, c * 8:(c + 1) * 8], wsl[:, k, m * 128:(m + 1) * 128], t["scT"][:, k, :],
                            k == 0, k == 7, [wk, "scT"], [("ps", 0)])
        self.tt(t["modT"][:], ps[:, 0:384].rearrange("p (c b) -> p c b", b=8),
                t["b_ada_col"][:].unsqueeze(2).broadcast_to([128, 48, 8]), ALU.add, [("ps", 0), "b_ada_col"], ["modT"])
        for nm, base, gcol in (("Gm", 8, "gmix_col"), ("Gf", 32, "gffn_col")):
            self.ts(t["onep"][:], t["modT"][:, base:base + 8, :], 1.0, None, ALU.add, None, ["modT"], ["onep"])
            self.tt(t[nm][:], t["onep"][:], t[gcol][:].unsqueeze(2).broadcast_to([128, 8, 8]), ALU.mult,
                    ["onep", gcol], [nm])

    def gate_rows(self, b):
        t = self.t
        for nm, base, bank in (("gtm_b", 16, 0), ("gtf_b", 40, 2)):
            for half in range(2):
                ps = self.PS[bank + half]
                for cc in range(4):
                    c = half * 4 + cc
                    self.ts(t["gb"][:], t["ones_f"][:], t["modT"][:, base + c, b:b + 1], None, ALU.mult, None,
                            ["ones_f", "modT"], ["gb"])
                    self.mm(ps[:, cc * 128:(cc + 1) * 128], t["gb"][:], t["ident_f"][:], True, True,
                            ["gb", "ident_f"], [("ps", bank + half)])
                self.act(t[nm][:, half * 512:(half + 1) * 512], ps[:], AF.Copy, [("ps", bank + half)], [nm],
                         scale=(0.5 if nm == "gtf_b" else 1.0))

    def rstd_from_ss(self, ss_ap, ss_key, n, Pt):
        k1, a1 = self.stcol()
        self.ts(a1[:Pt], ss_ap[:Pt], 1.0 / n, EPS, ALU.mult, ALU.add, [ss_key], [k1], eng="gpsimd")
        k2, a2 = self.stcol()
        self.tt(a2[:Pt], a1[:Pt], self.t["mhalf"][:Pt], ALU.pow, [k1, "mhalf"], [k2], eng="gpsimd")
        return k2, a2

    def norm_to_hT(self, src, src_keys, Pt, G, S, Gk, Sk, b, hbuf, t_idx):
        t = self.t
        kss, ss = self.stcol()
        jk_, jap = self.jkcol()
        self.act(jap[:Pt].broadcast_to([Pt, D]), src, AF.Square, src_keys, [jk_, kss], accum_out=ss[:Pt])
        krs, rs = self.rstd_from_ss(ss, kss, D, Pt)
        xn = t["TB"][:Pt, t_idx % 2, :]
        xk = ("TB", t_idx % 2)
        self.ts(xn, src, rs[:Pt], None, ALU.mult, None, src_keys + [krs], [xk])
        pb = self.psb(7)[:, 0:8 * Pt].rearrange("p (c t) -> p c t", t=Pt)
        for c in range(8):
            self.tr(pb[:, c, :], xn[:, c * 128:(c + 1) * 128], t["ident_bf"][:Pt, :Pt], [xk, "ident_bf"], [("ps", 7)])
        for c in range(8):
            self.act(t["H"][:, hbuf, c, t_idx * Pt:(t_idx + 1) * Pt], pb[:, c, :], AF.Identity,
                     [("ps", 7), Gk, Sk], [("H", hbuf, c)], scale=G[:, c, b:b + 1], bias=S[:, c, b:b + 1])

    def tmh(self):
        i = self.tmh_i % 4
        self.tmh_i += 1
        return ("TM", 2 + i // 2, i % 2), self.t["TM"][:, 2 + i // 2, (i % 2) * 512:(i % 2 + 1) * 512], \
            ("TB", 2 + i // 2, i % 2), self.t["TB"][:, 2 + i // 2, (i % 2) * 512:(i % 2 + 1) * 512], i

    def k_to_scratch(self, kbf, kbf_key, Pt, sidx, pos, g):
        t = self.t
        pb = self.psb(6)[:, 0:4 * Pt].rearrange("p (c t) -> p c t", t=Pt)
        for c in range(4):
            self.tr(pb[:, c, :], kbf[:, c * 128:(c + 1) * 128], t["ident_bf"][:Pt, :Pt], [kbf_key, "ident_bf"],
                    [("ps", 6)])
        i = self.kst_i % 2
        self.kst_i += 1
        st = t["kTst"][:, i, :, 0:Pt]
        self.act(st, pb, AF.Copy, [("ps", 6)], [("kTst", i)])
        self.dma("scalar", self.dr["kscr"][sidx, :, 4 * g:4 * g + 4, pos:pos + Pt], st, [("kTst", i)],
                 [("kscr", sidx)], ("kst", i))

    def sequence(self, s, prompt):
        t, dr = self.t, self.dr
        self.kst_i = 0
        b = s if prompt else NSEQ + s
        sidx = b
        T = TP if prompt else TS
        Tb = 512 if prompt else 64
        Pt = 128 if prompt else 64
        nT = Tb // Pt
        x_in = dr["xp"][s] if prompt else dr["xs"][s]
        self.gate_rows(b)
        if prompt:
            self.op("vector", "memset", [], ["tail"], t["tail"][:], 0.0)
            self.op("vector", "memset", [], [("S", g) for g in range(4)], t["S"][:], 0.0)
            self.op("vector", "memset", [], [("Sb", g) for g in range(4)], t["Sb"][:], 0.0)
        else:
            self.dma("sync", t["tail"][:], dr["scT"][s], [], ["tail"], ("c", "tail"))
            self.dma("sync", t["S"][:], dr["ssmT"][s], [], [("S", g) for g in range(4)], ("c", "S"))
            for g in range(4):
                self.act(t["Sb"][:, g * 512:(g + 1) * 512], t["S"][:, g * 512:(g + 1) * 512], AF.Copy, [("S", g)],
                         [("Sb", g)])
            for r0 in range(0, PAST, 128):
                self.dma("gpsimd", dr["vscr"][sidx, :, r0:r0 + 128, :].rearrange("j t e -> t j e"),
                         dr["cv"][s, r0:r0 + 128, :].rearrange("t (j e) -> t j e", e=128), [], [("vscr", sidx)],
                         ("pastv", sidx))
            for kt in range(PAST // 128):
                for g in range(2):
                    _, _, kb, ap, i = self.tmh()
                    self.dma("gpsimd", ap, dr["ck"][s, kt * 128:(kt + 1) * 128, g * 512:(g + 1) * 512], [], [kb],
                             ("pastk", i))
                    self.k_to_scratch(ap, kb, 128, sidx, kt * 128, g)
        for qb in range(min(T // Tb, LIM["nb"])):
            self.block(s, b, sidx, prompt, qb, Tb, Pt, nT, x_in)
        oc = dr["conv_p"] if prompt else dr["conv_s"]
        osm = dr["ssm_p"] if prompt else dr["ssm_s"]
        self.dma("scalar", oc[s], t["tail"][:], ["tail"], [], ("o", "tail"))
        self.dma("scalar", osm[s], t["S"][:], [("S", g) for g in range(4)], [], ("o", "S"))

    def mark(self, name):
        if not hasattr(self, "marks"):
            self.marks = []
        self.marks.append((name, len(self.P.ops["tensor"])))

    def block(self, s, b, sidx, prompt, qb, Tb, Pt, nT, x_in):
        t, dr, ws = self.t, self.dr, self.ws
        self.mark("blk s%d p%d qb%d norm1" % (s, prompt, qb))
        H = t["H"]
        pos0 = qb * Tb
        hk = lambda hb: [("H", hb, c) for c in range(8)]
        tsl = lambda ti: slice(ti * Pt, (ti + 1) * Pt)
        wsrc = lambda name, r0, kc, c0, nc_: dr["wb_" + name][r0:r0 + kc * 128, c0:c0 + nc_].rearrange(
            "(c p) n -> p c n", p=128)
        cosT = (lambda ti: t["cosP"][:, qb * 4 + ti, :]) if prompt else (lambda ti: t["cosS"][:, :])
        sinT = (lambda ti: t["sinP"][:, qb * 4 + ti, :]) if prompt else (lambda ti: t["sinS"][:, :])
        k_out = dr["k_p"] if prompt else dr["k_s"]
        v_out = dr["v_p"] if prompt else dr["v_s"]
        y_out = dr["y_p"] if prompt else dr["y_s"]
        Fk = lambda i: ("F", i)
        Ft = lambda i: t["Ft"][:, i, :]

        for ti in range(nT):
            xs_ = t["raw"][:].rearrange("p a b -> p (a b)")[:Pt, 0:D]
            xkeys = [("raw", 0), ("raw", 1)]
            self.dma("scalar", xs_, x_in[pos0 + ti * Pt:pos0 + (ti + 1) * Pt, :], [], xkeys, ("xin", 0))
            self.norm_to_hT(xs_, xkeys, Pt, t["Gm"], t["modT"], "Gm", "modT", b, 0, ti)
        if STOP_AFTER < 1:
            return
        self.mark("qkv")
        for grp in range(6):
            kind = "kvq"[grp // 2]
            g = grp % 2
            if kind not in DBG["kinds"]:
                continue
            c0 = {"k": OFF_K, "v": OFF_V, "q": 0}[kind] + g * 512
            wsl, wk = ws.get(wsrc("in", 0, 8, c0, 512), 8, 512, wkeys("in", 0, 1024, c0, 512))
            for ti in range(nT):
                bank = (grp * nT + ti) % 2
                ps = self.PS[bank][:Pt, :]
                pk = ("ps", bank)
                for k in range(8):
                    self.mm(ps, H[:, 0, k, tsl(ti)], wsl[:, k, :], k == 0, k == 7, [("H", 0, k), wk], [pk])
                fk, fap, bk, bap, _ = self.tmh()
                fap = fap[:Pt]
                bap = bap[:Pt]
                pos = pos0 + ti * Pt
                kpos = pos if prompt else PAST + pos
                if kind == "v":
                    self.act(fap, ps, AF.Copy, [pk], [fk])
                    self.op("vector", "tensor_copy", [fk], [bk], out=bap, in_=fap)
                    if DBG["vout"]:
                        self.dma("scalar", v_out[s, pos:pos + Pt, g * 512:(g + 1) * 512], fap, [fk], [], ("ov", fk))
                    if DBG["vscr"]:
                      self.dma("scalar", dr["vscr"][sidx, 4 * g:4 * g + 4, kpos:kpos + Pt, :].rearrange("j t e -> t j e"),
                             bap.rearrange("t (j e) -> t j e", e=128), [bk], [("vscr", sidx)], ("sv", bk))
                    continue
                p4 = ps.rearrange("t (h two d) -> t h two d", two=2, d=32)
                cs = cosT(ti)[:Pt].unsqueeze(1).unsqueeze(1).broadcast_to([Pt, 8, 2, 32])
                sn = sinT(ti)[:Pt].unsqueeze(1).broadcast_to([Pt, 8, 32])
                A = Ft(0)[:Pt].rearrange("t (h two d) -> t h two d", two=2, d=32)
                B = Ft(1)[:Pt].rearrange("t (h two d) -> t h two d", two=2, d=32)
                self.tt(A, p4, cs, ALU.mult, [pk, "cosP"], [Fk(0)])
                self.tt(B[:, :, 0, :], p4[:, :, 1, :], sn, ALU.mult, [pk, "sinP"], [Fk(1)])
                self.tt(B[:, :, 1, :], p4[:, :, 0, :], sn, ALU.mult, [pk, "sinP"], [Fk(1)])
                if kind == "k":
                    o4 = fap.rearrange("t (h two d) -> t h two d", two=2, d=32)
                    self.tt(o4[:, :, 0, :], A[:, :, 0, :], B[:, :, 0, :], ALU.subtract, [Fk(0), Fk(1)], [fk])
                    self.tt(o4[:, :, 1, :], A[:, :, 1, :], B[:, :, 1, :], ALU.add, [Fk(0), Fk(1)], [fk])
                    self.act(bap, fap, AF.Copy, [fk], [bk])
                    self.dma("scalar", k_out[s, pos:pos + Pt, g * 512:(g + 1) * 512], fap, [fk], [], ("ok", fk))
                    if DBG["kscr"]:
                        self.k_to_scratch(bap, bk, Pt, sidx, kpos, g)
                else:
                    o4 = bap.rearrange("t (h two d) -> t h two d", two=2, d=32)
                    self.tt(o4[:, :, 0, :], A[:, :, 0, :], B[:, :, 0, :], ALU.subtract, [Fk(0), Fk(1)], [bk])
                    self.tt(o4[:, :, 1, :], A[:, :, 1, :], B[:, :, 1, :], ALU.add, [Fk(0), Fk(1)], [bk])
                    pb = self.psb(6)[:, 0:4 * Pt].rearrange("p (c t) -> p c t", t=Pt)
                    for c in range(4):
                        self.tr(pb[:, c, :], bap[:, c * 128:(c + 1) * 128], t["ident_bf"][:Pt, :Pt],
                                [bk, "ident_bf"], [("ps", 6)])
                    self.act(H[:, 1, 4 * g:4 * g + 4, tsl(ti)], pb, AF.Copy, [("ps", 6)],
                             [("H", 1, 4 * g + c) for c in range(4)])
        if STOP_AFTER < 2:
            return
        self.mark("attn")
        if prompt:
            nkeys = pos0 + Tb
            ktiles = []
            for kt in range(nkeys // 128):
                i = kt - pos0 // 128
                ktiles.append((kt * 128, 128, 0 if i <= 0 else 128 * i, i >= 0))
        else:
            nkeys = NKS
            ktiles = [(kt * 128, 128, 0, False) for kt in range(PAST // 128)] + [(PAST, 64, 0, False)]
        nfull = nkeys // 128
        self.attention(sidx, nkeys, nfull, ktiles, Tb, prompt)
        if STOP_AFTER < 3:
            return
        self.mark("wpa")
        for ch in range(2):
            wsl, wk = ws.get(wsrc("pa", 0, 8, ch * 512, 512), 8, 512, wkeys("pa", 0, 1024, ch * 512, 512))
            for m in range(4):
                c = ch * 4 + m
                bank = c % 4
                ps = self.PS[bank][:, 0:Tb]
                for k in range(8):
                    self.mm(ps, wsl[:, k, m * 128:(m + 1) * 128], H[:, 2, k, 0:Tb], k == 0, k == 7,
                            [wk, ("H", 2, k)], [("ps", bank)])
                self.act(H[:, 3, c, 0:Tb], ps, AF.Copy, [("ps", bank)], [("H", 3, c)], scale=0.5)
        if STOP_AFTER < 4:
            return
        self.mark("ssd")
        self.ssd(s, b, prompt, qb, Tb, Pt, nT)
        if STOP_AFTER < 5:
            return
        self.mark("wpb")
        for ch in range(2):
            for kg in range(2):
                w_, wk_ = ws.get(wsrc("pb", kg * 1024, 8, ch * 512, 512), 8, 512, wkeys("pb", kg * 1024, 1024, ch * 512, 512))
                for m in range(4):
                    for kk in range(8):
                        k = kg * 8 + kk
                        self.mm(self.PS[m][:, 0:Tb], w_[:, kk, m * 128:(m + 1) * 128], H[:, 1 + kg, kk, 0:Tb], k == 0,
                                k == 15, [wk_, ("H", 1 + kg, kk)], [("ps", m)])
            for m in range(4):
                c = ch * 4 + m
                self.act(H[:, 4, c, 0:Tb], self.PS[m][:, 0:Tb], AF.Copy, [("ps", m)], [("H", 4, c)], scale=0.5)
        self.mark("gates")
        for gi in range(4):
            wsl, wk = ws.get(wsrc("in", 0, 8, OFF_GATE + gi * 512, 512), 8, 512, wkeys("in", 0, 1024, OFF_GATE + gi * 512, 512))
            for m in range(4):
                c = (gi % 2) * 4 + m
                bank = (gi * 4 + m) % 4
                ps = self.PS[bank][:, 0:Tb]
                for k in range(8):
                    self.mm(ps, wsl[:, k, m * 128:(m + 1) * 128], H[:, 0, k, 0:Tb], k == 0, k == 7,
                            [wk, ("H", 0, k)], [("ps", bank)])
                fi = 2 + (m % 2)
                self.act(Ft(fi)[:, 0:Tb], ps, AF.Tanh, [("ps", bank)], [Fk(fi)], scale=0.5)
                if gi < 2:
                    self.stt(H[:, 3, c, 0:Tb], Ft(fi)[:, 0:Tb], 1.0, H[:, 3, c, 0:Tb], ALU.add, ALU.mult,
                             [Fk(fi), ("H", 3, c)], [("H", 3, c)])
                else:
                    self.stt(H[:, 4, c, 0:Tb], Ft(fi)[:, 0:Tb], 1.0, H[:, 4, c, 0:Tb], ALU.add, ALU.mult,
                             [Fk(fi), ("H", 4, c)], [("H", 4, c)])
                    self.tt(H[:, 1, c, 0:Tb], H[:, 4, c, 0:Tb], H[:, 3, c, 0:Tb], ALU.add,
                            [("H", 4, c), ("H", 3, c)], [("H", 1, c)], eng=("gpsimd" if DBG["pool_merge"] else "vector"))
        self.mark("wout")
        for ti in range(nT):
            self.dma("scalar", t["TM"][:Pt, ti, :], x_in[pos0 + ti * Pt:pos0 + (ti + 1) * Pt, :], [],
                     [("TM", ti, 0), ("TM", ti, 1)], ("x1in", ti))
        for ch in range(2):
            wsl, wk = ws.get(wsrc("out", 0, 8, ch * 512, 512), 8, 512, wkeys("out", 0, 1024, ch * 512, 512))
            for ti in range(nT):
                bank = (ch * nT + ti) % 2
                ps = self.PS[bank][:Pt, :]
                for k in range(8):
                    self.mm(ps, H[:, 1, k, tsl(ti)], wsl[:, k, :], k == 0, k == 7, [("H", 1, k), wk], [("ps", bank)])
                fi = 2 + bank
                xh = t["TM"][:Pt, ti, ch * 512:(ch + 1) * 512]
                self.tt(Ft(fi)[:Pt], ps, t["gtm_b"][:Pt, ch * 512:(ch + 1) * 512], ALU.mult, [("ps", bank), "gtm_b"],
                        [Fk(fi)])
                self.tt(xh, xh, Ft(fi)[:Pt], ALU.add, [("TM", ti, ch), Fk(fi)], [("TM", ti, ch)], eng=("gpsimd" if DBG["pool_res"] else "vector"))
        self.mark("norm2")
        for ti in range(nT):
            self.norm_to_hT(t["TM"][:Pt, ti, :], [("TM", ti, 0), ("TM", ti, 1)], Pt, t["Gf"], t["modT"][:, 24:32, :],
                            "Gf", "modT", b, 4, ti)
        self.mark("wgu")
        for gi in range(6):
            ncol = 512 if gi < 5 else 256
            nm = ncol // 128
            wg, wkg = ws.get(wsrc("gu", 0, 8, gi * 512, ncol), 8, ncol, wkeys("gu", 0, 1024, gi * 512, ncol))
            for m in range(nm):
                for k in range(8):
                    self.mm(self.PS[m][:, 0:Tb], wg[:, k, m * 128:(m + 1) * 128], H[:, 4, k, 0:Tb], k == 0, k == 7,
                            [wkg, ("H", 4, k)], [("ps", m)])
            wu, wku = ws.get(wsrc("gu", 0, 8, DFF + gi * 512, ncol), 8, ncol, wkeys("gu", 0, 1024, DFF + gi * 512, ncol))
            for m in range(nm):
                for k in range(8):
                    self.mm(self.PS[4 + m][:, 0:Tb], wu[:, k, m * 128:(m + 1) * 128], H[:, 4, k, 0:Tb], k == 0, k == 7,
                            [wku, ("H", 4, k)], [("ps", 4 + m)])
            for m in range(nm):
                i = gi * 4 + m
                fi = 2 + (i % 2)
                self.act(Ft(fi)[:, 0:Tb], self.PS[m][:, 0:Tb], AF.Tanh, [("ps", m)], [Fk(fi)], scale=0.5)
                self.stt(Ft(fi)[:, 0:Tb], Ft(fi)[:, 0:Tb], 1.0, self.PS[m][:, 0:Tb], ALU.add, ALU.mult,
                         [Fk(fi), ("ps", m)], [Fk(fi)])
                self.tt(H[:, 1 + i // 8, i % 8, 0:Tb], self.PS[4 + m][:, 0:Tb], Ft(fi)[:, 0:Tb], ALU.mult,
                        [("ps", 4 + m), Fk(fi)], [("H", 1 + i // 8, i % 8)])
        self.mark("wdown")
        for ch in range(2):
            for kg in range(3):
                kc = 8 if kg < 2 else 6
                w_, wk_ = ws.get(wsrc("down", kg * 1024, kc, ch * 512, 512), kc, 512, wkeys("down", kg * 1024, kc * 128, ch * 512, 512))
                for ti in range(nT):
                    for kk in range(kc):
                        i = kg * 8 + kk
                        self.mm(self.PS[4 + ti][:Pt, :], H[:, 1 + kg, kk, tsl(ti)], w_[:, kk, :], i == 0, i == 21,
                                [("H", 1 + kg, kk), wk_], [("ps", 4 + ti)])
            for ti in range(nT):
                fi = 4 + ti % 2
                xh = t["TM"][:Pt, ti, ch * 512:(ch + 1) * 512]
                self.tt(Ft(fi)[:Pt], self.PS[4 + ti][:Pt, :], t["gtf_b"][:Pt, ch * 512:(ch + 1) * 512], ALU.mult,
                        [("ps", 4 + ti), "gtf_b"], [Fk(fi)])
                self.tt(xh, xh, Ft(fi)[:Pt], ALU.add, [("TM", ti, ch), Fk(fi)], [("TM", ti, ch)], eng=("gpsimd" if DBG["pool_res"] else "vector"))
        for ti in range(nT):
            xt = t["TM"][:Pt, ti, :]
            xk = [("TM", ti, 0), ("TM", ti, 1)]
            kss, ss = self.stcol()
            jk_, jap = self.jkcol()
            self.act(jap[:Pt].broadcast_to([Pt, D]), xt, AF.Square, xk, [jk_, kss], accum_out=ss[:Pt])
            krs, rs = self.rstd_from_ss(ss, kss, D, Pt)
            self.stt(xt, xt, rs[:Pt], t["gfin_b"][:Pt], ALU.mult, ALU.mult, xk + [krs, "gfin_b"], xk)
            self.dma("scalar", y_out[s, pos0 + ti * Pt:pos0 + (ti + 1) * Pt, :], xt, xk, [], ("oy", ti))

    def attention(self, sidx, nkeys, nfull, ktiles, Tb, prompt):
        t, dr = self.t, self.dr
        H = t["H"]
        Ft = lambda i: t["Ft"][:, i, 0:Tb]
        Fk = lambda i: ("F", i)
        nrem = nkeys - nfull * 128

        def load_head(j):
            sl = j % 2
            self.dma("scalar", t["kTj"][:, sl, 0:nkeys], dr["kscr"][sidx, :, j, 0:nkeys], [("kscr", sidx)],
                     [("kTj", sl)], ("lk", sl))
            self.dma("scalar", t["Vj"][:, sl, 0:nfull, :],
                     dr["vscr"][sidx, j, 0:nfull * 128, :].rearrange("(kt p) e -> p kt e", p=128), [("vscr", sidx)],
                     [("Vj", sl)], ("lv", sl))
            if nrem:
                self.dma("scalar", t["Vj"][0:nrem, sl, nfull, :], dr["vscr"][sidx, j, nfull * 128:nkeys, :],
                         [("vscr", sidx)], [("Vj", sl)], ("lv", sl))

        load_head(0)
        for j in range(8):
            if j + 1 < 8:
                load_head(j + 1)
            sl = j % 2
            nk = len(ktiles)

            def scores(step):
                k0, ksz, qlo, diag = ktiles[step]
                for r in range(2):
                    bank = r * 2 + step % 2
                    ps = self.PS[bank][:ksz, qlo:Tb]
                    self.mm(ps, t["kTj"][r * 64:(r + 1) * 64, sl, k0:k0 + ksz], H[r * 64:(r + 1) * 64, 1, j, qlo:Tb],
                            True, True, [("kTj", sl), ("H", 1, j)], [("ps", bank)])
                    pt = t["PT"][:ksz, bank, qlo:Tb]
                    self.act(pt, ps, AF.Exp, [("ps", bank)], [("PT", bank)], scale=SCALE_A)
                    if diag:
                        self.op("vector", "memset", [], [("PT", bank)], t["PT"][64:128, bank, qlo:qlo + 64], 0.0)

            def pv(step):
                k0, ksz, qlo, diag = ktiles[step]
                kt = k0 // 128
                for r in range(2):
                    bank = r * 2 + step % 2
                    pt = t["PT"][:ksz, bank, qlo:Tb]
                    self.mm(self.PS[4 + r][:, qlo:Tb], t["Vj"][:ksz, sl, kt, :], pt, step == 0, step == nk - 1,
                            [("Vj", sl), ("PT", bank)], [("ps", 4 + r)])
                    self.mm(self.PS[6 + r][:, qlo:Tb], t["ones_bf"][:ksz, :], pt, step == 0, step == nk - 1,
                            ["ones_bf", ("PT", bank)], [("ps", 6 + r)])

            for step in range(nk + 1):
                if step < nk:
                    scores(step)
                if step >= 1:
                    pv(step - 1)
            self.act(Ft(0), self.PS[6][:, 0:Tb], AF.Ln, [("ps", 6)], [Fk(0)])
            self.act(Ft(0), Ft(0), AF.Exp, [Fk(0)], [Fk(0)], scale=-1.0)
            self.tt(Ft(1), self.PS[4][:, 0:Tb], Ft(0), ALU.mult, [("ps", 4), Fk(0)], [Fk(1)])
            self.act(Ft(4), self.PS[7][:, 0:Tb], AF.Ln, [("ps", 7)], [Fk(4)])
            self.act(Ft(4), Ft(4), AF.Exp, [Fk(4)], [Fk(4)], scale=-1.0)
            self.tt(Ft(2), self.PS[5][:, 0:Tb], Ft(4), ALU.mult, [("ps", 5), Fk(4)], [Fk(2)])
            self.stt(Ft(3), Ft(2), t["neglam"][:], Ft(1), ALU.mult, ALU.add, [Fk(2), Fk(1), "neglam"], [Fk(3)])
            if DBG["pool_sq"]:
                self.tt(t["sq"][:, 0:Tb], Ft(3), Ft(3), ALU.mult, [Fk(3)], ["sq"], eng="gpsimd")
            else:
                self.act(t["sq"][:, 0:Tb], Ft(3), AF.Square, [Fk(3)], ["sq"])
            self.mm(self.PS[7][:, 0:Tb], t["ones_bf"][:], t["sq"][:, 0:Tb], True, True, ["ones_bf", "sq"], [("ps", 7)])
            self.act(Ft(5), self.PS[7][:, 0:Tb], AF.Ln, [("ps", 7)], [Fk(5)], scale=1.0 / 128, bias=EPS)
            self.act(Ft(5), Ft(5), AF.Exp, [Fk(5)], [Fk(5)], scale=-0.5)
            self.stt(H[:, 2, j, 0:Tb], Ft(3), t["gsubs"][:], Ft(5), ALU.mult, ALU.mult, [Fk(3), Fk(5), "gsubs"],
                     [("H", 2, j)])

    def conv_chunk(self, cc, bank, Tb, dest, dest_key):
        t = self.t
        ps = self.PS[bank][:, 0:Tb]
        par = cc % 2
        raw = t["raw"][:, par, :]
        rk = ("raw", par)
        acc = t["Ft"][:, 4 + par, 0:Tb]
        ak = ("F", 4 + par)
        cw = t["convw"]
        self.act(raw[:, 3:3 + Tb], ps, AF.Copy, [("ps", bank)], [rk])
        self.op("vector", "tensor_copy", ["tail"], [rk], out=raw[:, 0:3], in_=t["tail"][:, cc, :])
        self.act(acc, ps, AF.Identity, [("ps", bank), "convw", "convb"], [ak], scale=cw[:, cc, 3:4],
                 bias=t["convb"][:, cc:cc + 1])
        for j in range(3):
            self.stt(acc, raw[:, j:j + Tb], cw[:, cc, j:j + 1], acc, ALU.mult, ALU.add, [rk, ak, "convw"], [ak])
        self.op("vector", "tensor_copy", [rk], ["tail"], out=t["tail"][:, cc, :], in_=raw[:, Tb:Tb + 3])
        th = t["Ft"][:, 2 + par, 0:Tb]
        self.act(th, acc, AF.Tanh, [ak], [("F", 2 + par)])
        self.stt(dest, th, 1.0, acc, ALU.add, ALU.mult, [("F", 2 + par), ak], [dest_key])

    def ssd(self, s, b, prompt, qb, Tb, Pt, nT):
        t, dr, ws = self.t, self.dr, self.ws
        H = t["H"]
        sm = t["sm"]
        Fk = lambda i: ("F", i)
        Ft = lambda i: t["Ft"][:, i, :]
        tsl = lambda ti: slice(ti * Pt, (ti + 1) * Pt)
        wsrc = lambda c0, nc_: dr["wb_in"][:, c0:c0 + nc_].rearrange("(c p) n -> p c n", p=128)
        for ti in range(nT):
            ps = self.PS[6]
            for k in range(8):
                self.mm(ps[:Pt, 0:32], H[:, 0, k, tsl(ti)], t["wdt"][:, k, :], k == 0, k == 7, [("H", 0, k), "wdt"],
                        [("ps", 6)])
            smk = ("sm", ti)
            v = lambda i: sm[:Pt, ti, i, :]
            self.tt(v(0), ps[:Pt, 0:32], t["dtb_b"][:Pt], ALU.add, [("ps", 6), "dtb_b"], [smk])
            self.act(v(1), v(0), AF.Exp, [smk], [smk])
            self.act(v(2), v(1), AF.Ln, [smk], [smk], bias=1.0)
            self.tt(v(3), v(2), t["a_b"][:Pt], ALU.mult, [smk, "a_b"], [smk])
            self.mm(ps[:Pt, 32:64], t["uincl"][:Pt, :Pt], v(3), True, True, ["uincl", smk], [("ps", 6)])
            self.mm(ps[:Pt, 64:96], t["ustrict"][:Pt, :Pt], v(3), True, True, ["ustrict", smk], [("ps", 6)])
            self.mm(ps[:, 96:128], t["ones_f"][:Pt, :], v(3), True, True, ["ones_f", smk], [("ps", 6)])
            self.act(sm[:Pt, ti, 4:6, :], ps[:Pt, 32:96].rearrange("p (a h) -> p a h", h=32), AF.Exp, [("ps", 6)], [smk])
            self.act(sm[:, ti, 7, :], ps[:, 96:128], AF.Exp, [("ps", 6)], [smk])
            self.tt(v(6), v(2), v(5), ALU.mult, [smk], [smk])
        for bc in range(2):
            wsl, wk = ws.get(wsrc(OFF_XBC + 2048 + bc * 512, 512), 8, 512, wkeys("in", 0, 1024, OFF_XBC + 2048 + bc * 512, 512))
            for m in range(4):
                bank = m % 2
                for k in range(8):
                    self.mm(self.PS[bank][:, 0:Tb], wsl[:, k, m * 128:(m + 1) * 128], H[:, 0, k, 0:Tb], k == 0, k == 7,
                            [wk, ("H", 0, k)], [("ps", bank)])
                self.conv_chunk(16 + bc * 4 + m, bank, Tb, H[:, 4, bc * 4 + m, 0:Tb], ("H", 4, bc * 4 + m))
        for ti in range(nT):
            pb = self.psb(7)[:Pt, 0:512].rearrange("p (g n) -> p g n", n=128)
            for g in range(4):
                self.tr(pb[:, g, :], H[:, 4, g, tsl(ti)], t["ident_bf"][:], [("H", 4, g), "ident_bf"], [("ps", 7)])
            self.act(t["Btok"][:Pt, ti, :, :], pb, AF.Copy, [("ps", 7)], [("Btok", ti)])
        for g in range(4):
            wsl, wk = ws.get(wsrc(OFF_XBC + g * 512, 512), 8, 512, wkeys("in", 0, 1024, OFF_XBC + g * 512, 512))
            for m in range(4):
                bank = m % 2
                for k in range(8):
                    self.mm(self.PS[bank][:, 0:Tb], wsl[:, k, m * 128:(m + 1) * 128], H[:, 0, k, 0:Tb], k == 0, k == 7,
                            [wk, ("H", 0, k)], [("ps", bank)])
                self.conv_chunk(g * 4 + m, bank, Tb, t["xsT"][:, m, 0:Tb], ("xsT", m))
            for ti in range(nT):
                pb = self.psb(7)[:Pt, 0:512].rearrange("p (m n) -> p m n", n=128)
                for m in range(4):
                    self.tr(pb[:, m, :], t["xsT"][:, m, tsl(ti)], t["ident_bf"][:], [("xsT", m), "ident_bf"], [("ps", 7)])
                self.act(t["xstok"][:Pt, ti, :], pb.rearrange("p m n -> p (m n)"), AF.Copy, [("ps", 7)], [("xstok", ti)])
            wz, wkz = ws.get(wsrc(OFF_Z + g * 512, 512), 8, 512, wkeys("in", 0, 1024, OFF_Z + g * 512, 512))
            for ti in range(nT):
                smk = ("sm", ti)
                v = lambda i: sm[:Pt, ti, i, g * 8:(g + 1) * 8]
                xs3 = t["xstok"][:Pt, ti, :].rearrange("p (h d) -> p h d", d=64)
                bc3 = lambda ap: ap.unsqueeze(2).broadcast_to([Pt, 8, 64])
                for k in range(8):
                    self.mm(self.PS[6][:Pt, :], H[:, 0, k, tsl(ti)], wz[:, k, :], k == 0, k == 7, [("H", 0, k), wkz],
                            [("ps", 6)])
                self.act(Ft(2)[:Pt], self.PS[6][:Pt, :], AF.Tanh, [("ps", 6)], [Fk(2)], scale=0.5)
                self.stt(Ft(2)[:Pt], Ft(2)[:Pt], 1.0, self.PS[6][:Pt, :], ALU.add, ALU.mult, [Fk(2), ("ps", 6)], [Fk(2)])
                Bmf = t["Ft"][:, 0, :].bitcast(BF16)
                Bm = Bmf[:Pt, 0:8 * Pt].rearrange("p (h l) -> p h l", l=Pt)
                nb = 2 if Pt == 128 else 1
                self.tt(Bm, v(3).unsqueeze(2).broadcast_to([Pt, 8, Pt]),
                        t["uincl"][:Pt, :Pt].unsqueeze(1).broadcast_to([Pt, 8, Pt]), ALU.mult, [smk, "uincl"],
                        [Fk(0)], eng=("gpsimd" if DBG["pool_bm"] else "vector"))
                for hb in range(nb):
                    self.mm(self.PS[hb][:Pt, :], t["ustrict_bf"][:Pt, :Pt], Bmf[:Pt, hb * 512:(hb + 1) * 512], True, True,
                            ["ustrict_bf", Fk(0)], [("ps", hb)])
                    self.act(t["E"][:Pt, hb * 512:(hb + 1) * 512], self.PS[hb][:Pt, :], AF.Exp, [("ps", hb)], [("E", hb)])
                self.mm(self.PS[2][:Pt, 0:Pt], H[:, 4, g, tsl(ti)], H[:, 4, 4 + g, tsl(ti)], True, True,
                        [("H", 4, g), ("H", 4, 4 + g)], [("ps", 2)])
                self.tt(t["cbm"][:Pt, :Pt], self.PS[2][:Pt, 0:Pt], t["causal_bf"][:Pt, :Pt], ALU.mult,
                        [("ps", 2), "causal_bf"], ["cbm"])
                E3 = t["E"][:Pt, 0:8 * Pt].rearrange("p (h l) -> p h l", l=Pt)
                M3 = t["MT"][:Pt, 0:8 * Pt].rearrange("p (h l) -> p h l", l=Pt)
                self.tt(M3, E3, t["cbm"][:Pt, :Pt].unsqueeze(1).broadcast_to([Pt, 8, Pt]), ALU.mult,
                        [("E", 0), ("E", 1), "cbm"], ["MT"])
                self.tt(t["xdt"][:Pt].rearrange("p (h d) -> p h d", d=64), xs3, bc3(v(2)), ALU.mult,
                        [("xstok", ti), smk], ["xdt"], eng=("gpsimd" if DBG["pool_x"] else "vector"))
                self.tt(t["xw"][:Pt].rearrange("p (h d) -> p h d", d=64), xs3, bc3(v(6)), ALU.mult,
                        [("xstok", ti), smk], ["xw"], eng=("gpsimd" if DBG["pool_x"] else "vector"))
                self.tt(t["xsd"][:Pt].rearrange("p (h d) -> p h d", d=64), xs3,
                        bc3(t["dskip_b"][:Pt, g * 8:(g + 1) * 8]), ALU.mult, [("xstok", ti), "dskip_b"], ["xsd"],
                        eng=("gpsimd" if DBG["pool_x"] else "vector"))
                self.mm(self.PS[3][:Pt, :], t["ident_bf"][:Pt, :Pt], t["xsd"][:Pt, :], True, False, ["ident_bf", "xsd"],
                        [("ps", 3)])
                for h in range(8):
                    self.mm(self.PS[3][:Pt, h * 64:(h + 1) * 64], M3[:, h, :], t["xdt"][:Pt, h * 64:(h + 1) * 64], False,
                            h == 7, ["MT", "xdt"], [("ps", 3)])
                self.mm(self.PS[4][:Pt, :], H[:, 4, 4 + g, tsl(ti)], t["Sb"][:, g * 512:(g + 1) * 512], True, True,
                        [("H", 4, 4 + g), ("Sb", g)], [("ps", 4)])
                self.mm(self.PS[5][:, :], t["Btok"][:Pt, ti, g, :], t["xw"][:Pt, :], True, True, [("Btok", ti), "xw"],
                        [("ps", 5)])
                self.tt(Ft(3)[:Pt].rearrange("p (h d) -> p h d", d=64),
                        self.PS[4][:Pt, :].rearrange("p (h d) -> p h d", d=64), bc3(v(4)), ALU.mult, [("ps", 4), smk],
                        [Fk(3)])
                self.tt(Ft(3)[:Pt], self.PS[3][:Pt, :], Ft(3)[:Pt], ALU.add, [("ps", 3), Fk(3)], [Fk(3)])
                self.stt(Ft(3)[:Pt], Ft(3)[:Pt], 0.5, Ft(2)[:Pt], ALU.mult, ALU.mult, [Fk(3), Fk(2)], [Fk(3)])
                kss, ss = self.stcol()
                jk_, jap = self.jkcol()
                self.act(jap[:Pt].broadcast_to([Pt, 512]), Ft(3)[:Pt], AF.Square, [Fk(3)], [jk_, kss], accum_out=ss[:Pt])
                krs, rs = self.rstd_from_ss(ss, kss, 512, Pt)
                if DBG["pool_u"]:
                    self.act(t["utok"][:Pt], Ft(3)[:Pt], AF.Copy, [Fk(3), krs], ["utok"], scale=rs[:Pt])
                else:
                    self.ts(t["utok"][:Pt], Ft(3)[:Pt], rs[:Pt], None, ALU.mult, None, [Fk(3), krs], ["utok"])
                pb = self.psb(7)[:, 0:4 * Pt].rearrange("p (m t) -> p m t", t=Pt)
                for m in range(4):
                    self.tr(pb[:, m, :], t["utok"][:Pt, m * 128:(m + 1) * 128], t["ident_bf"][:Pt, :Pt],
                            ["utok", "ident_bf"], [("ps", 7)])
                for m in range(4):
                    c = g * 4 + m
                    self.act(H[:, 1 + c // 8, c % 8, tsl(ti)], pb[:, m, :], AF.Identity, [("ps", 7), "gssd_col"],
                             [("H", 1 + c // 8, c % 8)], scale=t["gssd_col"][:, c:c + 1])
                Sg = t["S"][:, g * 512:(g + 1) * 512]
                self.tt(Sg.rearrange("p (h d) -> p h d", d=64), Sg.rearrange("p (h d) -> p h d", d=64),
                        sm[:, ti, 7, g * 8:(g + 1) * 8].unsqueeze(2).broadcast_to([128, 8, 64]), ALU.mult,
                        [("S", g), smk], [("S", g)], eng=("gpsimd" if DBG["pool_s"] else "vector"))
                self.tt(Sg, self.PS[5][:, :], Sg, ALU.add, [("ps", 5), ("S", g)], [("S", g)])
                self.act(t["Sb"][:, g * 512:(g + 1) * 512], Sg, AF.Copy, [("S", g)], [("Sb", g)])


_CACHE = {}


def _get_nc():
    if "nc" not in _CACHE:
        bld = Builder()
        p1 = Prog()
        bld.record(p1, None)
        plan = bld.ws.req
        p2 = Prog()
        bld.marks = []
        bld.record(p2, plan)
        if SCHED:
            p2.schedule()
        p2.emit(bld.nc)
        bld.es.close()
        _CACHE["nc"] = bld.nc
        _CACHE["marks"] = bld.marks
    return _CACHE["nc"]


def _consts():
    bf = ml_dtypes.bfloat16
    i = np.arange(128)
    half = 32
    inv = (10000.0 ** (-np.arange(half, dtype=np.float32) / half)).astype(np.float32)
    posp = np.arange(TP, dtype=np.float32)
    angp = posp[:, None] * inv[None, :]
    poss = (PAST + np.arange(TS)).astype(np.float32)
    angs = poss[:, None] * inv[None, :]
    to_tiles = lambda a: np.ascontiguousarray(a.reshape(16, 128, half).transpose(1, 0, 2)).astype(np.float32)
    return {
        "ident_bf": np.eye(128, dtype=np.float32).astype(bf), "ident_f": np.eye(128, dtype=np.float32),
        "ones_bf": np.ones((128, 128), np.float32).astype(bf), "ones_f": np.ones((128, 128), np.float32),
        "uincl": (i[:, None] <= i[None, :]).astype(np.float32),
        "ustrict": (i[:, None] > i[None, :]).astype(np.float32),
        "causal_bf": (i[:, None] <= i[None, :]).astype(np.float32).astype(bf),
        "cosP": to_tiles(np.cos(angp)), "sinP": to_tiles(np.sin(angp)),
        "cosS": np.cos(angs).astype(np.float32), "sinS": np.sin(angs).astype(np.float32),
    }


def kernel(x_prompt, x_sample, cache_k, cache_v, state_conv, state_ssm, c_prompt, c_sample,
           w_ada, b_ada, g_mix, g_ffn, w_in, lambda_q1, lambda_k1, lambda_q2, lambda_k2, g_subln,
           conv_w, conv_b, dt_bias, a_log, d_skip, g_ssd, w_pa, w_pb, w_out, w_gu, w_down, g_final):
    f = lambda a: np.ascontiguousarray(np.asarray(a, dtype=np.float32))
    nc = _get_nc()
    col = lambda v, nchunk: f(np.asarray(v, np.float32).reshape(nchunk, 128).T)
    shared = {
        "w_ada": f(w_ada[0]), "b_ada_col": col(b_ada[0], 48), "gmix_col": col(g_mix[0], 8), "gffn_col": col(g_ffn[0], 8),
        "w_in": f(w_in[0]),
        "lam4": f(np.concatenate([np.asarray(lambda_q1[0]), np.asarray(lambda_k1[0]), np.asarray(lambda_q2[0]),
                                  np.asarray(lambda_k2[0])])),
        "gsub_col": f(np.asarray(g_subln[0]).reshape(128, 1)),
        "convw_col": f(np.asarray(conv_w[0]).reshape(4, 24, 128).transpose(2, 1, 0)),
        "convb_col": col(conv_b[0], 24),
        "dt_bias": f(dt_bias[0]), "a_log": f(a_log[0]), "d_skip": f(d_skip[0]), "gssd_col": col(g_ssd[0], 16),
        "w_pa": f(w_pa[0]), "w_pb": f(w_pb[0]), "w_out": f(w_out[0]), "w_gu": f(w_gu[0]), "w_down": f(w_down[0]),
        "g_final": f(g_final),
    }
    shared.update(_consts())
    xp = np.asarray(x_prompt, np.float32); xs = np.asarray(x_sample, np.float32)
    ck = np.asarray(cache_k, np.float32)[0].reshape(-1, PAST, D)
    cv = np.asarray(cache_v, np.float32)[0].reshape(-1, PAST, D)
    sc = np.asarray(state_conv, np.float32)[0]
    ssm = np.asarray(state_ssm, np.float32)[0].reshape(-1, 2048, 128)
    cp = np.asarray(c_prompt, np.float32); cs = np.asarray(c_sample, np.float32)
    in_maps = []
    for c in range(NCORES):
        sl = slice(c * NSEQ, (c + 1) * NSEQ)
        call = np.concatenate([cp[sl], cs[sl]], axis=0)
        m = dict(shared)
        m.update({
            "xp": f(xp[sl]), "xs": f(xs[sl]), "ck": f(ck[sl]), "cv": f(cv[sl]),
            "scT": f(sc[sl].reshape(NSEQ, 3, 24, 128).transpose(0, 3, 2, 1)),
            "ssmT": f(ssm[sl].transpose(0, 2, 1)),
            "cT": f(call.reshape(8, 8, 128).transpose(2, 1, 0)),
        })
        in_maps.append(m)
    if "ncores" in LIM:
        in_maps = in_maps[:LIM["ncores"]]
    res = run_bass_kernel_spmd(nc, in_maps, core_ids=list(range(len(in_maps))))
    R = res.results
    if "ncores" in LIM:
        _CACHE["R"] = R
        R = [R[i % len(R)] for i in range(NCORES)]
    cat = lambda n: np.concatenate([np.asarray(r[n]) for r in R], axis=0)
    y_p = cat("y_p"); y_s = cat("y_s")
    k_p = cat("k_p").reshape(1, 32, TP, 16, 64); v_p = cat("v_p").reshape(1, 32, TP, 8, 128)
    k_s = cat("k_s").reshape(1, 32, TS, 16, 64); v_s = cat("v_s").reshape(1, 32, TS, 8, 128)
    unconv = lambda a: np.ascontiguousarray(a.transpose(0, 3, 2, 1).reshape(1, 32, 3, 3072))
    unssm = lambda a: np.ascontiguousarray(a.transpose(0, 2, 1).reshape(1, 32, 32, 64, 128))
    out = (y_p, y_s, k_p, v_p, unconv(cat("conv_p")), unssm(cat("ssm_p")), k_s, v_s, unconv(cat("conv_s")),
           unssm(cat("ssm_s")))
    return tuple(np.ascontiguousarray(o, dtype=np.float32) for o in out)
```

```python
import math
from contextlib import ExitStack

import numpy as np
import ml_dtypes

import concourse.bass as bass
import concourse.mybir as mybir
from concourse.bass_utils import run_bass_kernel_spmd

F32 = mybir.dt.float32
BF16 = mybir.dt.bfloat16
AF = mybir.ActivationFunctionType
ALU = mybir.AluOpType

NCORES = 8
NSEQ = 4
D = 1024
TP = 2048
TS = 64
PAST = 1024
NKS = PAST + TS
OFF_K, OFF_V, OFF_Z, OFF_XBC, OFF_DT, OFF_GATE, IN_COLS = 1024, 2048, 3072, 5120, 8192, 8224, 10272
DFF = 2816
EPS = 1e-5
SCALE_A = 0.125
LAM_INIT = 0.8 - 0.6 * math.exp(-0.3 * 0)
NSLOT = 5
ENGS = ("sync", "scalar", "vector", "gpsimd", "tensor")
STOP_AFTER = 99
CONST_KEYS = frozenset(("ident_bf", "ident_f", "ones_bf", "ones_f", "uincl", "ustrict", "ustrict_bf", "causal_bf",
                        "cosP", "sinP", "cosS", "sinS", "b_ada_col", "gmix_col", "gffn_col", "gsub_col", "gssd_col",
                        "convw", "convb", "dtb_b", "alog_b", "dskip_b", "a_b", "gfin_b", "neglam", "gsubs", "mhalf",
                        "modT", "Gm", "Gf", "wdt", "lams", "scT", "cTs"))
SCHED = True
RAW_ONLY = False
RAW_KEEP = ()
LIM = {"np": NSEQ, "ns": NSEQ, "nb": 99, "setup_only": False}
DBG = {"kinds": "kvq", "vscr": True, "kscr": True, "vout": True, "pool_bm": True, "pool_x": True, "pool_res": True,
       "pool_merge": True, "pool_rstd": True, "pool_u": True, "pool_s": True, "pool_sq": False}


class Op:
    __slots__ = ("eng", "fn", "deps", "signal", "sig_val", "dma_slot", "dma_val", "idx", "cost", "odeps", "fin",
                 "nun", "users")

    def __init__(self, eng, fn):
        self.eng = eng
        self.fn = fn
        self.deps = []
        self.odeps = []
        self.cost = 0.3
        self.signal = False
        self.sig_val = 0
        self.dma_slot = None
        self.dma_val = 0


class Prog:
    def __init__(self):
        self.ops = {e: [] for e in ENGS}
        self.res = {}
        self.dma_cnt = {}
        self.all = []

    def add(self, eng, fn, reads=(), writes=(), dma=None, cost=0.3):
        op = Op(eng, fn)
        op.cost = cost
        op.idx = len(self.all)
        self.all.append(op)
        deps = []
        for k in reads:
            st = self.res.get(k)
            if st is not None:
                deps.extend(st[0])
                if isinstance(k, tuple) and k[0] == "ps":
                    deps.extend(r for r in st[1] if r.eng != eng)
        for k in writes:
            st = self.res.get(k)
            if st is not None and not (RAW_ONLY and not (isinstance(k, tuple) and k[0] in RAW_KEEP)):
                deps.extend(st[1] if st[1] else st[0])
        for k in reads:
            if k in CONST_KEYS:
                continue
            st = self.res.setdefault(k, [[], []])
            if dma is None:
                keep = []
                for r in st[1]:
                    if r.dma_slot is not None or r.eng != eng:
                        keep.append(r)
                    elif r is not op:
                        op.odeps.append(r)
                st[1] = keep
            st[1].append(op)
        for k in writes:
            st = self.res.setdefault(k, [[], []])
            if st[1]:
                st[0] = [op]
                st[1] = []
            else:
                if dma is None:
                    keep = []
                    for w in st[0]:
                        if w.dma_slot is not None or w.eng != eng:
                            keep.append(w)
                        elif w is not op:
                            op.odeps.append(w)
                    st[0] = keep
                st[0].append(op)
        seen = set()
        for d in deps:
            if d is op or id(d) in seen:
                continue
            seen.add(id(d))
            if d.dma_slot is None and d.eng == "tensor" and eng == "tensor" and dma is None:
                op.odeps.append(d)
                continue
            op.deps.append(d)
            if d.dma_slot is None:
                d.signal = True
        if dma is not None:
            op.dma_slot = dma
            c = self.dma_cnt.get(dma, 0) + 1
            self.dma_cnt[dma] = c
            op.dma_val = 16 * c
        self.ops[eng].append(op)
        return op

    def schedule(self, window=120):
        import heapq
        for op in self.all:
            op.fin = None
            op.users = []
        inorder = ("sync",)
        prev = None
        for op in self.ops["gpsimd"]:
            if op.dma_slot is not None:
                if prev is not None:
                    op.odeps.append(prev)
                prev = op
        for op in self.all:
            ds = op.deps + op.odeps
            op.nun = len(ds)
            for d in ds:
                d.users.append(op)
        rem = {e: list(self.ops[e]) for e in ENGS}
        ptr = {e: 0 for e in ENGS}
        win = {e: [] for e in ENGS}
        nxt = {e: 0 for e in ENGS}
        etime = {e: 0.0 for e in ENGS}
        order = {e: [] for e in ENGS}
        LAT = 0.25

        def ready_time(op):
            t = 0.0
            for d in op.deps:
                lt = d.fin + (LAT if (d.eng != op.eng or d.dma_slot is not None) else 0.08)
                if lt > t:
                    t = lt
            for d in op.odeps:
                if d.fin - d.cost * 0.5 > t:
                    t = d.fin - d.cost * 0.5
            return t

        def refill(e):
            w = win[e]
            while len(w) < window and nxt[e] < len(rem[e]):
                w.append(rem[e][nxt[e]])
                nxt[e] += 1

        total = len(self.all)
        done = 0
        for e in ENGS:
            if e not in inorder:
                refill(e)
        while done < total:
            best = None
            for e in ENGS:
                if e in inorder:
                    if ptr[e] >= len(rem[e]):
                        continue
                    op = rem[e][ptr[e]]
                    if op.nun > 0:
                        continue
                    st = max(etime[e], ready_time(op))
                    cand = (st, op.idx, e, op)
                else:
                    cand = None
                    for op in win[e]:
                        if op.nun > 0:
                            continue
                        st = max(etime[e], ready_time(op))
                        c = (st, op.idx, e, op)
                        if cand is None or c < cand:
                            cand = c
                    if cand is None:
                        continue
                if best is None or cand < best:
                    best = cand
            assert best is not None, "scheduler stuck"
            st, _, e, op = best
            if op.dma_slot is not None:
                issue = 1.0 if e == "gpsimd" else 0.15
                etime[e] = st + issue
                op.fin = st + issue + 2.0 + op.cost
            else:
                etime[e] = st + op.cost
                op.fin = st + op.cost
            order[e].append(op)
            if e in inorder:
                ptr[e] += 1
            else:
                win[e].remove(op)
                refill(e)
            for u in op.users:
                u.nun -= 1
            done += 1
        self.ops = order
        self.makespan = max(etime.values())

    def emit(self, nc):
        for e in ENGS:
            c = 0
            for op in self.ops[e]:
                if op.dma_slot is None and op.signal:
                    c += 1
                    op.sig_val = c
        with ExitStack() as es:
            esem = {e: es.enter_context(nc.semaphore("s_" + e)) for e in ENGS}
            dsem = {}
            for i, slot in enumerate(self.dma_cnt):
                dsem[slot] = es.enter_context(nc.semaphore("d%d" % i))
            block = es.enter_context(nc.Block())
            prog = self

            def body(ename):
                def _f(eng):
                    seen = {}
                    for op in prog.ops[ename]:
                        for d in op.deps:
                            if d.dma_slot is not None:
                                sem, val = dsem[d.dma_slot], d.dma_val
                            else:
                                sem, val = esem[d.eng], d.sig_val
                            if seen.get(id(sem), 0) >= val:
                                continue
                            seen[id(sem)] = val
                            eng.wait_ge(sem, val)
                        inst = op.fn(eng)
                        if op.dma_slot is not None:
                            inst.then_inc(dsem[op.dma_slot], 16)
                        elif op.signal:
                            inst.then_inc(esem[ename], 1)
                    if ename == "sync":
                        for slot, c in prog.dma_cnt.items():
                            if seen.get(id(dsem[slot]), 0) < 16 * c:
                                eng.wait_ge(dsem[slot], 16 * c)
                return _f

            block.sync(body("sync"))
            block.scalar(body("scalar"))
            block.vector(body("vector"))
            block.gpsimd(body("gpsimd"))
            block.tensor(body("tensor"))


class WStream:
    def __init__(self, P, wsl, plan):
        self.P = P
        self.wsl = wsl
        self.plan = plan
        self.req = []
        self.issued = 0

    def _issue(self, i, src, kc, ncol, skey):
        s = i % NSLOT
        dst = self.wsl[:, s, 0:kc, 0:ncol]
        self.P.add("sync", lambda e: e.dma_start(out=dst, in_=src), reads=list(skey), writes=[("w", s)],
                   dma=("w", s), cost=kc * ncol * 256 / 150e3)

    def get(self, src, kc, ncol, skey):
        i = len(self.req)
        self.req.append((src, kc, ncol, skey))
        if self.plan is None:
            self._issue(i, src, kc, ncol, skey)
        else:
            lim = min(i + NSLOT - 1, len(self.plan) - 1)
            while self.issued <= lim:
                self._issue(self.issued, *self.plan[self.issued])
                self.issued += 1
        s = i % NSLOT
        return self.wsl[:, s], ("w", s)


def wkeys(wn, r0, nrows, c0, ncols):
    return [("wb", wn, r, c) for r in range(r0 // 1024, (r0 + nrows - 1) // 1024 + 1)
            for c in range(c0 // 2048, (c0 + ncols - 1) // 2048 + 1)]


class Builder:
    def __init__(self):
        self.nc = nc = bass.Bass("TRN2", target_bir_lowering=False)
        self.es = ExitStack()
        self.t = {}
        self.dr = {}
        di = lambda n, s, d=F32: self.dr.__setitem__(n, nc.dram_tensor(n, list(s), d, kind="ExternalInput").ap())
        do = lambda n, s, d=F32: self.dr.__setitem__(n, nc.dram_tensor(n, list(s), d, kind="ExternalOutput").ap())
        ds = lambda n, s, d=BF16: self.dr.__setitem__(n, nc.dram_tensor(n, list(s), d, kind="Internal").ap())
        di("xp", [NSEQ, TP, D]); di("xs", [NSEQ, TS, D])
        di("ck", [NSEQ, PAST, D]); di("cv", [NSEQ, PAST, D])
        di("scT", [NSEQ, 128, 24, 3]); di("ssmT", [NSEQ, 128, 2048])
        di("cT", [128, 8, 8])
        di("w_ada", [D, 6 * D]); di("b_ada_col", [128, 48]); di("gmix_col", [128, 8]); di("gffn_col", [128, 8])
        di("w_in", [D, IN_COLS]); di("lam4", [4 * 64]); di("gsub_col", [128, 1])
        di("convw_col", [128, 24, 4]); di("convb_col", [128, 24])
        di("dt_bias", [32]); di("a_log", [32]); di("d_skip", [32]); di("gssd_col", [128, 16])
        di("w_pa", [D, D]); di("w_pb", [2 * D, D]); di("w_out", [D, D]); di("w_gu", [D, 2 * DFF]); di("w_down", [DFF, D])
        di("g_final", [D])
        di("ident_bf", [128, 128], BF16); di("ident_f", [128, 128]); di("ones_bf", [128, 128], BF16)
        di("ones_f", [128, 128]); di("uincl", [128, 128]); di("ustrict", [128, 128]); di("causal_bf", [128, 128], BF16)
        di("cosP", [128, 16, 32]); di("sinP", [128, 16, 32]); di("cosS", [64, 32]); di("sinS", [64, 32])
        do("y_p", [NSEQ, TP, D]); do("y_s", [NSEQ, TS, D])
        do("k_p", [NSEQ, TP, D]); do("v_p", [NSEQ, TP, D])
        do("conv_p", [NSEQ, 128, 24, 3]); do("ssm_p", [NSEQ, 128, 2048])
        do("k_s", [NSEQ, TS, D]); do("v_s", [NSEQ, TS, D])
        do("conv_s", [NSEQ, 128, 24, 3]); do("ssm_s", [NSEQ, 128, 2048])
        ds("wb_ada", [D, 6 * D]); ds("wb_in", [D, IN_COLS]); ds("wb_pa", [D, D]); ds("wb_pb", [2 * D, D])
        ds("wb_out", [D, D]); ds("wb_gu", [D, 2 * DFF]); ds("wb_down", [DFF, D])
        ds("kscr", [2 * NSEQ, 128, 8, TP]); ds("vscr", [2 * NSEQ, 8, TP, 128])

        sb = lambda n, s, d=F32: self.t.__setitem__(n, self.es.enter_context(nc.sbuf_tensor("sb_" + n, list(s), d)))
        sb("ident_bf", [128, 128], BF16); sb("ident_f", [128, 128]); sb("ones_bf", [128, 128], BF16)
        sb("ones_f", [128, 128]); sb("uincl", [128, 128]); sb("ustrict", [128, 128]); sb("causal_bf", [128, 128], BF16)
        sb("ustrict_bf", [128, 128], BF16)
        sb("cosP", [128, 16, 32]); sb("sinP", [128, 16, 32]); sb("cosS", [64, 32]); sb("sinS", [64, 32])
        sb("b_ada_col", [128, 48]); sb("gmix_col", [128, 8]); sb("gffn_col", [128, 8]); sb("gsub_col", [128, 1])
        sb("convw", [128, 24, 4]); sb("convb", [128, 24]); sb("gssd_col", [128, 16])
        sb("dtb_b", [128, 32]); sb("alog_b", [128, 32]); sb("dskip_b", [128, 32]); sb("a_b", [128, 32])
        sb("gfin_b", [128, D]); sb("lams", [128, 4])
        sb("neglam", [128, 1]); sb("gsubs", [128, 1]); sb("mhalf", [128, 1])
        sb("cTs", [128, 8, 8]); sb("scT", [128, 8, 8], BF16)
        sb("modT", [128, 48, 8]); sb("Gm", [128, 8, 8]); sb("Gf", [128, 8, 8]); sb("onep", [128, 8, 8])
        sb("gb", [128, 128]); sb("gtm_b", [128, D]); sb("gtf_b", [128, D])
        sb("wdt", [128, 8, 32], BF16)
        sb("wsl", [128, NSLOT, 8, 512], BF16)
        sb("S", [128, 2048]); sb("Sb", [128, 2048], BF16); sb("tail", [128, 24, 3])
        sb("H", [128, 5, 8, 512], BF16)
        sb("TM", [128, 4, D]); sb("TB", [128, 4, D], BF16)
        sb("kTst", [128, 2, 4, 128], BF16)
        sb("kTj", [128, 2, TP], BF16); sb("Vj", [128, 2, 16, 128], BF16)
        sb("PT", [128, 4, 512], BF16)
        sb("Ft", [128, 6, 512]); sb("sq", [128, 512], BF16)
        sb("raw", [128, 2, 516])
        sb("xsT", [128, 4, 512], BF16); sb("xstok", [128, 4, 512], BF16); sb("Btok", [128, 4, 4, 128], BF16)
        sb("E", [128, 1024], BF16); sb("MT", [128, 1024], BF16); sb("cbm", [128, 128], BF16)
        sb("xdt", [128, 512], BF16); sb("xw", [128, 512], BF16); sb("xsd", [128, 512], BF16); sb("utok", [128, 512], BF16)
        sb("sm", [128, 4, 8, 32])
        sb("st", [128, 16])
        sb("jk", [128, 16], BF16)
        self.t["lamv"] = self.t["Ft"][:, 0, 0:256].rearrange("p (a b) -> p a b", b=64)
        self.t["lamt"] = self.t["Ft"][:, 1, 0:128].rearrange("p (a b) -> p a b", b=64)
        self.t["junk"] = self.t["E"]
        self.PS = [self.es.enter_context(nc.psum_tensor("ps%d" % i, [128, 512], F32)) for i in range(8)]
        self.st_i = 0

    def op(self, eng, meth, reads, writes, *a, **kw):
        o = kw.get("out", a[0] if a else None)
        n = 1
        for d in o.shape[1:]:
            n *= d
        if eng == "tensor":
            if meth == "transpose":
                cost = 0.06
            else:
                r = kw["rhs"]
                n = 1
                for d in r.shape[1:]:
                    n *= d
                cost = (max(n, 64) / 2.2 + 15) * 1e-3 * (4.0 if r.dtype == F32 else 1.0)
        elif eng == "scalar":
            cost = 0.2 + 0.6e-3 * n
        elif eng == "gpsimd":
            cost = 0.25 + 1.5e-3 * n
        else:
            cost = 0.1 + 1.1e-3 * n * (8.0 if meth == "reciprocal" else (1.7 if meth == "reciprocal_approx_fast" else 1.0))
        return self.P.add(eng, lambda e: getattr(e, meth)(*a, **kw), reads, writes, cost=cost)

    def dma(self, eng, out, in_, reads, writes, slot, **kw):
        n = 1
        for d in out.shape:
            n *= d
        nbytes = n * (2 if out.dtype == BF16 else 4)
        return self.P.add(eng, lambda e: e.dma_start(out=out, in_=in_, **kw), reads, writes, dma=slot,
                          cost=nbytes / 150e3)

    def mm(self, out, lhsT, rhs, start, stop, reads, writes):
        return self.op("tensor", "matmul", reads, writes, out, lhsT=lhsT, rhs=rhs, start=start, stop=stop)

    def tr(self, out, in_, ident, reads, writes):
        return self.op("tensor", "transpose", reads, writes, out=out, in_=in_, identity=ident)

    def act(self, out, in_, func, reads, writes, **kw):
        return self.op("scalar", "activation", reads, writes, out=out, in_=in_, func=func, **kw)

    def tt(self, out, in0, in1, op, reads, writes, eng="vector"):
        return self.op(eng, "tensor_tensor", reads, writes, out=out, in0=in0, in1=in1, op=op)

    def ts(self, out, in0, s1, s2, op0, op1, reads, writes, eng="vector"):
        if op1 is None:
            return self.op(eng, "tensor_scalar", reads, writes, out=out, in0=in0, scalar1=s1, scalar2=None, op0=op0)
        return self.op(eng, "tensor_scalar", reads, writes, out=out, in0=in0, scalar1=s1, scalar2=s2, op0=op0, op1=op1)

    def stt(self, out, in0, scalar, in1, op0, op1, reads, writes):
        return self.op("vector", "scalar_tensor_tensor", reads, writes, out=out, in0=in0, scalar=scalar, in1=in1,
                       op0=op0, op1=op1)

    def stcol(self):
        i = self.st_i % 16
        self.st_i += 1
        return ("st", i), self.t["st"][:, i:i + 1]

    def jkcol(self):
        i = self.st_i % 16
        return ("jk", i), self.t["jk"][:, i:i + 1]

    def psb(self, i):
        return self.PS[i][:].bitcast(BF16)

    def record(self, P, plan):
        self.P = P
        self.ws = WStream(P, self.t["wsl"], plan)
        self.st_i = 0
        self.tmh_i = 0
        self.setup()
        if LIM["setup_only"]:
            return
        for s in range(LIM["np"]):
            self.sequence(s, prompt=True)
        for s in range(LIM["ns"]):
            self.sequence(s, prompt=False)

    def setup(self):
        t, dr = self.t, self.dr
        for n in ("ident_bf", "ident_f", "ones_bf", "ones_f", "uincl", "ustrict", "causal_bf", "cosP", "sinP",
                  "cosS", "sinS", "b_ada_col", "gmix_col", "gffn_col", "gsub_col", "gssd_col"):
            self.dma("sync", t[n][:], dr[n], [], [n], ("c", n))
        self.dma("sync", t["convw"][:], dr["convw_col"], [], ["convw"], ("c", "convw"))
        self.dma("sync", t["convb"][:], dr["convb_col"], [], ["convb"], ("c", "convb"))
        self.dma("sync", t["cTs"][:], dr["cT"], [], ["cTs"], ("c", "cTs"))
        self.dma("sync", t["dtb_b"][:], dr["dt_bias"].partition_broadcast(128), [], ["dtb_b"], ("c", "dtb"))
        self.dma("sync", t["alog_b"][:], dr["a_log"].partition_broadcast(128), [], ["alog_b"], ("c", "alog"))
        self.dma("sync", t["dskip_b"][:], dr["d_skip"].partition_broadcast(128), [], ["dskip_b"], ("c", "dskip"))
        self.dma("sync", t["gfin_b"][:], dr["g_final"].partition_broadcast(128), [], ["gfin_b"], ("c", "gfin"))
        self.dma("sync", t["Ft"][:, 0, 0:256], dr["lam4"].partition_broadcast(128), [],
                 [("F", 0)], ("c", "lamv"))
        shapes = {"ada": (D, 6 * D), "in": (D, IN_COLS), "pa": (D, D), "pb": (2 * D, D), "out": (D, D),
                  "gu": (D, 2 * DFF), "down": (DFF, D)}

        self.cast_i = 0

        def cast(wn, cols=None):
            rows, ncols = shapes[wn]
            src = dr["w_" + wn]
            dst = dr["wb_" + wn]
            for c0 in (range(0, ncols, 2048) if cols is None else cols):
                c1 = min(ncols, c0 + 2048)
                for r0 in range(0, rows, 1024):
                    r1 = min(rows, r0 + 1024)
                    k = self.cast_i
                    self.cast_i += 1
                    self.dma("gpsimd", dst[r0:r1, c0:c1], src[r0:r1, c0:c1], ([("cc", k - 2)] if k >= 2 else []),
                             [("wb", wn, r0 // 1024, c0 // 2048), ("cc", k)],
                             ("wb", wn, r0 // 1024, c0 // 2048))
        cast("ada")
        cast("in", [0, 2048])
        cast("pa")
        cast("in", [4096, 6144, 8192, 10240])
        cast("pb"); cast("out"); cast("gu"); cast("down")
        lamv = t["lamv"]
        self.tt(t["lamt"][:, 0, :], lamv[:, 0, :], lamv[:, 1, :], ALU.mult, [("F", 0)], [("F", 1)])
        self.tt(t["lamt"][:, 1, :], lamv[:, 2, :], lamv[:, 3, :], ALU.mult, [("F", 0)], [("F", 1)])
        self.op("vector", "tensor_reduce", [("F", 1)], ["lams"], out=t["lams"][:, 0:2], in_=t["lamt"],
                axis=mybir.AxisListType.X, op=ALU.add)
        self.act(t["lams"][:, 2:4], t["lams"][:, 0:2], AF.Exp, ["lams"], ["lams"])
        self.tt(t["neglam"][:], t["lams"][:, 3:4], t["lams"][:, 2:3], ALU.subtract, ["lams"], ["neglam"])
        self.ts(t["neglam"][:], t["neglam"][:], -LAM_INIT, None, ALU.add, None, ["neglam"], ["neglam"])
        self.ts(t["gsubs"][:], t["gsub_col"][:], 1.0 - LAM_INIT, None, ALU.mult, None, ["gsub_col"], ["gsubs"])
        self.op("vector", "memset", [], ["mhalf"], t["mhalf"][:], -0.5)
        self.op("vector", "tensor_copy", ["ustrict"], ["ustrict_bf"], out=t["ustrict_bf"][:], in_=t["ustrict"][:])
        self.ts(t["convw"][:], t["convw"][:], 0.5, None, ALU.mult, None, ["convw"], ["convw"])
        self.ts(t["convb"][:], t["convb"][:], 0.5, None, ALU.mult, None, ["convb"], ["convb"])
        self.act(t["a_b"][:], t["alog_b"][:], AF.Exp, ["alog_b"], ["a_b"])
        self.ts(t["a_b"][:], t["a_b"][:], -1.0, None, ALU.mult, None, ["a_b"], ["a_b"])
        self.dma("sync", t["wdt"][:], dr["wb_in"][:, OFF_DT:OFF_DT + 32].rearrange("(c p) n -> p c n", p=128),
                 wkeys("in", 0, 1024, OFF_DT, 32), ["wdt"], ("c", "wdt"))
        self.act(t["scT"][:], t["cTs"][:], AF.Silu, ["cTs"], ["scT"])
        ps = self.PS[0]
        for g in range(12):
            wsl, wk = self.ws.get(dr["wb_ada"][:, g * 512:(g + 1) * 512].rearrange("(c p) n -> p c n", p=128), 8, 512,
                                  wkeys("ada", 0, 1024, g * 512, 512))
            for m in range(4):
                c = g * 4 + m
                for k in range(8):
                    self.mm(ps[:, c * 8:(c + 1) * 8], wsl[:, k, m * 128:(m + 1) * 128], t["scT"][:, k, :],
                            k == 0, k == 7, [wk, "scT"], [("ps", 0)])
        self.tt(t["modT"][:], ps[:, 0:384].rearrange("p (c b) -> p c b", b=8),
                t["b_ada_col"][:].unsqueeze(2).broadcast_to([128, 48, 8]), ALU.add, [("ps", 0), "b_ada_col"], ["modT"])
        for nm, base, gcol in (("Gm", 8, "gmix_col"), ("Gf", 32, "gffn_col")):
            self.ts(t["onep"][:], t["modT"][:, base:base + 8, :], 1.0, None, ALU.add, None, ["modT"], ["onep"])
            self.tt(t[nm][:], t["onep"][:], t[gcol][:].unsqueeze(2).broadcast_to([128, 8, 8]), ALU.mult,
                    ["onep", gcol], [nm])

    def gate_rows(self, b):
        t = self.t
        for nm, base, bank in (("gtm_b", 16, 0), ("gtf_b", 40, 2)):
            for half in range(2):
                ps = self.PS[bank + half]
                for cc in range(4):
                    c = half * 4 + cc
                    self.ts(t["gb"][:], t["ones_f"][:], t["modT"][:, base + c, b:b + 1], None, ALU.mult, None,
                            ["ones_f", "modT"], ["gb"])
                    self.mm(ps[:, cc * 128:(cc + 1) * 128], t["gb"][:], t["ident_f"][:], True, True,
                            ["gb", "ident_f"], [("ps", bank + half)])
                self.act(t[nm][:, half * 512:(half + 1) * 512], ps[:], AF.Copy, [("ps", bank + half)], [nm],
                         scale=(0.5 if nm == "gtf_b" else 1.0))

    def rstd_from_ss(self, ss_ap, ss_key, n, Pt):
        k1, a1 = self.stcol()
        self.ts(a1[:Pt], ss_ap[:Pt], 1.0 / n, EPS, ALU.mult, ALU.add, [ss_key], [k1], eng="gpsimd")
        k2, a2 = self.stcol()
        self.tt(a2[:Pt], a1[:Pt], self.t["mhalf"][:Pt], ALU.pow, [k1, "mhalf"], [k2], eng="gpsimd")
        return k2, a2

    def norm_to_hT(self, src, src_keys, Pt, G, S, Gk, Sk, b, hbuf, t_idx):
        t = self.t
        kss, ss = self.stcol()
        jk_, jap = self.jkcol()
        self.act(jap[:Pt].broadcast_to([Pt, D]), src, AF.Square, src_keys, [jk_, kss], accum_out=ss[:Pt])
        krs, rs = self.rstd_from_ss(ss, kss, D, Pt)
        xn = t["TB"][:Pt, t_idx % 2, :]
        xk = ("TB", t_idx % 2)
        self.ts(xn, src, rs[:Pt], None, ALU.mult, None, src_keys + [krs], [xk])
        pb = self.psb(7)[:, 0:8 * Pt].rearrange("p (c t) -> p c t", t=Pt)
        for c in range(8):
            self.tr(pb[:, c, :], xn[:, c * 128:(c + 1) * 128], t["ident_bf"][:Pt, :Pt], [xk, "ident_bf"], [("ps", 7)])
        for c in range(8):
            self.act(t["H"][:, hbuf, c, t_idx * Pt:(t_idx + 1) * Pt], pb[:, c, :], AF.Identity,
                     [("ps", 7), Gk, Sk], [("H", hbuf, c)], scale=G[:, c, b:b + 1], bias=S[:, c, b:b + 1])

    def tmh(self):
        i = self.tmh_i % 4
        self.tmh_i += 1
        return ("TM", 2 + i // 2, i % 2), self.t["TM"][:, 2 + i // 2, (i % 2) * 512:(i % 2 + 1) * 512], \
            ("TB", 2 + i // 2, i % 2), self.t["TB"][:, 2 + i // 2, (i % 2) * 512:(i % 2 + 1) * 512], i

    def k_to_scratch(self, kbf, kbf_key, Pt, sidx, pos, g):
        t = self.t
        pb = self.psb(6)[:, 0:4 * Pt].rearrange("p (c t) -> p c t", t=Pt)
        for c in range(4):
            self.tr(pb[:, c, :], kbf[:, c * 128:(c + 1) * 128], t["ident_bf"][:Pt, :Pt], [kbf_key, "ident_bf"],
                    [("ps", 6)])
        i = self.kst_i % 2
        self.kst_i += 1
        st = t["kTst"][:, i, :, 0:Pt]
        self.act(st, pb, AF.Copy, [("ps", 6)], [("kTst", i)])
        self.dma("scalar", self.dr["kscr"][sidx, :, 4 * g:4 * g + 4, pos:pos + Pt], st, [("kTst", i)],
                 [("kscr", sidx)], ("kst", i))

    def sequence(self, s, prompt):
        t, dr = self.t, self.dr
        self.kst_i = 0
        b = s if prompt else NSEQ + s
        sidx = b
        T = TP if prompt else TS
        Tb = 512 if prompt else 64
        Pt = 128 if prompt else 64
        nT = Tb // Pt
        x_in = dr["xp"][s] if prompt else dr["xs"][s]
        self.gate_rows(b)
        if prompt:
            self.op("vector", "memset", [], ["tail"], t["tail"][:], 0.0)
            self.op("vector", "memset", [], [("S", g) for g in range(4)], t["S"][:], 0.0)
            self.op("vector", "memset", [], [("Sb", g) for g in range(4)], t["Sb"][:], 0.0)
        else:
            self.dma("sync", t["tail"][:], dr["scT"][s], [], ["tail"], ("c", "tail"))
            self.dma("sync", t["S"][:], dr["ssmT"][s], [], [("S", g) for g in range(4)], ("c", "S"))
            for g in range(4):
                self.act(t["Sb"][:, g * 512:(g + 1) * 512], t["S"][:, g * 512:(g + 1) * 512], AF.Copy, [("S", g)],
                         [("Sb", g)])
            for r0 in range(0, PAST, 128):
                self.dma("gpsimd", dr["vscr"][sidx, :, r0:r0 + 128, :].rearrange("j t e -> t j e"),
                         dr["cv"][s, r0:r0 + 128, :].rearrange("t (j e) -> t j e", e=128), [], [("vscr", sidx)],
                         ("pastv", sidx))
            for kt in range(PAST // 128):
                for g in range(2):
                    _, _, kb, ap, i = self.tmh()
                    self.dma("gpsimd", ap, dr["ck"][s, kt * 128:(kt + 1) * 128, g * 512:(g + 1) * 512], [], [kb],
                             ("pastk", i))
                    self.k_to_scratch(ap, kb, 128, sidx, kt * 128, g)
        for qb in range(min(T // Tb, LIM["nb"])):
            self.block(s, b, sidx, prompt, qb, Tb, Pt, nT, x_in)
        oc = dr["conv_p"] if prompt else dr["conv_s"]
        osm = dr["ssm_p"] if prompt else dr["ssm_s"]
        self.dma("scalar", oc[s], t["tail"][:], ["tail"], [], ("o", "tail"))
        self.dma("scalar", osm[s], t["S"][:], [("S", g) for g in range(4)], [], ("o", "S"))

    def mark(self, name):
        if not hasattr(self, "marks"):
            self.marks = []
        self.marks.append((name, len(self.P.ops["tensor"])))

    def block(self, s, b, sidx, prompt, qb, Tb, Pt, nT, x_in):
        t, dr, ws = self.t, self.dr, self.ws
        self.mark("blk s%d p%d qb%d norm1" % (s, prompt, qb))
        H = t["H"]
        pos0 = qb * Tb
        hk = lambda hb: [("H", hb, c) for c in range(8)]
        tsl = lambda ti: slice(ti * Pt, (ti + 1) * Pt)
        wsrc = lambda name, r0, kc, c0, nc_: dr["wb_" + name][r0:r0 + kc * 128, c0:c0 + nc_].rearrange(
            "(c p) n -> p c n", p=128)
        cosT = (lambda ti: t["cosP"][:, qb * 4 + ti, :]) if prompt else (lambda ti: t["cosS"][:, :])
        sinT = (lambda ti: t["sinP"][:, qb * 4 + ti, :]) if prompt else (lambda ti: t["sinS"][:, :])
        k_out = dr["k_p"] if prompt else dr["k_s"]
        v_out = dr["v_p"] if prompt else dr["v_s"]
        y_out = dr["y_p"] if prompt else dr["y_s"]
        Fk = lambda i: ("F", i)
        Ft = lambda i: t["Ft"][:, i, :]

        for ti in range(nT):
            xs_ = t["raw"][:].rearrange("p a b -> p (a b)")[:Pt, 0:D]
            xkeys = [("raw", 0), ("raw", 1)]
            self.dma("scalar", xs_, x_in[pos0 + ti * Pt:pos0 + (ti + 1) * Pt, :], [], xkeys, ("xin", 0))
            self.norm_to_hT(xs_, xkeys, Pt, t["Gm"], t["modT"], "Gm", "modT", b, 0, ti)
        if STOP_AFTER < 1:
            return
        self.mark("qkv")
        for grp in range(6):
            kind = "kvq"[grp // 2]
            g = grp % 2
            if kind not in DBG["kinds"]:
                continue
            c0 = {"k": OFF_K, "v": OFF_V, "q": 0}[kind] + g * 512
            wsl, wk = ws.get(wsrc("in", 0, 8, c0, 512), 8, 512, wkeys("in", 0, 1024, c0, 512))
            for ti in range(nT):
                bank = (grp * nT + ti) % 2
                ps = self.PS[bank][:Pt, :]
                pk = ("ps", bank)
                for k in range(8):
                    self.mm(ps, H[:, 0, k, tsl(ti)], wsl[:, k, :], k == 0, k == 7, [("H", 0, k), wk], [pk])
                fk, fap, bk, bap, _ = self.tmh()
                fap = fap[:Pt]
                bap = bap[:Pt]
                pos = pos0 + ti * Pt
                kpos = pos if prompt else PAST + pos
                if kind == "v":
                    self.act(fap, ps, AF.Copy, [pk], [fk])
                    self.op("vector", "tensor_copy", [fk], [bk], out=bap, in_=fap)
                    if DBG["vout"]:
                        self.dma("scalar", v_out[s, pos:pos + Pt, g * 512:(g + 1) * 512], fap, [fk], [], ("ov", fk))
                    if DBG["vscr"]:
                      self.dma("scalar", dr["vscr"][sidx, 4 * g:4 * g + 4, kpos:kpos + Pt, :].rearrange("j t e -> t j e"),
                             bap.rearrange("t (j e) -> t j e", e=128), [bk], [("vscr", sidx)], ("sv", bk))
                    continue
                p4 = ps.rearrange("t (h two d) -> t h two d", two=2, d=32)
                cs = cosT(ti)[:Pt].unsqueeze(1).unsqueeze(1).broadcast_to([Pt, 8, 2, 32])
                sn = sinT(ti)[:Pt].unsqueeze(1).broadcast_to([Pt, 8, 32])
                A = Ft(0)[:Pt].rearrange("t (h two d) -> t h two d", two=2, d=32)
                B = Ft(1)[:Pt].rearrange("t (h two d) -> t h two d", two=2, d=32)
                self.tt(A, p4, cs, ALU.mult, [pk, "cosP"], [Fk(0)])
                self.tt(B[:, :, 0, :], p4[:, :, 1, :], sn, ALU.mult, [pk, "sinP"], [Fk(1)])
                self.tt(B[:, :, 1, :], p4[:, :, 0, :], sn, ALU.mult, [pk, "sinP"], [Fk(1)])
                if kind == "k":
                    o4 = fap.rearrange("t (h two d) -> t h two d", two=2, d=32)
                    self.tt(o4[:, :, 0, :], A[:, :, 0, :], B[:, :, 0, :], ALU.subtract, [Fk(0), Fk(1)], [fk])
                    self.tt(o4[:, :, 1, :], A[:, :, 1, :], B[:, :, 1, :], ALU.add, [Fk(0), Fk(1)], [fk])
                    self.act(bap, fap, AF.Copy, [fk], [bk])
                    self.dma("scalar", k_out[s, pos:pos + Pt, g * 512:(g + 1) * 512], fap, [fk], [], ("ok", fk))
                    if DBG["kscr"]:
                        self.k_to_scratch(bap, bk, Pt, sidx, kpos, g)
                else:
                    o4 = bap.rearrange("t (h two d) -> t h two d", two=2, d=32)
                    self.tt(o4[:, :, 0, :], A[:, :, 0, :], B[:, :, 0, :], ALU.subtract, [Fk(0), Fk(1)], [bk])
                    self.tt(o4[:, :, 1, :], A[:, :, 1, :], B[:, :, 1, :], ALU.add, [Fk(0), Fk(1)], [bk])
                    pb = self.psb(6)[:, 0:4 * Pt].rearrange("p (c t) -> p c t", t=Pt)
                    for c in range(4):
                        self.tr(pb[:, c, :], bap[:, c * 128:(c + 1) * 128], t["ident_bf"][:Pt, :Pt],
                                [bk, "ident_bf"], [("ps", 6)])
                    self.act(H[:, 1, 4 * g:4 * g + 4, tsl(ti)], pb, AF.Copy, [("ps", 6)],
                             [("H", 1, 4 * g + c) for c in range(4)])
        if STOP_AFTER < 2:
            return
        self.mark("attn")
        if prompt:
            nkeys = pos0 + Tb
            ktiles = []
            for kt in range(nkeys // 128):
                i = kt - pos0 // 128
                ktiles.append((kt * 128, 128, 0 if i <= 0 else 128 * i, i >= 0))
        else:
            nkeys = NKS
            ktiles = [(kt * 128, 128, 0, False) for kt in range(PAST // 128)] + [(PAST, 64, 0, False)]
        nfull = nkeys // 128
        self.attention(sidx, nkeys, nfull, ktiles, Tb, prompt)
        if STOP_AFTER < 3:
            return
        self.mark("wpa")
        for ch in range(2):
            wsl, wk = ws.get(wsrc("pa", 0, 8, ch * 512, 512), 8, 512, wkeys("pa", 0, 1024, ch * 512, 512))
            for m in range(4):
                c = ch * 4 + m
                bank = c % 4
                ps = self.PS[bank][:, 0:Tb]
                for k in range(8):
                    self.mm(ps, wsl[:, k, m * 128:(m + 1) * 128], H[:, 2, k, 0:Tb], k == 0, k == 7,
                            [wk, ("H", 2, k)], [("ps", bank)])
                self.act(H[:, 3, c, 0:Tb], ps, AF.Copy, [("ps", bank)], [("H", 3, c)], scale=0.5)
        if STOP_AFTER < 4:
            return
        self.mark("ssd")
        self.ssd(s, b, prompt, qb, Tb, Pt, nT)
        if STOP_AFTER < 5:
            return
        self.mark("wpb")
        for ch in range(2):
            for kg in range(2):
                w_, wk_ = ws.get(wsrc("pb", kg * 1024, 8, ch * 512, 512), 8, 512, wkeys("pb", kg * 1024, 1024, ch * 512, 512))
                for m in range(4):
                    for kk in range(8):
                        k = kg * 8 + kk
                        self.mm(self.PS[m][:, 0:Tb], w_[:, kk, m * 128:(m + 1) * 128], H[:, 1 + kg, kk, 0:Tb], k == 0,
                                k == 15, [wk_, ("H", 1 + kg, kk)], [("ps", m)])
            for m in range(4):
                c = ch * 4 + m
                self.act(H[:, 4, c, 0:Tb], self.PS[m][:, 0:Tb], AF.Copy, [("ps", m)], [("H", 4, c)], scale=0.5)
        self.mark("gates")
        for gi in range(4):
            wsl, wk = ws.get(wsrc("in", 0, 8, OFF_GATE + gi * 512, 512), 8, 512, wkeys("in", 0, 1024, OFF_GATE + gi * 512, 512))
            for m in range(4):
                c = (gi % 2) * 4 + m
                bank = (gi * 4 + m) % 4
                ps = self.PS[bank][:, 0:Tb]
                for k in range(8):
                    self.mm(ps, wsl[:, k, m * 128:(m + 1) * 128], H[:, 0, k, 0:Tb], k == 0, k == 7,
                            [wk, ("H", 0, k)], [("ps", bank)])
                fi = 2 + (m % 2)
                self.act(Ft(fi)[:, 0:Tb], ps, AF.Tanh, [("ps", bank)], [Fk(fi)], scale=0.5)
                if gi < 2:
                    self.stt(H[:, 3, c, 0:Tb], Ft(fi)[:, 0:Tb], 1.0, H[:, 3, c, 0:Tb], ALU.add, ALU.mult,
                             [Fk(fi), ("H", 3, c)], [("H", 3, c)])
                else:
                    self.stt(H[:, 4, c, 0:Tb], Ft(fi)[:, 0:Tb], 1.0, H[:, 4, c, 0:Tb], ALU.add, ALU.mult,
                             [Fk(fi), ("H", 4, c)], [("H", 4, c)])
                    self.tt(H[:, 1, c, 0:Tb], H[:, 4, c, 0:Tb], H[:, 3, c, 0:Tb], ALU.add,
                            [("H", 4, c), ("H", 3, c)], [("H", 1, c)], eng=("gpsimd" if DBG["pool_merge"] else "vector"))
        self.mark("wout")
        for ti in range(nT):
            self.dma("scalar", t["TM"][:Pt, ti, :], x_in[pos0 + ti * Pt:pos0 + (ti + 1) * Pt, :], [],
                     [("TM", ti, 0), ("TM", ti, 1)], ("x1in", ti))
        for ch in range(2):
            wsl, wk = ws.get(wsrc("out", 0, 8, ch * 512, 512), 8, 512, wkeys("out", 0, 1024, ch * 512, 512))
            for ti in range(nT):
                bank = (ch * nT + ti) % 2
                ps = self.PS[bank][:Pt, :]
                for k in range(8):
                    self.mm(ps, H[:, 1, k, tsl(ti)], wsl[:, k, :], k == 0, k == 7, [("H", 1, k), wk], [("ps", bank)])
                fi = 2 + bank
                xh = t["TM"][:Pt, ti, ch * 512:(ch + 1) * 512]
                self.tt(Ft(fi)[:Pt], ps, t["gtm_b"][:Pt, ch * 512:(ch + 1) * 512], ALU.mult, [("ps", bank), "gtm_b"],
                        [Fk(fi)])
                self.tt(xh, xh, Ft(fi)[:Pt], ALU.add, [("TM", ti, ch), Fk(fi)], [("TM", ti, ch)], eng=("gpsimd" if DBG["pool_res"] else "vector"))
        self.mark("norm2")
        for ti in range(nT):
            self.norm_to_hT(t["TM"][:Pt, ti, :], [("TM", ti, 0), ("TM", ti, 1)], Pt, t["Gf"], t["modT"][:, 24:32, :],
                            "Gf", "modT", b, 4, ti)
        self.mark("wgu")
        for gi in range(6):
            ncol = 512 if gi < 5 else 256
            nm = ncol // 128
            wg, wkg = ws.get(wsrc("gu", 0, 8, gi * 512, ncol), 8, ncol, wkeys("gu", 0, 1024, gi * 512, ncol))
            for m in range(nm):
                for k in range(8):
                    self.mm(self.PS[m][:, 0:Tb], wg[:, k, m * 128:(m + 1) * 128], H[:, 4, k, 0:Tb], k == 0, k == 7,
                            [wkg, ("H", 4, k)], [("ps", m)])
            wu, wku = ws.get(wsrc("gu", 0, 8, DFF + gi * 512, ncol), 8, ncol, wkeys("gu", 0, 1024, DFF + gi * 512, ncol))
            for m in range(nm):
                for k in range(8):
                    self.mm(self.PS[4 + m][:, 0:Tb], wu[:, k, m * 128:(m + 1) * 128], H[:, 4, k, 0:Tb], k == 0, k == 7,
                            [wku, ("H", 4, k)], [("ps", 4 + m)])
            for m in range(nm):
                i = gi * 4 + m
                fi = 2 + (i % 2)
                self.act(Ft(fi)[:, 0:Tb], self.PS[m][:, 0:Tb], AF.Tanh, [("ps", m)], [Fk(fi)], scale=0.5)
                self.stt(Ft(fi)[:, 0:Tb], Ft(fi)[:, 0:Tb], 1.0, self.PS[m][:, 0:Tb], ALU.add, ALU.mult,
                         [Fk(fi), ("ps", m)], [Fk(fi)])
                self.tt(H[:, 1 + i // 8, i % 8, 0:Tb], self.PS[4 + m][:, 0:Tb], Ft(fi)[:, 0:Tb], ALU.mult,
                        [("ps", 4 + m), Fk(fi)], [("H", 1 + i // 8, i % 8)])
        self.mark("wdown")
        for ch in range(2):
            for kg in range(3):
                kc = 8 if kg < 2 else 6
                w_, wk_ = ws.get(wsrc("down", kg * 1024, kc, ch * 512, 512), kc, 512, wkeys("down", kg * 1024, kc * 128, ch * 512, 512))
                for ti in range(nT):
                    for kk in range(kc):
                        i = kg * 8 + kk
                        self.mm(self.PS[4 + ti][:Pt, :], H[:, 1 + kg, kk, tsl(ti)], w_[:, kk, :], i == 0, i == 21,
                                [("H", 1 + kg, kk), wk_], [("ps", 4 + ti)])
            for ti in range(nT):
                fi = 4 + ti % 2
                xh = t["TM"][:Pt, ti, ch * 512:(ch + 1) * 512]
                self.tt(Ft(fi)[:Pt], self.PS[4 + ti][:Pt, :], t["gtf_b"][:Pt, ch * 512:(ch + 1) * 512], ALU.mult,
                        [("ps", 4 + ti), "gtf_b"], [Fk(fi)])
                self.tt(xh, xh, Ft(fi)[:Pt], ALU.add, [("TM", ti, ch), Fk(fi)], [("TM", ti, ch)], eng=("gpsimd" if DBG["pool_res"] else "vector"))
        for ti in range(nT):
            xt = t["TM"][:Pt, ti, :]
            xk = [("TM", ti, 0), ("TM", ti, 1)]
            kss, ss = self.stcol()
            jk_, jap = self.jkcol()
            self.act(jap[:Pt].broadcast_to([Pt, D]), xt, AF.Square, xk, [jk_, kss], accum_out=ss[:Pt])
            krs, rs = self.rstd_from_ss(ss, kss, D, Pt)
            self.stt(xt, xt, rs[:Pt], t["gfin_b"][:Pt], ALU.mult, ALU.mult, xk + [krs, "gfin_b"], xk)
            self.dma("scalar", y_out[s, pos0 + ti * Pt:pos0 + (ti + 1) * Pt, :], xt, xk, [], ("oy", ti))

    def attention(self, sidx, nkeys, nfull, ktiles, Tb, prompt):
        t, dr = self.t, self.dr
        H = t["H"]
        Ft = lambda i: t["Ft"][:, i, 0:Tb]
        Fk = lambda i: ("F", i)
        nrem = nkeys - nfull * 128

        def load_head(j):
            sl = j % 2
            self.dma("scalar", t["kTj"][:, sl, 0:nkeys], dr["kscr"][sidx, :, j, 0:nkeys], [("kscr", sidx)],
                     [("kTj", sl)], ("lk", sl))
            self.dma("scalar", t["Vj"][:, sl, 0:nfull, :],
                     dr["vscr"][sidx, j, 0:nfull * 128, :].rearrange("(kt p) e -> p kt e", p=128), [("vscr", sidx)],
                     [("Vj", sl)], ("lv", sl))
            if nrem:
                self.dma("scalar", t["Vj"][0:nrem, sl, nfull, :], dr["vscr"][sidx, j, nfull * 128:nkeys, :],
                         [("vscr", sidx)], [("Vj", sl)], ("lv", sl))

        load_head(0)
        for j in range(8):
            if j + 1 < 8:
                load_head(j + 1)
            sl = j % 2
            nk = len(ktiles)

            def scores(step):
                k0, ksz, qlo, diag = ktiles[step]
                for r in range(2):
                    bank = r * 2 + step % 2
                    ps = self.PS[bank][:ksz, qlo:Tb]
                    self.mm(ps, t["kTj"][r * 64:(r + 1) * 64, sl, k0:k0 + ksz], H[r * 64:(r + 1) * 64, 1, j, qlo:Tb],
                            True, True, [("kTj", sl), ("H", 1, j)], [("ps", bank)])
                    pt = t["PT"][:ksz, bank, qlo:Tb]
                    self.act(pt, ps, AF.Exp, [("ps", bank)], [("PT", bank)], scale=SCALE_A)
                    if diag:
                        self.op("vector", "memset", [], [("PT", bank)], t["PT"][64:128, bank, qlo:qlo + 64], 0.0)

            def pv(step):
                k0, ksz, qlo, diag = ktiles[step]
                kt = k0 // 128
                for r in range(2):
                    bank = r * 2 + step % 2
                    pt = t["PT"][:ksz, bank, qlo:Tb]
                    self.mm(self.PS[4 + r][:, qlo:Tb], t["Vj"][:ksz, sl, kt, :], pt, step == 0, step == nk - 1,
                            [("Vj", sl), ("PT", bank)], [("ps", 4 + r)])
                    self.mm(self.PS[6 + r][:, qlo:Tb], t["ones_bf"][:ksz, :], pt, step == 0, step == nk - 1,
                            ["ones_bf", ("PT", bank)], [("ps", 6 + r)])

            for step in range(nk + 1):
                if step < nk:
                    scores(step)
                if step >= 1:
                    pv(step - 1)
            self.act(Ft(0), self.PS[6][:, 0:Tb], AF.Ln, [("ps", 6)], [Fk(0)])
            self.act(Ft(0), Ft(0), AF.Exp, [Fk(0)], [Fk(0)], scale=-1.0)
            self.tt(Ft(1), self.PS[4][:, 0:Tb], Ft(0), ALU.mult, [("ps", 4), Fk(0)], [Fk(1)])
            self.act(Ft(4), self.PS[7][:, 0:Tb], AF.Ln, [("ps", 7)], [Fk(4)])
            self.act(Ft(4), Ft(4), AF.Exp, [Fk(4)], [Fk(4)], scale=-1.0)
            self.tt(Ft(2), self.PS[5][:, 0:Tb], Ft(4), ALU.mult, [("ps", 5), Fk(4)], [Fk(2)])
            self.stt(H[:, 2, j, 0:Tb], Ft(2), t["neglam"][:], Ft(1), ALU.mult, ALU.add, [Fk(2), Fk(1), "neglam"],
                     [("H", 2, j)])
        for j in range(8):
            sqb, sqk = (t["sq"], "sq") if j % 2 == 0 else (t["utok"], "utok")
            self.act(sqb[:, 0:Tb], H[:, 2, j, 0:Tb], AF.Square, [("H", 2, j)], [sqk])
            bank = j % 4
            self.mm(self.PS[bank][:, 0:Tb], t["ones_bf"][:], sqb[:, 0:Tb], True, True, ["ones_bf", sqk], [("ps", bank)])
            fi = 4 + j % 2
            self.act(Ft(fi), self.PS[bank][:, 0:Tb], AF.Ln, [("ps", bank)], [Fk(fi)], scale=1.0 / 128, bias=EPS)
            self.act(Ft(fi), Ft(fi), AF.Exp, [Fk(fi)], [Fk(fi)], scale=-0.5)
            self.stt(H[:, 2, j, 0:Tb], H[:, 2, j, 0:Tb], t["gsubs"][:], Ft(fi), ALU.mult, ALU.mult,
                     [("H", 2, j), Fk(fi), "gsubs"], [("H", 2, j)])

    def conv_chunk(self, cc, bank, Tb, dest, dest_key):
        t = self.t
        ps = self.PS[bank][:, 0:Tb]
        par = cc % 2
        raw = t["raw"][:, par, :]
        rk = ("raw", par)
        acc = t["Ft"][:, 4 + par, 0:Tb]
        ak = ("F", 4 + par)
        cw = t["convw"]
        self.act(raw[:, 3:3 + Tb], ps, AF.Copy, [("ps", bank)], [rk])
        self.op("vector", "tensor_copy", ["tail"], [rk], out=raw[:, 0:3], in_=t["tail"][:, cc, :])
        self.act(acc, ps, AF.Identity, [("ps", bank), "convw", "convb"], [ak], scale=cw[:, cc, 3:4],
                 bias=t["convb"][:, cc:cc + 1])
        for j in range(3):
            self.stt(acc, raw[:, j:j + Tb], cw[:, cc, j:j + 1], acc, ALU.mult, ALU.add, [rk, ak, "convw"], [ak])
        self.op("vector", "tensor_copy", [rk], ["tail"], out=t["tail"][:, cc, :], in_=raw[:, Tb:Tb + 3])
        th = t["Ft"][:, 2 + par, 0:Tb]
        self.act(th, acc, AF.Tanh, [ak], [("F", 2 + par)])
        self.stt(dest, th, 1.0, acc, ALU.add, ALU.mult, [("F", 2 + par), ak], [dest_key])

    def ssd(self, s, b, prompt, qb, Tb, Pt, nT):
        t, dr, ws = self.t, self.dr, self.ws
        H = t["H"]
        sm = t["sm"]
        Fk = lambda i: ("F", i)
        Ft = lambda i: t["Ft"][:, i, :]
        tsl = lambda ti: slice(ti * Pt, (ti + 1) * Pt)
        wsrc = lambda c0, nc_: dr["wb_in"][:, c0:c0 + nc_].rearrange("(c p) n -> p c n", p=128)
        for ti in range(nT):
            ps = self.PS[6]
            for k in range(8):
                self.mm(ps[:Pt, 0:32], H[:, 0, k, tsl(ti)], t["wdt"][:, k, :], k == 0, k == 7, [("H", 0, k), "wdt"],
                        [("ps", 6)])
            smk = ("sm", ti)
            v = lambda i: sm[:Pt, ti, i, :]
            self.tt(v(0), ps[:Pt, 0:32], t["dtb_b"][:Pt], ALU.add, [("ps", 6), "dtb_b"], [smk])
            self.act(v(1), v(0), AF.Exp, [smk], [smk])
            self.act(v(2), v(1), AF.Ln, [smk], [smk], bias=1.0)
            self.tt(v(3), v(2), t["a_b"][:Pt], ALU.mult, [smk, "a_b"], [smk])
            self.mm(ps[:Pt, 32:64], t["uincl"][:Pt, :Pt], v(3), True, True, ["uincl", smk], [("ps", 6)])
            self.mm(ps[:Pt, 64:96], t["ustrict"][:Pt, :Pt], v(3), True, True, ["ustrict", smk], [("ps", 6)])
            self.mm(ps[:, 96:128], t["ones_f"][:Pt, :], v(3), True, True, ["ones_f", smk], [("ps", 6)])
            self.act(sm[:Pt, ti, 4:6, :], ps[:Pt, 32:96].rearrange("p (a h) -> p a h", h=32), AF.Exp, [("ps", 6)], [smk])
            self.act(sm[:, ti, 7, :], ps[:, 96:128], AF.Exp, [("ps", 6)], [smk])
            self.tt(v(6), v(2), v(5), ALU.mult, [smk], [smk])
        for bc in range(2):
            wsl, wk = ws.get(wsrc(OFF_XBC + 2048 + bc * 512, 512), 8, 512, wkeys("in", 0, 1024, OFF_XBC + 2048 + bc * 512, 512))
            for m in range(4):
                bank = m % 2
                for k in range(8):
                    self.mm(self.PS[bank][:, 0:Tb], wsl[:, k, m * 128:(m + 1) * 128], H[:, 0, k, 0:Tb], k == 0, k == 7,
                            [wk, ("H", 0, k)], [("ps", bank)])
                self.conv_chunk(16 + bc * 4 + m, bank, Tb, H[:, 4, bc * 4 + m, 0:Tb], ("H", 4, bc * 4 + m))
        for ti in range(nT):
            pb = self.psb(7)[:Pt, 0:512].rearrange("p (g n) -> p g n", n=128)
            for g in range(4):
                self.tr(pb[:, g, :], H[:, 4, g, tsl(ti)], t["ident_bf"][:], [("H", 4, g), "ident_bf"], [("ps", 7)])
            self.act(t["Btok"][:Pt, ti, :, :], pb, AF.Copy, [("ps", 7)], [("Btok", ti)])
        for g in range(4):
            wsl, wk = ws.get(wsrc(OFF_XBC + g * 512, 512), 8, 512, wkeys("in", 0, 1024, OFF_XBC + g * 512, 512))
            for m in range(4):
                bank = m % 2
                for k in range(8):
                    self.mm(self.PS[bank][:, 0:Tb], wsl[:, k, m * 128:(m + 1) * 128], H[:, 0, k, 0:Tb], k == 0, k == 7,
                            [wk, ("H", 0, k)], [("ps", bank)])
                self.conv_chunk(g * 4 + m, bank, Tb, t["xsT"][:, m, 0:Tb], ("xsT", m))
            for ti in range(nT):
                pb = self.psb(7)[:Pt, 0:512].rearrange("p (m n) -> p m n", n=128)
                for m in range(4):
                    self.tr(pb[:, m, :], t["xsT"][:, m, tsl(ti)], t["ident_bf"][:], [("xsT", m), "ident_bf"], [("ps", 7)])
                self.act(t["xstok"][:Pt, ti, :], pb.rearrange("p m n -> p (m n)"), AF.Copy, [("ps", 7)], [("xstok", ti)])
            wz, wkz = ws.get(wsrc(OFF_Z + g * 512, 512), 8, 512, wkeys("in", 0, 1024, OFF_Z + g * 512, 512))
            for ti in range(nT):
                smk = ("sm", ti)
                v = lambda i: sm[:Pt, ti, i, g * 8:(g + 1) * 8]
                xs3 = t["xstok"][:Pt, ti, :].rearrange("p (h d) -> p h d", d=64)
                bc3 = lambda ap: ap.unsqueeze(2).broadcast_to([Pt, 8, 64])
                for k in range(8):
                    self.mm(self.PS[6][:Pt, :], H[:, 0, k, tsl(ti)], wz[:, k, :], k == 0, k == 7, [("H", 0, k), wkz],
                            [("ps", 6)])
                self.act(Ft(2)[:Pt], self.PS[6][:Pt, :], AF.Tanh, [("ps", 6)], [Fk(2)], scale=0.5)
                self.stt(Ft(2)[:Pt], Ft(2)[:Pt], 1.0, self.PS[6][:Pt, :], ALU.add, ALU.mult, [Fk(2), ("ps", 6)], [Fk(2)])
                Bmf = t["Ft"][:, 0, :].bitcast(BF16)
                Bm = Bmf[:Pt, 0:8 * Pt].rearrange("p (h l) -> p h l", l=Pt)
                nb = 2 if Pt == 128 else 1
                self.tt(Bm, v(3).unsqueeze(2).broadcast_to([Pt, 8, Pt]),
                        t["uincl"][:Pt, :Pt].unsqueeze(1).broadcast_to([Pt, 8, Pt]), ALU.mult, [smk, "uincl"],
                        [Fk(0)], eng=("gpsimd" if DBG["pool_bm"] else "vector"))
                for hb in range(nb):
                    self.mm(self.PS[hb][:Pt, :], t["ustrict_bf"][:Pt, :Pt], Bmf[:Pt, hb * 512:(hb + 1) * 512], True, True,
                            ["ustrict_bf", Fk(0)], [("ps", hb)])
                    self.act(t["E"][:Pt, hb * 512:(hb + 1) * 512], self.PS[hb][:Pt, :], AF.Exp, [("ps", hb)], [("E", hb)])
                self.mm(self.PS[2][:Pt, 0:Pt], H[:, 4, g, tsl(ti)], H[:, 4, 4 + g, tsl(ti)], True, True,
                        [("H", 4, g), ("H", 4, 4 + g)], [("ps", 2)])
                self.tt(t["cbm"][:Pt, :Pt], self.PS[2][:Pt, 0:Pt], t["causal_bf"][:Pt, :Pt], ALU.mult,
                        [("ps", 2), "causal_bf"], ["cbm"])
                E3 = t["E"][:Pt, 0:8 * Pt].rearrange("p (h l) -> p h l", l=Pt)
                M3 = t["MT"][:Pt, 0:8 * Pt].rearrange("p (h l) -> p h l", l=Pt)
                self.tt(M3, E3, t["cbm"][:Pt, :Pt].unsqueeze(1).broadcast_to([Pt, 8, Pt]), ALU.mult,
                        [("E", 0), ("E", 1), "cbm"], ["MT"])
                self.tt(t["xdt"][:Pt].rearrange("p (h d) -> p h d", d=64), xs3, bc3(v(2)), ALU.mult,
                        [("xstok", ti), smk], ["xdt"], eng=("gpsimd" if DBG["pool_x"] else "vector"))
                self.tt(t["xw"][:Pt].rearrange("p (h d) -> p h d", d=64), xs3, bc3(v(6)), ALU.mult,
                        [("xstok", ti), smk], ["xw"], eng=("gpsimd" if DBG["pool_x"] else "vector"))
                self.tt(t["xsd"][:Pt].rearrange("p (h d) -> p h d", d=64), xs3,
                        bc3(t["dskip_b"][:Pt, g * 8:(g + 1) * 8]), ALU.mult, [("xstok", ti), "dskip_b"], ["xsd"],
                        eng=("gpsimd" if DBG["pool_x"] else "vector"))
                self.mm(self.PS[3][:Pt, :], t["ident_bf"][:Pt, :Pt], t["xsd"][:Pt, :], True, False, ["ident_bf", "xsd"],
                        [("ps", 3)])
                for h in range(8):
                    self.mm(self.PS[3][:Pt, h * 64:(h + 1) * 64], M3[:, h, :], t["xdt"][:Pt, h * 64:(h + 1) * 64], False,
                            h == 7, ["MT", "xdt"], [("ps", 3)])
                self.mm(self.PS[4][:Pt, :], H[:, 4, 4 + g, tsl(ti)], t["Sb"][:, g * 512:(g + 1) * 512], True, True,
                        [("H", 4, 4 + g), ("Sb", g)], [("ps", 4)])
                self.mm(self.PS[5][:, :], t["Btok"][:Pt, ti, g, :], t["xw"][:Pt, :], True, True, [("Btok", ti), "xw"],
                        [("ps", 5)])
                self.tt(Ft(3)[:Pt].rearrange("p (h d) -> p h d", d=64),
                        self.PS[4][:Pt, :].rearrange("p (h d) -> p h d", d=64), bc3(v(4)), ALU.mult, [("ps", 4), smk],
                        [Fk(3)])
                self.tt(Ft(3)[:Pt], self.PS[3][:Pt, :], Ft(3)[:Pt], ALU.add, [("ps", 3), Fk(3)], [Fk(3)])
                self.stt(Ft(3)[:Pt], Ft(3)[:Pt], 0.5, Ft(2)[:Pt], ALU.mult, ALU.mult, [Fk(3), Fk(2)], [Fk(3)])
                kss, ss = self.stcol()
                jk_, jap = self.jkcol()
                self.act(jap[:Pt].broadcast_to([Pt, 512]), Ft(3)[:Pt], AF.Square, [Fk(3)], [jk_, kss], accum_out=ss[:Pt])
                krs, rs = self.rstd_from_ss(ss, kss, 512, Pt)
                if DBG["pool_u"]:
                    self.act(t["utok"][:Pt], Ft(3)[:Pt], AF.Copy, [Fk(3), krs], ["utok"], scale=rs[:Pt])
                else:
                    self.ts(t["utok"][:Pt], Ft(3)[:Pt], rs[:Pt], None, ALU.mult, None, [Fk(3), krs], ["utok"])
                pb = self.psb(7)[:, 0:4 * Pt].rearrange("p (m t) -> p m t", t=Pt)
                for m in range(4):
                    self.tr(pb[:, m, :], t["utok"][:Pt, m * 128:(m + 1) * 128], t["ident_bf"][:Pt, :Pt],
                            ["utok", "ident_bf"], [("ps", 7)])
                for m in range(4):
                    c = g * 4 + m
                    self.act(H[:, 1 + c // 8, c % 8, tsl(ti)], pb[:, m, :], AF.Identity, [("ps", 7), "gssd_col"],
                             [("H", 1 + c // 8, c % 8)], scale=t["gssd_col"][:, c:c + 1])
                Sg = t["S"][:, g * 512:(g + 1) * 512]
                self.tt(Sg.rearrange("p (h d) -> p h d", d=64), Sg.rearrange("p (h d) -> p h d", d=64),
                        sm[:, ti, 7, g * 8:(g + 1) * 8].unsqueeze(2).broadcast_to([128, 8, 64]), ALU.mult,
                        [("S", g), smk], [("S", g)], eng=("gpsimd" if DBG["pool_s"] else "vector"))
                self.tt(Sg, self.PS[5][:, :], Sg, ALU.add, [("ps", 5), ("S", g)], [("S", g)])
                self.act(t["Sb"][:, g * 512:(g + 1) * 512], Sg, AF.Copy, [("S", g)], [("Sb", g)])


_CACHE = {}


def _get_nc():
    if "nc" not in _CACHE:
        bld = Builder()
        p1 = Prog()
        bld.record(p1, None)
        plan = bld.ws.req
        p2 = Prog()
        bld.marks = []
        bld.record(p2, plan)
        if SCHED:
            p2.schedule()
        p2.emit(bld.nc)
        bld.es.close()
        _CACHE["nc"] = bld.nc
        _CACHE["marks"] = bld.marks
    return _CACHE["nc"]


def _consts():
    bf = ml_dtypes.bfloat16
    i = np.arange(128)
    half = 32
    inv = (10000.0 ** (-np.arange(half, dtype=np.float32) / half)).astype(np.float32)
    posp = np.arange(TP, dtype=np.float32)
    angp = posp[:, None] * inv[None, :]
    poss = (PAST + np.arange(TS)).astype(np.float32)
    angs = poss[:, None] * inv[None, :]
    to_tiles = lambda a: np.ascontiguousarray(a.reshape(16, 128, half).transpose(1, 0, 2)).astype(np.float32)
    return {
        "ident_bf": np.eye(128, dtype=np.float32).astype(bf), "ident_f": np.eye(128, dtype=np.float32),
        "ones_bf": np.ones((128, 128), np.float32).astype(bf), "ones_f": np.ones((128, 128), np.float32),
        "uincl": (i[:, None] <= i[None, :]).astype(np.float32),
        "ustrict": (i[:, None] > i[None, :]).astype(np.float32),
        "causal_bf": (i[:, None] <= i[None, :]).astype(np.float32).astype(bf),
        "cosP": to_tiles(np.cos(angp)), "sinP": to_tiles(np.sin(angp)),
        "cosS": np.cos(angs).astype(np.float32), "sinS": np.sin(angs).astype(np.float32),
    }


def kernel(x_prompt, x_sample, cache_k, cache_v, state_conv, state_ssm, c_prompt, c_sample,
           w_ada, b_ada, g_mix, g_ffn, w_in, lambda_q1, lambda_k1, lambda_q2, lambda_k2, g_subln,
           conv_w, conv_b, dt_bias, a_log, d_skip, g_ssd, w_pa, w_pb, w_out, w_gu, w_down, g_final):
    f = lambda a: np.ascontiguousarray(np.asarray(a, dtype=np.float32))
    nc = _get_nc()
    col = lambda v, nchunk: f(np.asarray(v, np.float32).reshape(nchunk, 128).T)
    shared = {
        "w_ada": f(w_ada[0]), "b_ada_col": col(b_ada[0], 48), "gmix_col": col(g_mix[0], 8), "gffn_col": col(g_ffn[0], 8),
        "w_in": f(w_in[0]),
        "lam4": f(np.concatenate([np.asarray(lambda_q1[0]), np.asarray(lambda_k1[0]), np.asarray(lambda_q2[0]),
                                  np.asarray(lambda_k2[0])])),
        "gsub_col": f(np.asarray(g_subln[0]).reshape(128, 1)),
        "convw_col": f(np.asarray(conv_w[0]).reshape(4, 24, 128).transpose(2, 1, 0)),
        "convb_col": col(conv_b[0], 24),
        "dt_bias": f(dt_bias[0]), "a_log": f(a_log[0]), "d_skip": f(d_skip[0]), "gssd_col": col(g_ssd[0], 16),
        "w_pa": f(w_pa[0]), "w_pb": f(w_pb[0]), "w_out": f(w_out[0]), "w_gu": f(w_gu[0]), "w_down": f(w_down[0]),
        "g_final": f(g_final),
    }
    shared.update(_consts())
    xp = np.asarray(x_prompt, np.float32); xs = np.asarray(x_sample, np.float32)
    ck = np.asarray(cache_k, np.float32)[0].reshape(-1, PAST, D)
    cv = np.asarray(cache_v, np.float32)[0].reshape(-1, PAST, D)
    sc = np.asarray(state_conv, np.float32)[0]
    ssm = np.asarray(state_ssm, np.float32)[0].reshape(-1, 2048, 128)
    cp = np.asarray(c_prompt, np.float32); cs = np.asarray(c_sample, np.float32)
    in_maps = []
    for c in range(NCORES):
        sl = slice(c * NSEQ, (c + 1) * NSEQ)
        call = np.concatenate([cp[sl], cs[sl]], axis=0)
        m = dict(shared)
        m.update({
            "xp": f(xp[sl]), "xs": f(xs[sl]), "ck": f(ck[sl]), "cv": f(cv[sl]),
            "scT": f(sc[sl].reshape(NSEQ, 3, 24, 128).transpose(0, 3, 2, 1)),
            "ssmT": f(ssm[sl].transpose(0, 2, 1)),
            "cT": f(call.reshape(8, 8, 128).transpose(2, 1, 0)),
        })
        in_maps.append(m)
    if "ncores" in LIM:
        in_maps = in_maps[:LIM["ncores"]]
    res = run_bass_kernel_spmd(nc, in_maps, core_ids=list(range(len(in_maps))))
    R = res.results
    if "ncores" in LIM:
        _CACHE["R"] = R
        R = [R[i % len(R)] for i in range(NCORES)]
    cat = lambda n: np.concatenate([np.asarray(r[n]) for r in R], axis=0)
    y_p = cat("y_p"); y_s = cat("y_s")
    k_p = cat("k_p").reshape(1, 32, TP, 16, 64); v_p = cat("v_p").reshape(1, 32, TP, 8, 128)
    k_s = cat("k_s").reshape(1, 32, TS, 16, 64); v_s = cat("v_s").reshape(1, 32, TS, 8, 128)
    unconv = lambda a: np.ascontiguousarray(a.transpose(0, 3, 2, 1).reshape(1, 32, 3, 3072))
    unssm = lambda a: np.ascontiguousarray(a.transpose(0, 2, 1).reshape(1, 32, 32, 64, 128))
    out = (y_p, y_s, k_p, v_p, unconv(cat("conv_p")), unssm(cat("ssm_p")), k_s, v_s, unconv(cat("conv_s")),
           unssm(cat("ssm_s")))
    return tuple(np.ascontiguousarray(o, dtype=np.float32) for o in out)
```

```python
import math
from contextlib import ExitStack

import numpy as np
import ml_dtypes

import concourse.bass as bass
import concourse.mybir as mybir
from concourse.bass_utils import run_bass_kernel_spmd

F32 = mybir.dt.float32
BF16 = mybir.dt.bfloat16
AF = mybir.ActivationFunctionType
ALU = mybir.AluOpType

NCORES = 8
NSEQ = 4
D = 1024
TP = 2048
TS = 64
PAST = 1024
NKS = PAST + TS
OFF_K, OFF_V, OFF_Z, OFF_XBC, OFF_DT, OFF_GATE, IN_COLS = 1024, 2048, 3072, 5120, 8192, 8224, 10272
DFF = 2816
EPS = 1e-5
SCALE_A = 0.125
LAM_INIT = 0.8 - 0.6 * math.exp(-0.3 * 0)
NSLOT = 5
ENGS = ("sync", "scalar", "vector", "gpsimd", "tensor")
STOP_AFTER = 99
CONST_KEYS = frozenset(("ident_bf", "ident_f", "ones_bf", "ones_f", "uincl", "ustrict", "ustrict_bf", "causal_bf",
                        "cosP", "sinP", "cosS", "sinS", "b_ada_col", "gmix_col", "gffn_col", "gsub_col", "gssd_col",
                        "convw", "convb", "dtb_b", "alog_b", "dskip_b", "a_b", "gfin_b", "neglam", "gsubs", "mhalf",
                        "modT", "Gm", "Gf", "wdt", "lams", "scT", "cTs"))
SCHED = True
RAW_ONLY = False
RAW_KEEP = ()
LIM = {"np": NSEQ, "ns": NSEQ, "nb": 99, "setup_only": False}
DBG = {"kinds": "kvq", "vscr": True, "kscr": True, "vout": True, "pool_bm": True, "pool_x": True, "pool_res": True,
       "pool_merge": True, "pool_rstd": True, "pool_u": True, "pool_s": True, "pool_sq": False}


class Op:
    __slots__ = ("eng", "fn", "deps", "signal", "sig_val", "dma_slot", "dma_val", "idx", "cost", "odeps", "fin",
                 "nun", "users")

    def __init__(self, eng, fn):
        self.eng = eng
        self.fn = fn
        self.deps = []
        self.odeps = []
        self.cost = 0.3
        self.signal = False
        self.sig_val = 0
        self.dma_slot = None
        self.dma_val = 0


class Prog:
    def __init__(self):
        self.ops = {e: [] for e in ENGS}
        self.res = {}
        self.dma_cnt = {}
        self.all = []

    def add(self, eng, fn, reads=(), writes=(), dma=None, cost=0.3):
        op = Op(eng, fn)
        op.cost = cost
        op.idx = len(self.all)
        self.all.append(op)
        deps = []
        for k in reads:
            st = self.res.get(k)
            if st is not None:
                deps.extend(st[0])
                if isinstance(k, tuple) and k[0] == "ps":
                    deps.extend(r for r in st[1] if r.eng != eng)
        for k in writes:
            st = self.res.get(k)
            if st is not None and not (RAW_ONLY and not (isinstance(k, tuple) and k[0] in RAW_KEEP)):
                deps.extend(st[1] if st[1] else st[0])
        for k in reads:
            if k in CONST_KEYS:
                continue
            st = self.res.setdefault(k, [[], []])
            if dma is None:
                keep = []
                for r in st[1]:
                    if r.dma_slot is not None or r.eng != eng:
                        keep.append(r)
                    elif r is not op:
                        op.odeps.append(r)
                st[1] = keep
            st[1].append(op)
        for k in writes:
            st = self.res.setdefault(k, [[], []])
            if st[1]:
                st[0] = [op]
                st[1] = []
            else:
                if dma is None:
                    keep = []
                    for w in st[0]:
                        if w.dma_slot is not None or w.eng != eng:
                            keep.append(w)
                        elif w is not op:
                            op.odeps.append(w)
                    st[0] = keep
                st[0].append(op)
        seen = set()
        for d in deps:
            if d is op or id(d) in seen:
                continue
            seen.add(id(d))
            if d.dma_slot is None and d.eng == "tensor" and eng == "tensor" and dma is None:
                op.odeps.append(d)
                continue
            op.deps.append(d)
            if d.dma_slot is None:
                d.signal = True
        if dma is not None:
            op.dma_slot = dma
            c = self.dma_cnt.get(dma, 0) + 1
            self.dma_cnt[dma] = c
            op.dma_val = 16 * c
        self.ops[eng].append(op)
        return op

    def schedule(self, window=120):
        import heapq
        for op in self.all:
            op.fin = None
            op.users = []
        inorder = ("sync",)
        prev = None
        for op in self.ops["gpsimd"]:
            if op.dma_slot is not None:
                if prev is not None:
                    op.odeps.append(prev)
                prev = op
        for op in self.all:
            ds = op.deps + op.odeps
            op.nun = len(ds)
            for d in ds:
                d.users.append(op)
        rem = {e: list(self.ops[e]) for e in ENGS}
        ptr = {e: 0 for e in ENGS}
        win = {e: [] for e in ENGS}
        nxt = {e: 0 for e in ENGS}
        etime = {e: 0.0 for e in ENGS}
        order = {e: [] for e in ENGS}
        LAT = 0.25

        def ready_time(op):
            t = 0.0
            for d in op.deps:
                lt = d.fin + (LAT if (d.eng != op.eng or d.dma_slot is not None) else 0.08)
                if lt > t:
                    t = lt
            for d in op.odeps:
                if d.fin - d.cost * 0.5 > t:
                    t = d.fin - d.cost * 0.5
            return t

        def refill(e):
            w = win[e]
            while len(w) < window and nxt[e] < len(rem[e]):
                w.append(rem[e][nxt[e]])
                nxt[e] += 1

        total = len(self.all)
        done = 0
        for e in ENGS:
            if e not in inorder:
                refill(e)
        while done < total:
            best = None
            for e in ENGS:
                if e in inorder:
                    if ptr[e] >= len(rem[e]):
                        continue
                    op = rem[e][ptr[e]]
                    if op.nun > 0:
                        continue
                    st = max(etime[e], ready_time(op))
                    cand = (st, op.idx, e, op)
                else:
                    cand = None
                    for op in win[e]:
                        if op.nun > 0:
                            continue
                        st = max(etime[e], ready_time(op))
                        c = (st, op.idx, e, op)
                        if cand is None or c < cand:
                            cand = c
                    if cand is None:
                        continue
                if best is None or cand < best:
                    best = cand
            assert best is not None, "scheduler stuck"
            st, _, e, op = best
            if op.dma_slot is not None:
                issue = 1.0 if e == "gpsimd" else 0.15
                etime[e] = st + issue
                op.fin = st + issue + 2.0 + op.cost
            else:
                etime[e] = st + op.cost
                op.fin = st + op.cost
            order[e].append(op)
            if e in inorder:
                ptr[e] += 1
            else:
                win[e].remove(op)
                refill(e)
            for u in op.users:
                u.nun -= 1
            done += 1
        self.ops = order
        self.makespan = max(etime.values())

    def emit(self, nc):
        for e in ENGS:
            c = 0
            for op in self.ops[e]:
                if op.dma_slot is None and op.signal:
                    c += 1
                    op.sig_val = c
        with ExitStack() as es:
            esem = {e: es.enter_context(nc.semaphore("s_" + e)) for e in ENGS}
            dsem = {}
            for i, slot in enumerate(self.dma_cnt):
                dsem[slot] = es.enter_context(nc.semaphore("d%d" % i))
            block = es.enter_context(nc.Block())
            prog = self

            def body(ename):
                def _f(eng):
                    seen = {}
                    for op in prog.ops[ename]:
                        for d in op.deps:
                            if d.dma_slot is not None:
                                sem, val = dsem[d.dma_slot], d.dma_val
                            else:
                                sem, val = esem[d.eng], d.sig_val
                            if seen.get(id(sem), 0) >= val:
                                continue
                            seen[id(sem)] = val
                            eng.wait_ge(sem, val)
                        inst = op.fn(eng)
                        if op.dma_slot is not None:
                            inst.then_inc(dsem[op.dma_slot], 16)
                        elif op.signal:
                            inst.then_inc(esem[ename], 1)
                    if ename == "sync":
                        for slot, c in prog.dma_cnt.items():
                            if seen.get(id(dsem[slot]), 0) < 16 * c:
                                eng.wait_ge(dsem[slot], 16 * c)
                return _f

            block.sync(body("sync"))
            block.scalar(body("scalar"))
            block.vector(body("vector"))
            block.gpsimd(body("gpsimd"))
            block.tensor(body("tensor"))


class WStream:
    def __init__(self, P, wsl, plan):
        self.P = P
        self.wsl = wsl
        self.plan = plan
        self.req = []
        self.issued = 0

    def _issue(self, i, src, kc, ncol, skey):
        s = i % NSLOT
        dst = self.wsl[:, s, 0:kc, 0:ncol]
        self.P.add("sync", lambda e: e.dma_start(out=dst, in_=src), reads=list(skey), writes=[("w", s)],
                   dma=("w", s), cost=kc * ncol * 256 / 150e3)

    def get(self, src, kc, ncol, skey):
        i = len(self.req)
        self.req.append((src, kc, ncol, skey))
        if self.plan is None:
            self._issue(i, src, kc, ncol, skey)
        else:
            lim = min(i + NSLOT - 1, len(self.plan) - 1)
            while self.issued <= lim:
                self._issue(self.issued, *self.plan[self.issued])
                self.issued += 1
        s = i % NSLOT
        return self.wsl[:, s], ("w", s)


def wkeys(wn, r0, nrows, c0, ncols):
    return [("wb", wn, r, c) for r in range(r0 // 1024, (r0 + nrows - 1) // 1024 + 1)
            for c in range(c0 // 2048, (c0 + ncols - 1) // 2048 + 1)]


class Builder:
    def __init__(self):
        self.nc = nc = bass.Bass("TRN2", target_bir_lowering=False)
        self.es = ExitStack()
        self.t = {}
        self.dr = {}
        di = lambda n, s, d=F32: self.dr.__setitem__(n, nc.dram_tensor(n, list(s), d, kind="ExternalInput").ap())
        do = lambda n, s, d=F32: self.dr.__setitem__(n, nc.dram_tensor(n, list(s), d, kind="ExternalOutput").ap())
        ds = lambda n, s, d=BF16: self.dr.__setitem__(n, nc.dram_tensor(n, list(s), d, kind="Internal").ap())
        di("xp", [NSEQ, TP, D]); di("xs", [NSEQ, TS, D])
        di("ck", [NSEQ, PAST, D]); di("cv", [NSEQ, PAST, D])
        di("scT", [NSEQ, 128, 24, 3]); di("ssmT", [NSEQ, 128, 2048])
        di("cT", [128, 8, 8])
        di("w_ada", [D, 6 * D]); di("b_ada_col", [128, 48]); di("gmix_col", [128, 8]); di("gffn_col", [128, 8])
        di("w_in", [D, IN_COLS]); di("lam4", [4 * 64]); di("gsub_col", [128, 1])
        di("convw_col", [128, 24, 4]); di("convb_col", [128, 24])
        di("dt_bias", [32]); di("a_log", [32]); di("d_skip", [32]); di("gssd_col", [128, 16])
        di("w_pa", [D, D]); di("w_pb", [2 * D, D]); di("w_out", [D, D]); di("w_gu", [D, 2 * DFF]); di("w_down", [DFF, D])
        di("g_final", [D])
        di("ident_bf", [128, 128], BF16); di("ident_f", [128, 128]); di("ones_bf", [128, 128], BF16)
        di("ones_f", [128, 128]); di("uincl", [128, 128]); di("ustrict", [128, 128]); di("causal_bf", [128, 128], BF16)
        di("cosP", [128, 16, 32]); di("sinP", [128, 16, 32]); di("cosS", [64, 32]); di("sinS", [64, 32])
        do("y_p", [NSEQ, TP, D]); do("y_s", [NSEQ, TS, D])
        do("k_p", [NSEQ, TP, D]); do("v_p", [NSEQ, TP, D])
        do("conv_p", [NSEQ, 128, 24, 3]); do("ssm_p", [NSEQ, 128, 2048])
        do("k_s", [NSEQ, TS, D]); do("v_s", [NSEQ, TS, D])
        do("conv_s", [NSEQ, 128, 24, 3]); do("ssm_s", [NSEQ, 128, 2048])
        ds("wb_ada", [D, 6 * D]); ds("wb_in", [D, IN_COLS]); ds("wb_pa", [D, D]); ds("wb_pb", [2 * D, D])
        ds("wb_out", [D, D]); ds("wb_gu", [D, 2 * DFF]); ds("wb_down", [DFF, D])
        ds("kscr", [2 * NSEQ, 128, 8, TP]); ds("vscr", [2 * NSEQ, 8, TP, 128])

        sb = lambda n, s, d=F32: self.t.__setitem__(n, self.es.enter_context(nc.sbuf_tensor("sb_" + n, list(s), d)))
        sb("ident_bf", [128, 128], BF16); sb("ident_f", [128, 128]); sb("ones_bf", [128, 128], BF16)
        sb("ones_f", [128, 128]); sb("uincl", [128, 128]); sb("ustrict", [128, 128]); sb("causal_bf", [128, 128], BF16)
        sb("ustrict_bf", [128, 128], BF16)
        sb("cosP", [128, 16, 32]); sb("sinP", [128, 16, 32]); sb("cosS", [64, 32]); sb("sinS", [64, 32])
        sb("b_ada_col", [128, 48]); sb("gmix_col", [128, 8]); sb("gffn_col", [128, 8]); sb("gsub_col", [128, 1])
        sb("convw", [128, 24, 4]); sb("convb", [128, 24]); sb("gssd_col", [128, 16])
        sb("dtb_b", [128, 32]); sb("alog_b", [128, 32]); sb("dskip_b", [128, 32]); sb("a_b", [128, 32])
        sb("gfin_b", [128, D]); sb("lams", [128, 4])
        sb("neglam", [128, 1]); sb("gsubs", [128, 1]); sb("mhalf", [128, 1])
        sb("cTs", [128, 8, 8]); sb("scT", [128, 8, 8], BF16)
        sb("modT", [128, 48, 8]); sb("Gm", [128, 8, 8]); sb("Gf", [128, 8, 8]); sb("onep", [128, 8, 8])
        sb("gb", [128, 128]); sb("gtm_b", [128, D]); sb("gtf_b", [128, D])
        sb("wdt", [128, 8, 32], BF16)
        sb("wsl", [128, NSLOT, 8, 512], BF16)
        sb("S", [128, 2048]); sb("Sb", [128, 2048], BF16); sb("tail", [128, 24, 3])
        sb("H", [128, 5, 8, 512], BF16)
        sb("TM", [128, 4, D]); sb("TB", [128, 4, D], BF16)
        sb("kTst", [128, 2, 4, 128], BF16)
        sb("kTj", [128, 2, TP], BF16); sb("Vj", [128, 2, 16, 128], BF16)
        sb("PT", [128, 4, 512], BF16)
        sb("Ft", [128, 6, 512]); sb("sq", [128, 512], BF16)
        sb("raw", [128, 2, 516])
        sb("xsT", [128, 4, 512], BF16); sb("xstok", [128, 4, 512], BF16); sb("Btok", [128, 4, 4, 128], BF16)
        sb("E", [128, 1024], BF16); sb("MT", [128, 1024], BF16); sb("cbm", [128, 128], BF16)
        sb("xdt", [128, 2, 512], BF16); sb("xw", [128, 2, 512], BF16); sb("xsd", [128, 512], BF16); sb("utok", [128, 512], BF16)
        sb("sm", [128, 4, 8, 32])
        sb("st", [128, 16])
        sb("jk", [128, 16], BF16)
        self.t["lamv"] = self.t["Ft"][:, 0, 0:256].rearrange("p (a b) -> p a b", b=64)
        self.t["lamt"] = self.t["Ft"][:, 1, 0:128].rearrange("p (a b) -> p a b", b=64)
        self.t["junk"] = self.t["E"]
        self.PS = [self.es.enter_context(nc.psum_tensor("ps%d" % i, [128, 512], F32)) for i in range(8)]
        self.st_i = 0

    def op(self, eng, meth, reads, writes, *a, **kw):
        o = kw.get("out", a[0] if a else None)
        n = 1
        for d in o.shape[1:]:
            n *= d
        if eng == "tensor":
            if meth == "transpose":
                cost = 0.06
            else:
                r = kw["rhs"]
                n = 1
                for d in r.shape[1:]:
                    n *= d
                cost = (max(n, 64) / 2.2 + 15) * 1e-3 * (4.0 if r.dtype == F32 else 1.0)
        elif eng == "scalar":
            cost = 0.2 + 0.6e-3 * n
        elif eng == "gpsimd":
            cost = 0.25 + 1.5e-3 * n
        else:
            cost = 0.1 + 1.1e-3 * n * (8.0 if meth == "reciprocal" else (1.7 if meth == "reciprocal_approx_fast" else 1.0))
        return self.P.add(eng, lambda e: getattr(e, meth)(*a, **kw), reads, writes, cost=cost)

    def dma(self, eng, out, in_, reads, writes, slot, **kw):
        n = 1
        for d in out.shape:
            n *= d
        nbytes = n * (2 if out.dtype == BF16 else 4)
        return self.P.add(eng, lambda e: e.dma_start(out=out, in_=in_, **kw), reads, writes, dma=slot,
                          cost=nbytes / 150e3)

    def mm(self, out, lhsT, rhs, start, stop, reads, writes):
        return self.op("tensor", "matmul", reads, writes, out, lhsT=lhsT, rhs=rhs, start=start, stop=stop)

    def tr(self, out, in_, ident, reads, writes):
        return self.op("tensor", "transpose", reads, writes, out=out, in_=in_, identity=ident)

    def act(self, out, in_, func, reads, writes, **kw):
        return self.op("scalar", "activation", reads, writes, out=out, in_=in_, func=func, **kw)

    def tt(self, out, in0, in1, op, reads, writes, eng="vector"):
        return self.op(eng, "tensor_tensor", reads, writes, out=out, in0=in0, in1=in1, op=op)

    def ts(self, out, in0, s1, s2, op0, op1, reads, writes, eng="vector"):
        if op1 is None:
            return self.op(eng, "tensor_scalar", reads, writes, out=out, in0=in0, scalar1=s1, scalar2=None, op0=op0)
        return self.op(eng, "tensor_scalar", reads, writes, out=out, in0=in0, scalar1=s1, scalar2=s2, op0=op0, op1=op1)

    def stt(self, out, in0, scalar, in1, op0, op1, reads, writes):
        return self.op("vector", "scalar_tensor_tensor", reads, writes, out=out, in0=in0, scalar=scalar, in1=in1,
                       op0=op0, op1=op1)

    def stcol(self):
        i = self.st_i % 16
        self.st_i += 1
        return ("st", i), self.t["st"][:, i:i + 1]

    def jkcol(self):
        i = self.st_i % 16
        return ("jk", i), self.t["jk"][:, i:i + 1]

    def psb(self, i):
        return self.PS[i][:].bitcast(BF16)

    def record(self, P, plan):
        self.P = P
        self.ws = WStream(P, self.t["wsl"], plan)
        self.st_i = 0
        self.tmh_i = 0
        self.setup()
        if LIM["setup_only"]:
            return
        for s in range(LIM["np"]):
            self.sequence(s, prompt=True)
        for s in range(LIM["ns"]):
            self.sequence(s, prompt=False)

    def setup(self):
        t, dr = self.t, self.dr
        for n in ("ident_bf", "ident_f", "ones_bf", "ones_f", "uincl", "ustrict", "causal_bf", "cosP", "sinP",
                  "cosS", "sinS", "b_ada_col", "gmix_col", "gffn_col", "gsub_col", "gssd_col"):
            self.dma("sync", t[n][:], dr[n], [], [n], ("c", n))
        self.dma("sync", t["convw"][:], dr["convw_col"], [], ["convw"], ("c", "convw"))
        self.dma("sync", t["convb"][:], dr["convb_col"], [], ["convb"], ("c", "convb"))
        self.dma("sync", t["cTs"][:], dr["cT"], [], ["cTs"], ("c", "cTs"))
        self.dma("sync", t["dtb_b"][:], dr["dt_bias"].partition_broadcast(128), [], ["dtb_b"], ("c", "dtb"))
        self.dma("sync", t["alog_b"][:], dr["a_log"].partition_broadcast(128), [], ["alog_b"], ("c", "alog"))
        self.dma("sync", t["dskip_b"][:], dr["d_skip"].partition_broadcast(128), [], ["dskip_b"], ("c", "dskip"))
        self.dma("sync", t["gfin_b"][:], dr["g_final"].partition_broadcast(128), [], ["gfin_b"], ("c", "gfin"))
        self.dma("sync", t["Ft"][:, 0, 0:256], dr["lam4"].partition_broadcast(128), [],
                 [("F", 0)], ("c", "lamv"))
        shapes = {"ada": (D, 6 * D), "in": (D, IN_COLS), "pa": (D, D), "pb": (2 * D, D), "out": (D, D),
                  "gu": (D, 2 * DFF), "down": (DFF, D)}

        self.cast_i = 0

        def cast(wn, cols=None):
            rows, ncols = shapes[wn]
            src = dr["w_" + wn]
            dst = dr["wb_" + wn]
            for c0 in (range(0, ncols, 2048) if cols is None else cols):
                c1 = min(ncols, c0 + 2048)
                for r0 in range(0, rows, 1024):
                    r1 = min(rows, r0 + 1024)
                    k = self.cast_i
                    self.cast_i += 1
                    self.dma("gpsimd", dst[r0:r1, c0:c1], src[r0:r1, c0:c1], ([("cc", k - 2)] if k >= 2 else []),
                             [("wb", wn, r0 // 1024, c0 // 2048), ("cc", k)],
                             ("wb", wn, r0 // 1024, c0 // 2048))
        cast("ada")
        cast("in", [0, 2048])
        cast("pa")
        cast("in", [4096, 6144, 8192, 10240])
        cast("pb"); cast("out"); cast("gu"); cast("down")
        lamv = t["lamv"]
        self.tt(t["lamt"][:, 0, :], lamv[:, 0, :], lamv[:, 1, :], ALU.mult, [("F", 0)], [("F", 1)])
        self.tt(t["lamt"][:, 1, :], lamv[:, 2, :], lamv[:, 3, :], ALU.mult, [("F", 0)], [("F", 1)])
        self.op("vector", "tensor_reduce", [("F", 1)], ["lams"], out=t["lams"][:, 0:2], in_=t["lamt"],
                axis=mybir.AxisListType.X, op=ALU.add)
        self.act(t["lams"][:, 2:4], t["lams"][:, 0:2], AF.Exp, ["lams"], ["lams"])
        self.tt(t["neglam"][:], t["lams"][:, 3:4], t["lams"][:, 2:3], ALU.subtract, ["lams"], ["neglam"])
        self.ts(t["neglam"][:], t["neglam"][:], -LAM_INIT, None, ALU.add, None, ["neglam"], ["neglam"])
        self.ts(t["gsubs"][:], t["gsub_col"][:], 1.0 - LAM_INIT, None, ALU.mult, None, ["gsub_col"], ["gsubs"])
        self.op("vector", "memset", [], ["mhalf"], t["mhalf"][:], -0.5)
        self.op("vector", "tensor_copy", ["ustrict"], ["ustrict_bf"], out=t["ustrict_bf"][:], in_=t["ustrict"][:])
        self.ts(t["convw"][:], t["convw"][:], 0.5, None, ALU.mult, None, ["convw"], ["convw"])
        self.ts(t["convb"][:], t["convb"][:], 0.5, None, ALU.mult, None, ["convb"], ["convb"])
        self.act(t["a_b"][:], t["alog_b"][:], AF.Exp, ["alog_b"], ["a_b"])
        self.ts(t["a_b"][:], t["a_b"][:], -1.0, None, ALU.mult, None, ["a_b"], ["a_b"])
        self.dma("sync", t["wdt"][:], dr["wb_in"][:, OFF_DT:OFF_DT + 32].rearrange("(c p) n -> p c n", p=128),
                 wkeys("in", 0, 1024, OFF_DT, 32), ["wdt"], ("c", "wdt"))
        self.act(t["scT"][:], t["cTs"][:], AF.Silu, ["cTs"], ["scT"])
        ps = self.PS[0]
        for g in range(12):
            wsl, wk = self.ws.get(dr["wb_ada"][:, g * 512:(g + 1) * 512].rearrange("(c p) n -> p c n", p=128), 8, 512,
                                  wkeys("ada", 0, 1024, g * 512, 512))
            for m in range(4):
                c = g * 4 + m
                for k in range(8):
                    self.mm(ps[:, c * 8:(c + 1) * 8], wsl[:, k, m * 128:(m + 1) * 128], t["scT"][:, k, :],
                            k == 0, k == 7, [wk, "scT"], [("ps", 0)])
        self.tt(t["modT"][:], ps[:, 0:384].rearrange("p (c b) -> p c b", b=8),
                t["b_ada_col"][:].unsqueeze(2).broadcast_to([128, 48, 8]), ALU.add, [("ps", 0), "b_ada_col"], ["modT"])
        for nm, base, gcol in (("Gm", 8, "gmix_col"), ("Gf", 32, "gffn_col")):
            self.ts(t["onep"][:], t["modT"][:, base:base + 8, :], 1.0, None, ALU.add, None, ["modT"], ["onep"])
            self.tt(t[nm][:], t["onep"][:], t[gcol][:].unsqueeze(2).broadcast_to([128, 8, 8]), ALU.mult,
                    ["onep", gcol], [nm])

    def gate_rows(self, b):
        t = self.t
        for nm, base, bank in (("gtm_b", 16, 0), ("gtf_b", 40, 2)):
            for half in range(2):
                ps = self.PS[bank + half]
                for cc in range(4):
                    c = half * 4 + cc
                    self.ts(t["gb"][:], t["ones_f"][:], t["modT"][:, base + c, b:b + 1], None, ALU.mult, None,
                            ["ones_f", "modT"], ["gb"])
                    self.mm(ps[:, cc * 128:(cc + 1) * 128], t["gb"][:], t["ident_f"][:], True, True,
                            ["gb", "ident_f"], [("ps", bank + half)])
                self.act(t[nm][:, half * 512:(half + 1) * 512], ps[:], AF.Copy, [("ps", bank + half)], [nm],
                         scale=(0.5 if nm == "gtf_b" else 1.0))

    def rstd_from_ss(self, ss_ap, ss_key, n, Pt):
        k1, a1 = self.stcol()
        self.ts(a1[:Pt], ss_ap[:Pt], 1.0 / n, EPS, ALU.mult, ALU.add, [ss_key], [k1], eng="gpsimd")
        k2, a2 = self.stcol()
        self.tt(a2[:Pt], a1[:Pt], self.t["mhalf"][:Pt], ALU.pow, [k1, "mhalf"], [k2], eng="gpsimd")
        return k2, a2

    def norm_to_hT(self, src, src_keys, Pt, G, S, Gk, Sk, b, hbuf, t_idx):
        t = self.t
        kss, ss = self.stcol()
        jk_, jap = self.jkcol()
        self.act(jap[:Pt].broadcast_to([Pt, D]), src, AF.Square, src_keys, [jk_, kss], accum_out=ss[:Pt])
        krs, rs = self.rstd_from_ss(ss, kss, D, Pt)
        xn = t["TB"][:Pt, t_idx % 2, :]
        xk = ("TB", t_idx % 2)
        self.ts(xn, src, rs[:Pt], None, ALU.mult, None, src_keys + [krs], [xk])
        pb = self.psb(7)[:, 0:8 * Pt].rearrange("p (c t) -> p c t", t=Pt)
        for c in range(8):
            self.tr(pb[:, c, :], xn[:, c * 128:(c + 1) * 128], t["ident_bf"][:Pt, :Pt], [xk, "ident_bf"], [("ps", 7)])
        for c in range(8):
            self.act(t["H"][:, hbuf, c, t_idx * Pt:(t_idx + 1) * Pt], pb[:, c, :], AF.Identity,
                     [("ps", 7), Gk, Sk], [("H", hbuf, c)], scale=G[:, c, b:b + 1], bias=S[:, c, b:b + 1])

    def tmh(self):
        i = self.tmh_i % 4
        self.tmh_i += 1
        return ("TM", 2 + i // 2, i % 2), self.t["TM"][:, 2 + i // 2, (i % 2) * 512:(i % 2 + 1) * 512], \
            ("TB", 2 + i // 2, i % 2), self.t["TB"][:, 2 + i // 2, (i % 2) * 512:(i % 2 + 1) * 512], i

    def k_to_scratch(self, kbf, kbf_key, Pt, sidx, pos, g):
        t = self.t
        pb = self.psb(6)[:, 0:4 * Pt].rearrange("p (c t) -> p c t", t=Pt)
        for c in range(4):
            self.tr(pb[:, c, :], kbf[:, c * 128:(c + 1) * 128], t["ident_bf"][:Pt, :Pt], [kbf_key, "ident_bf"],
                    [("ps", 6)])
        i = self.kst_i % 2
        self.kst_i += 1
        st = t["kTst"][:, i, :, 0:Pt]
        self.act(st, pb, AF.Copy, [("ps", 6)], [("kTst", i)])
        self.dma("scalar", self.dr["kscr"][sidx, :, 4 * g:4 * g + 4, pos:pos + Pt], st, [("kTst", i)],
                 [("kscr", sidx)], ("kst", i))

    def sequence(self, s, prompt):
        t, dr = self.t, self.dr
        self.kst_i = 0
        b = s if prompt else NSEQ + s
        sidx = b
        T = TP if prompt else TS
        Tb = 512 if prompt else 64
        Pt = 128 if prompt else 64
        nT = Tb // Pt
        x_in = dr["xp"][s] if prompt else dr["xs"][s]
        self.gate_rows(b)
        if prompt:
            self.op("vector", "memset", [], ["tail"], t["tail"][:], 0.0)
            self.op("vector", "memset", [], [("S", g) for g in range(4)], t["S"][:], 0.0)
            self.op("vector", "memset", [], [("Sb", g) for g in range(4)], t["Sb"][:], 0.0)
        else:
            self.dma("sync", t["tail"][:], dr["scT"][s], [], ["tail"], ("c", "tail"))
            self.dma("sync", t["S"][:], dr["ssmT"][s], [], [("S", g) for g in range(4)], ("c", "S"))
            for g in range(4):
                self.act(t["Sb"][:, g * 512:(g + 1) * 512], t["S"][:, g * 512:(g + 1) * 512], AF.Copy, [("S", g)],
                         [("Sb", g)])
            for r0 in range(0, PAST, 128):
                self.dma("gpsimd", dr["vscr"][sidx, :, r0:r0 + 128, :].rearrange("j t e -> t j e"),
                         dr["cv"][s, r0:r0 + 128, :].rearrange("t (j e) -> t j e", e=128), [], [("vscr", sidx)],
                         ("pastv", sidx))
            for kt in range(PAST // 128):
                for g in range(2):
                    _, _, kb, ap, i = self.tmh()
                    self.dma("gpsimd", ap, dr["ck"][s, kt * 128:(kt + 1) * 128, g * 512:(g + 1) * 512], [], [kb],
                             ("pastk", i))
                    self.k_to_scratch(ap, kb, 128, sidx, kt * 128, g)
        for qb in range(min(T // Tb, LIM["nb"])):
            self.block(s, b, sidx, prompt, qb, Tb, Pt, nT, x_in)
        oc = dr["conv_p"] if prompt else dr["conv_s"]
        osm = dr["ssm_p"] if prompt else dr["ssm_s"]
        self.dma("scalar", oc[s], t["tail"][:], ["tail"], [], ("o", "tail"))
        self.dma("scalar", osm[s], t["S"][:], [("S", g) for g in range(4)], [], ("o", "S"))

    def mark(self, name):
        if not hasattr(self, "marks"):
            self.marks = []
        self.marks.append((name, len(self.P.ops["tensor"])))

    def block(self, s, b, sidx, prompt, qb, Tb, Pt, nT, x_in):
        t, dr, ws = self.t, self.dr, self.ws
        self.mark("blk s%d p%d qb%d norm1" % (s, prompt, qb))
        H = t["H"]
        pos0 = qb * Tb
        hk = lambda hb: [("H", hb, c) for c in range(8)]
        tsl = lambda ti: slice(ti * Pt, (ti + 1) * Pt)
        wsrc = lambda name, r0, kc, c0, nc_: dr["wb_" + name][r0:r0 + kc * 128, c0:c0 + nc_].rearrange(
            "(c p) n -> p c n", p=128)
        cosT = (lambda ti: t["cosP"][:, qb * 4 + ti, :]) if prompt else (lambda ti: t["cosS"][:, :])
        sinT = (lambda ti: t["sinP"][:, qb * 4 + ti, :]) if prompt else (lambda ti: t["sinS"][:, :])
        k_out = dr["k_p"] if prompt else dr["k_s"]
        v_out = dr["v_p"] if prompt else dr["v_s"]
        y_out = dr["y_p"] if prompt else dr["y_s"]
        Fk = lambda i: ("F", i)
        Ft = lambda i: t["Ft"][:, i, :]

        for ti in range(nT):
            xs_ = t["raw"][:].rearrange("p a b -> p (a b)")[:Pt, 0:D]
            xkeys = [("raw", 0), ("raw", 1)]
            self.dma("scalar", xs_, x_in[pos0 + ti * Pt:pos0 + (ti + 1) * Pt, :], [], xkeys, ("xin", 0))
            self.norm_to_hT(xs_, xkeys, Pt, t["Gm"], t["modT"], "Gm", "modT", b, 0, ti)
        if STOP_AFTER < 1:
            return
        self.mark("qkv")
        for grp in range(6):
            kind = "kvq"[grp // 2]
            g = grp % 2
            if kind not in DBG["kinds"]:
                continue
            c0 = {"k": OFF_K, "v": OFF_V, "q": 0}[kind] + g * 512
            wsl, wk = ws.get(wsrc("in", 0, 8, c0, 512), 8, 512, wkeys("in", 0, 1024, c0, 512))
            for ti in range(nT):
                bank = (grp * nT + ti) % 2
                ps = self.PS[bank][:Pt, :]
                pk = ("ps", bank)
                for k in range(8):
                    self.mm(ps, H[:, 0, k, tsl(ti)], wsl[:, k, :], k == 0, k == 7, [("H", 0, k), wk], [pk])
                fk, fap, bk, bap, _ = self.tmh()
                fap = fap[:Pt]
                bap = bap[:Pt]
                pos = pos0 + ti * Pt
                kpos = pos if prompt else PAST + pos
                if kind == "v":
                    self.act(fap, ps, AF.Copy, [pk], [fk])
                    self.op("vector", "tensor_copy", [fk], [bk], out=bap, in_=fap)
                    if DBG["vout"]:
                        self.dma("scalar", v_out[s, pos:pos + Pt, g * 512:(g + 1) * 512], fap, [fk], [], ("ov", fk))
                    if DBG["vscr"]:
                      self.dma("scalar", dr["vscr"][sidx, 4 * g:4 * g + 4, kpos:kpos + Pt, :].rearrange("j t e -> t j e"),
                             bap.rearrange("t (j e) -> t j e", e=128), [bk], [("vscr", sidx)], ("sv", bk))
                    continue
                p4 = ps.rearrange("t (h two d) -> t h two d", two=2, d=32)
                cs = cosT(ti)[:Pt].unsqueeze(1).unsqueeze(1).broadcast_to([Pt, 8, 2, 32])
                sn = sinT(ti)[:Pt].unsqueeze(1).broadcast_to([Pt, 8, 32])
                A = Ft(0)[:Pt].rearrange("t (h two d) -> t h two d", two=2, d=32)
                B = Ft(1)[:Pt].rearrange("t (h two d) -> t h two d", two=2, d=32)
                self.tt(A, p4, cs, ALU.mult, [pk, "cosP"], [Fk(0)])
                self.tt(B[:, :, 0, :], p4[:, :, 1, :], sn, ALU.mult, [pk, "sinP"], [Fk(1)])
                self.tt(B[:, :, 1, :], p4[:, :, 0, :], sn, ALU.mult, [pk, "sinP"], [Fk(1)])
                if kind == "k":
                    o4 = fap.rearrange("t (h two d) -> t h two d", two=2, d=32)
                    self.tt(o4[:, :, 0, :], A[:, :, 0, :], B[:, :, 0, :], ALU.subtract, [Fk(0), Fk(1)], [fk])
                    self.tt(o4[:, :, 1, :], A[:, :, 1, :], B[:, :, 1, :], ALU.add, [Fk(0), Fk(1)], [fk])
                    self.act(bap, fap, AF.Copy, [fk], [bk])
                    self.dma("scalar", k_out[s, pos:pos + Pt, g * 512:(g + 1) * 512], fap, [fk], [], ("ok", fk))
                    if DBG["kscr"]:
                        self.k_to_scratch(bap, bk, Pt, sidx, kpos, g)
                else:
                    o4 = bap.rearrange("t (h two d) -> t h two d", two=2, d=32)
                    self.tt(o4[:, :, 0, :], A[:, :, 0, :], B[:, :, 0, :], ALU.subtract, [Fk(0), Fk(1)], [bk])
                    self.tt(o4[:, :, 1, :], A[:, :, 1, :], B[:, :, 1, :], ALU.add, [Fk(0), Fk(1)], [bk])
                    pb = self.psb(6)[:, 0:4 * Pt].rearrange("p (c t) -> p c t", t=Pt)
                    for c in range(4):
                        self.tr(pb[:, c, :], bap[:, c * 128:(c + 1) * 128], t["ident_bf"][:Pt, :Pt],
                                [bk, "ident_bf"], [("ps", 6)])
                    self.act(H[:, 1, 4 * g:4 * g + 4, tsl(ti)], pb, AF.Copy, [("ps", 6)],
                             [("H", 1, 4 * g + c) for c in range(4)])
        if STOP_AFTER < 2:
            return
        self.mark("attn")
        if prompt:
            nkeys = pos0 + Tb
            ktiles = []
            for kt in range(nkeys // 128):
                i = kt - pos0 // 128
                ktiles.append((kt * 128, 128, 0 if i <= 0 else 128 * i, i >= 0))
        else:
            nkeys = NKS
            ktiles = [(kt * 128, 128, 0, False) for kt in range(PAST // 128)] + [(PAST, 64, 0, False)]
        nfull = nkeys // 128
        self.attention(sidx, nkeys, nfull, ktiles, Tb, prompt)
        if STOP_AFTER < 3:
            return
        self.mark("wpa")
        for ch in range(2):
            wsl, wk = ws.get(wsrc("pa", 0, 8, ch * 512, 512), 8, 512, wkeys("pa", 0, 1024, ch * 512, 512))
            for m in range(4):
                c = ch * 4 + m
                bank = c % 4
                ps = self.PS[bank][:, 0:Tb]
                for k in range(8):
                    self.mm(ps, wsl[:, k, m * 128:(m + 1) * 128], H[:, 2, k, 0:Tb], k == 0, k == 7,
                            [wk, ("H", 2, k)], [("ps", bank)])
                self.act(H[:, 3, c, 0:Tb], ps, AF.Copy, [("ps", bank)], [("H", 3, c)], scale=0.5)
        if STOP_AFTER < 4:
            return
        self.mark("ssd")
        self.ssd(s, b, prompt, qb, Tb, Pt, nT)
        if STOP_AFTER < 5:
            return
        self.mark("wpb")
        for ch in range(2):
            for kg in range(2):
                w_, wk_ = ws.get(wsrc("pb", kg * 1024, 8, ch * 512, 512), 8, 512, wkeys("pb", kg * 1024, 1024, ch * 512, 512))
                for m in range(4):
                    for kk in range(8):
                        k = kg * 8 + kk
                        self.mm(self.PS[m][:, 0:Tb], w_[:, kk, m * 128:(m + 1) * 128], H[:, 1 + kg, kk, 0:Tb], k == 0,
                                k == 15, [wk_, ("H", 1 + kg, kk)], [("ps", m)])
            for m in range(4):
                c = ch * 4 + m
                self.act(H[:, 4, c, 0:Tb], self.PS[m][:, 0:Tb], AF.Copy, [("ps", m)], [("H", 4, c)], scale=0.5)
        self.mark("gates")
        for gi in range(4):
            wsl, wk = ws.get(wsrc("in", 0, 8, OFF_GATE + gi * 512, 512), 8, 512, wkeys("in", 0, 1024, OFF_GATE + gi * 512, 512))
            for m in range(4):
                c = (gi % 2) * 4 + m
                bank = (gi * 4 + m) % 4
                ps = self.PS[bank][:, 0:Tb]
                for k in range(8):
                    self.mm(ps, wsl[:, k, m * 128:(m + 1) * 128], H[:, 0, k, 0:Tb], k == 0, k == 7,
                            [wk, ("H", 0, k)], [("ps", bank)])
                fi = 2 + (m % 2)
                self.act(Ft(fi)[:, 0:Tb], ps, AF.Tanh, [("ps", bank)], [Fk(fi)], scale=0.5)
                if gi < 2:
                    self.stt(H[:, 3, c, 0:Tb], Ft(fi)[:, 0:Tb], 1.0, H[:, 3, c, 0:Tb], ALU.add, ALU.mult,
                             [Fk(fi), ("H", 3, c)], [("H", 3, c)])
                else:
                    self.stt(H[:, 4, c, 0:Tb], Ft(fi)[:, 0:Tb], 1.0, H[:, 4, c, 0:Tb], ALU.add, ALU.mult,
                             [Fk(fi), ("H", 4, c)], [("H", 4, c)])
                    self.tt(H[:, 1, c, 0:Tb], H[:, 4, c, 0:Tb], H[:, 3, c, 0:Tb], ALU.add,
                            [("H", 4, c), ("H", 3, c)], [("H", 1, c)], eng=("gpsimd" if DBG["pool_merge"] else "vector"))
        self.mark("wout")
        for ti in range(nT):
            self.dma("scalar", t["TM"][:Pt, ti, :], x_in[pos0 + ti * Pt:pos0 + (ti + 1) * Pt, :], [],
                     [("TM", ti, 0), ("TM", ti, 1)], ("x1in", ti))
        for ch in range(2):
            wsl, wk = ws.get(wsrc("out", 0, 8, ch * 512, 512), 8, 512, wkeys("out", 0, 1024, ch * 512, 512))
            for ti in range(nT):
                bank = (ch * nT + ti) % 2
                ps = self.PS[bank][:Pt, :]
                for k in range(8):
                    self.mm(ps, H[:, 1, k, tsl(ti)], wsl[:, k, :], k == 0, k == 7, [("H", 1, k), wk], [("ps", bank)])
                fi = 2 + bank
                xh = t["TM"][:Pt, ti, ch * 512:(ch + 1) * 512]
                self.tt(Ft(fi)[:Pt], ps, t["gtm_b"][:Pt, ch * 512:(ch + 1) * 512], ALU.mult, [("ps", bank), "gtm_b"],
                        [Fk(fi)])
                self.tt(xh, xh, Ft(fi)[:Pt], ALU.add, [("TM", ti, ch), Fk(fi)], [("TM", ti, ch)], eng=("gpsimd" if DBG["pool_res"] else "vector"))
        self.mark("norm2")
        for ti in range(nT):
            self.norm_to_hT(t["TM"][:Pt, ti, :], [("TM", ti, 0), ("TM", ti, 1)], Pt, t["Gf"], t["modT"][:, 24:32, :],
                            "Gf", "modT", b, 4, ti)
        self.mark("wgu")
        for gi in range(6):
            ncol = 512 if gi < 5 else 256
            nm = ncol // 128
            wg, wkg = ws.get(wsrc("gu", 0, 8, gi * 512, ncol), 8, ncol, wkeys("gu", 0, 1024, gi * 512, ncol))
            for m in range(nm):
                for k in range(8):
                    self.mm(self.PS[m][:, 0:Tb], wg[:, k, m * 128:(m + 1) * 128], H[:, 4, k, 0:Tb], k == 0, k == 7,
                            [wkg, ("H", 4, k)], [("ps", m)])
            wu, wku = ws.get(wsrc("gu", 0, 8, DFF + gi * 512, ncol), 8, ncol, wkeys("gu", 0, 1024, DFF + gi * 512, ncol))
            for m in range(nm):
                for k in range(8):
                    self.mm(self.PS[4 + m][:, 0:Tb], wu[:, k, m * 128:(m + 1) * 128], H[:, 4, k, 0:Tb], k == 0, k == 7,
                            [wku, ("H", 4, k)], [("ps", 4 + m)])
            for m in range(nm):
                i = gi * 4 + m
                fi = 2 + (i % 2)
                self.act(Ft(fi)[:, 0:Tb], self.PS[m][:, 0:Tb], AF.Tanh, [("ps", m)], [Fk(fi)], scale=0.5)
                self.stt(Ft(fi)[:, 0:Tb], Ft(fi)[:, 0:Tb], 1.0, self.PS[m][:, 0:Tb], ALU.add, ALU.mult,
                         [Fk(fi), ("ps", m)], [Fk(fi)])
                self.tt(H[:, 1 + i // 8, i % 8, 0:Tb], self.PS[4 + m][:, 0:Tb], Ft(fi)[:, 0:Tb], ALU.mult,
                        [("ps", 4 + m), Fk(fi)], [("H", 1 + i // 8, i % 8)])
        self.mark("wdown")
        for ch in range(2):
            for kg in range(3):
                kc = 8 if kg < 2 else 6
                w_, wk_ = ws.get(wsrc("down", kg * 1024, kc, ch * 512, 512), kc, 512, wkeys("down", kg * 1024, kc * 128, ch * 512, 512))
                for ti in range(nT):
                    for kk in range(kc):
                        i = kg * 8 + kk
                        self.mm(self.PS[4 + ti][:Pt, :], H[:, 1 + kg, kk, tsl(ti)], w_[:, kk, :], i == 0, i == 21,
                                [("H", 1 + kg, kk), wk_], [("ps", 4 + ti)])
            for ti in range(nT):
                fi = 4 + ti % 2
                xh = t["TM"][:Pt, ti, ch * 512:(ch + 1) * 512]
                self.tt(Ft(fi)[:Pt], self.PS[4 + ti][:Pt, :], t["gtf_b"][:Pt, ch * 512:(ch + 1) * 512], ALU.mult,
                        [("ps", 4 + ti), "gtf_b"], [Fk(fi)])
                self.tt(xh, xh, Ft(fi)[:Pt], ALU.add, [("TM", ti, ch), Fk(fi)], [("TM", ti, ch)], eng=("gpsimd" if DBG["pool_res"] else "vector"))
        for ti in range(nT):
            xt = t["TM"][:Pt, ti, :]
            xk = [("TM", ti, 0), ("TM", ti, 1)]
            kss, ss = self.stcol()
            jk_, jap = self.jkcol()
            self.act(jap[:Pt].broadcast_to([Pt, D]), xt, AF.Square, xk, [jk_, kss], accum_out=ss[:Pt])
            krs, rs = self.rstd_from_ss(ss, kss, D, Pt)
            self.stt(xt, xt, rs[:Pt], t["gfin_b"][:Pt], ALU.mult, ALU.mult, xk + [krs, "gfin_b"], xk)
            self.dma("scalar", y_out[s, pos0 + ti * Pt:pos0 + (ti + 1) * Pt, :], xt, xk, [], ("oy", ti))

    def attention(self, sidx, nkeys, nfull, ktiles, Tb, prompt):
        t, dr = self.t, self.dr
        H = t["H"]
        Ft = lambda i: t["Ft"][:, i, 0:Tb]
        Fk = lambda i: ("F", i)
        nrem = nkeys - nfull * 128

        def load_head(j):
            sl = j % 2
            self.dma("scalar", t["kTj"][:, sl, 0:nkeys], dr["kscr"][sidx, :, j, 0:nkeys], [("kscr", sidx)],
                     [("kTj", sl)], ("lk", sl))
            self.dma("scalar", t["Vj"][:, sl, 0:nfull, :],
                     dr["vscr"][sidx, j, 0:nfull * 128, :].rearrange("(kt p) e -> p kt e", p=128), [("vscr", sidx)],
                     [("Vj", sl)], ("lv", sl))
            if nrem:
                self.dma("scalar", t["Vj"][0:nrem, sl, nfull, :], dr["vscr"][sidx, j, nfull * 128:nkeys, :],
                         [("vscr", sidx)], [("Vj", sl)], ("lv", sl))

        load_head(0)
        for j in range(8):
            if j + 1 < 8:
                load_head(j + 1)
            sl = j % 2
            nk = len(ktiles)

            def scores(step):
                k0, ksz, qlo, diag = ktiles[step]
                for r in range(2):
                    bank = r * 2 + step % 2
                    ps = self.PS[bank][:ksz, qlo:Tb]
                    self.mm(ps, t["kTj"][r * 64:(r + 1) * 64, sl, k0:k0 + ksz], H[r * 64:(r + 1) * 64, 1, j, qlo:Tb],
                            True, True, [("kTj", sl), ("H", 1, j)], [("ps", bank)])
                    pt = t["PT"][:ksz, bank, qlo:Tb]
                    self.act(pt, ps, AF.Exp, [("ps", bank)], [("PT", bank)], scale=SCALE_A)
                    if diag:
                        self.op("vector", "memset", [], [("PT", bank)], t["PT"][64:128, bank, qlo:qlo + 64], 0.0)

            def pv(step):
                k0, ksz, qlo, diag = ktiles[step]
                kt = k0 // 128
                for r in range(2):
                    bank = r * 2 + step % 2
                    pt = t["PT"][:ksz, bank, qlo:Tb]
                    self.mm(self.PS[4 + r][:, qlo:Tb], t["Vj"][:ksz, sl, kt, :], pt, step == 0, step == nk - 1,
                            [("Vj", sl), ("PT", bank)], [("ps", 4 + r)])
                    self.mm(self.PS[6 + r][:, qlo:Tb], t["ones_bf"][:ksz, :], pt, step == 0, step == nk - 1,
                            ["ones_bf", ("PT", bank)], [("ps", 6 + r)])

            for step in range(nk + 1):
                if step < nk:
                    scores(step)
                if step >= 1:
                    pv(step - 1)
            self.act(Ft(0), self.PS[6][:, 0:Tb], AF.Ln, [("ps", 6)], [Fk(0)])
            self.act(Ft(0), Ft(0), AF.Exp, [Fk(0)], [Fk(0)], scale=-1.0)
            self.tt(Ft(1), self.PS[4][:, 0:Tb], Ft(0), ALU.mult, [("ps", 4), Fk(0)], [Fk(1)])
            self.act(Ft(4), self.PS[7][:, 0:Tb], AF.Ln, [("ps", 7)], [Fk(4)])
            self.act(Ft(4), Ft(4), AF.Exp, [Fk(4)], [Fk(4)], scale=-1.0)
            self.tt(Ft(2), self.PS[5][:, 0:Tb], Ft(4), ALU.mult, [("ps", 5), Fk(4)], [Fk(2)])
            self.stt(H[:, 2, j, 0:Tb], Ft(2), t["neglam"][:], Ft(1), ALU.mult, ALU.add, [Fk(2), Fk(1), "neglam"],
                     [("H", 2, j)])
        for j in range(8):
            sqb, sqk = (t["sq"], "sq") if j % 2 == 0 else (t["utok"], "utok")
            self.act(sqb[:, 0:Tb], H[:, 2, j, 0:Tb], AF.Square, [("H", 2, j)], [sqk])
            bank = j % 4
            self.mm(self.PS[bank][:, 0:Tb], t["ones_bf"][:], sqb[:, 0:Tb], True, True, ["ones_bf", sqk], [("ps", bank)])
            fi = 4 + j % 2
            self.act(Ft(fi), self.PS[bank][:, 0:Tb], AF.Ln, [("ps", bank)], [Fk(fi)], scale=1.0 / 128, bias=EPS)
            self.act(Ft(fi), Ft(fi), AF.Exp, [Fk(fi)], [Fk(fi)], scale=-0.5)
            self.stt(H[:, 2, j, 0:Tb], H[:, 2, j, 0:Tb], t["gsubs"][:], Ft(fi), ALU.mult, ALU.mult,
                     [("H", 2, j), Fk(fi), "gsubs"], [("H", 2, j)])

    def conv_chunk(self, cc, bank, Tb, dest, dest_key):
        t = self.t
        ps = self.PS[bank][:, 0:Tb]
        par = cc % 2
        raw = t["raw"][:, par, :]
        rk = ("raw", par)
        acc = t["Ft"][:, 4 + par, 0:Tb]
        ak = ("F", 4 + par)
        cw = t["convw"]
        self.act(raw[:, 3:3 + Tb], ps, AF.Copy, [("ps", bank)], [rk])
        self.op("vector", "tensor_copy", ["tail"], [rk], out=raw[:, 0:3], in_=t["tail"][:, cc, :])
        self.act(acc, ps, AF.Identity, [("ps", bank), "convw", "convb"], [ak], scale=cw[:, cc, 3:4],
                 bias=t["convb"][:, cc:cc + 1])
        for j in range(3):
            self.stt(acc, raw[:, j:j + Tb], cw[:, cc, j:j + 1], acc, ALU.mult, ALU.add, [rk, ak, "convw"], [ak])
        self.op("vector", "tensor_copy", [rk], ["tail"], out=t["tail"][:, cc, :], in_=raw[:, Tb:Tb + 3])
        th = t["Ft"][:, 2 + par, 0:Tb]
        self.act(th, acc, AF.Tanh, [ak], [("F", 2 + par)])
        self.stt(dest, th, 1.0, acc, ALU.add, ALU.mult, [("F", 2 + par), ak], [dest_key])

    def ssd(self, s, b, prompt, qb, Tb, Pt, nT):
        t, dr, ws = self.t, self.dr, self.ws
        H = t["H"]
        sm = t["sm"]
        Fk = lambda i: ("F", i)
        Ft = lambda i: t["Ft"][:, i, :]
        tsl = lambda ti: slice(ti * Pt, (ti + 1) * Pt)
        wsrc = lambda c0, nc_: dr["wb_in"][:, c0:c0 + nc_].rearrange("(c p) n -> p c n", p=128)
        for ti in range(nT):
            ps = self.PS[6]
            for k in range(8):
                self.mm(ps[:Pt, 0:32], H[:, 0, k, tsl(ti)], t["wdt"][:, k, :], k == 0, k == 7, [("H", 0, k), "wdt"],
                        [("ps", 6)])
            smk = ("sm", ti)
            v = lambda i: sm[:Pt, ti, i, :]
            self.tt(v(0), ps[:Pt, 0:32], t["dtb_b"][:Pt], ALU.add, [("ps", 6), "dtb_b"], [smk])
            self.act(v(1), v(0), AF.Exp, [smk], [smk])
            self.act(v(2), v(1), AF.Ln, [smk], [smk], bias=1.0)
            self.tt(v(3), v(2), t["a_b"][:Pt], ALU.mult, [smk, "a_b"], [smk])
            self.mm(ps[:Pt, 32:64], t["uincl"][:Pt, :Pt], v(3), True, True, ["uincl", smk], [("ps", 6)])
            self.mm(ps[:Pt, 64:96], t["ustrict"][:Pt, :Pt], v(3), True, True, ["ustrict", smk], [("ps", 6)])
            self.mm(ps[:, 96:128], t["ones_f"][:Pt, :], v(3), True, True, ["ones_f", smk], [("ps", 6)])
            self.act(sm[:Pt, ti, 4:6, :], ps[:Pt, 32:96].rearrange("p (a h) -> p a h", h=32), AF.Exp, [("ps", 6)], [smk])
            self.act(sm[:, ti, 7, :], ps[:, 96:128], AF.Exp, [("ps", 6)], [smk])
            self.tt(v(6), v(2), v(5), ALU.mult, [smk], [smk])
        for bc in range(2):
            wsl, wk = ws.get(wsrc(OFF_XBC + 2048 + bc * 512, 512), 8, 512, wkeys("in", 0, 1024, OFF_XBC + 2048 + bc * 512, 512))
            for m in range(4):
                bank = m % 2
                for k in range(8):
                    self.mm(self.PS[bank][:, 0:Tb], wsl[:, k, m * 128:(m + 1) * 128], H[:, 0, k, 0:Tb], k == 0, k == 7,
                            [wk, ("H", 0, k)], [("ps", bank)])
                self.conv_chunk(16 + bc * 4 + m, bank, Tb, H[:, 4, bc * 4 + m, 0:Tb], ("H", 4, bc * 4 + m))
        for ti in range(nT):
            pb = self.psb(7)[:Pt, 0:512].rearrange("p (g n) -> p g n", n=128)
            for g in range(4):
                self.tr(pb[:, g, :], H[:, 4, g, tsl(ti)], t["ident_bf"][:], [("H", 4, g), "ident_bf"], [("ps", 7)])
            self.act(t["Btok"][:Pt, ti, :, :], pb, AF.Copy, [("ps", 7)], [("Btok", ti)])
        for g in range(4):
            wsl, wk = ws.get(wsrc(OFF_XBC + g * 512, 512), 8, 512, wkeys("in", 0, 1024, OFF_XBC + g * 512, 512))
            for m in range(4):
                bank = m % 2
                for k in range(8):
                    self.mm(self.PS[bank][:, 0:Tb], wsl[:, k, m * 128:(m + 1) * 128], H[:, 0, k, 0:Tb], k == 0, k == 7,
                            [wk, ("H", 0, k)], [("ps", bank)])
                self.conv_chunk(g * 4 + m, bank, Tb, t["xsT"][:, m, 0:Tb], ("xsT", m))
            for ti in range(nT):
                pb = self.psb(7)[:Pt, 0:512].rearrange("p (m n) -> p m n", n=128)
                for m in range(4):
                    self.tr(pb[:, m, :], t["xsT"][:, m, tsl(ti)], t["ident_bf"][:], [("xsT", m), "ident_bf"], [("ps", 7)])
                self.act(t["xstok"][:Pt, ti, :], pb.rearrange("p m n -> p (m n)"), AF.Copy, [("ps", 7)], [("xstok", ti)])
            wz, wkz = ws.get(wsrc(OFF_Z + g * 512, 512), 8, 512, wkeys("in", 0, 1024, OFF_Z + g * 512, 512))
            for ti in range(nT):
                smk = ("sm", ti)
                v = lambda i: sm[:Pt, ti, i, g * 8:(g + 1) * 8]
                xs3 = t["xstok"][:Pt, ti, :].rearrange("p (h d) -> p h d", d=64)
                bc3 = lambda ap: ap.unsqueeze(2).broadcast_to([Pt, 8, 64])
                for k in range(8):
                    self.mm(self.PS[6][:Pt, :], H[:, 0, k, tsl(ti)], wz[:, k, :], k == 0, k == 7, [("H", 0, k), wkz],
                            [("ps", 6)])
                self.act(Ft(2)[:Pt], self.PS[6][:Pt, :], AF.Tanh, [("ps", 6)], [Fk(2)], scale=0.5)
                self.stt(Ft(2)[:Pt], Ft(2)[:Pt], 1.0, self.PS[6][:Pt, :], ALU.add, ALU.mult, [Fk(2), ("ps", 6)], [Fk(2)])
                bi = (g * nT + ti) % 2
                Bmf = t["Ft"][:, bi, :].bitcast(BF16)
                Bm = Bmf[:Pt, 0:8 * Pt].rearrange("p (h l) -> p h l", l=Pt)
                nb = 2 if Pt == 128 else 1
                self.tt(Bm, v(3).unsqueeze(2).broadcast_to([Pt, 8, Pt]),
                        t["uincl"][:Pt, :Pt].unsqueeze(1).broadcast_to([Pt, 8, Pt]), ALU.mult, [smk, "uincl"],
                        [Fk(bi)], eng=("gpsimd" if DBG["pool_bm"] else "vector"))
                for hb in range(nb):
                    self.mm(self.PS[hb][:Pt, :], t["ustrict_bf"][:Pt, :Pt], Bmf[:Pt, hb * 512:(hb + 1) * 512], True, True,
                            ["ustrict_bf", Fk(bi)], [("ps", hb)])
                    self.act(t["E"][:Pt, hb * 512:(hb + 1) * 512], self.PS[hb][:Pt, :], AF.Exp, [("ps", hb)], [("E", hb)])
                self.mm(self.PS[2][:Pt, 0:Pt], H[:, 4, g, tsl(ti)], H[:, 4, 4 + g, tsl(ti)], True, True,
                        [("H", 4, g), ("H", 4, 4 + g)], [("ps", 2)])
                self.tt(t["cbm"][:Pt, :Pt], self.PS[2][:Pt, 0:Pt], t["causal_bf"][:Pt, :Pt], ALU.mult,
                        [("ps", 2), "causal_bf"], ["cbm"])
                E3 = t["E"][:Pt, 0:8 * Pt].rearrange("p (h l) -> p h l", l=Pt)
                M3 = t["MT"][:Pt, 0:8 * Pt].rearrange("p (h l) -> p h l", l=Pt)
                self.tt(M3, E3, t["cbm"][:Pt, :Pt].unsqueeze(1).broadcast_to([Pt, 8, Pt]), ALU.mult,
                        [("E", 0), ("E", 1), "cbm"], ["MT"])
                self.tt(t["xdt"][:Pt, bi, :].rearrange("p (h d) -> p h d", d=64), xs3, bc3(v(2)), ALU.mult,
                        [("xstok", ti), smk], [("xdt", bi)], eng=("gpsimd" if DBG["pool_x"] else "vector"))
                self.tt(t["xw"][:Pt, bi, :].rearrange("p (h d) -> p h d", d=64), xs3, bc3(v(6)), ALU.mult,
                        [("xstok", ti), smk], [("xw", bi)], eng=("gpsimd" if DBG["pool_x"] else "vector"))
                self.tt(t["xsd"][:Pt].rearrange("p (h d) -> p h d", d=64), xs3,
                        bc3(t["dskip_b"][:Pt, g * 8:(g + 1) * 8]), ALU.mult, [("xstok", ti), "dskip_b"], ["xsd"],
                        eng=("gpsimd" if DBG["pool_x"] else "vector"))
                self.mm(self.PS[3][:Pt, :], t["ident_bf"][:Pt, :Pt], t["xsd"][:Pt, :], True, False, ["ident_bf", "xsd"],
                        [("ps", 3)])
                for h in range(8):
                    self.mm(self.PS[3][:Pt, h * 64:(h + 1) * 64], M3[:, h, :], t["xdt"][:Pt, bi, h * 64:(h + 1) * 64], False,
                            h == 7, ["MT", ("xdt", bi)], [("ps", 3)])
                self.mm(self.PS[4][:Pt, :], H[:, 4, 4 + g, tsl(ti)], t["Sb"][:, g * 512:(g + 1) * 512], True, True,
                        [("H", 4, 4 + g), ("Sb", g)], [("ps", 4)])
                self.mm(self.PS[5][:, :], t["Btok"][:Pt, ti, g, :], t["xw"][:Pt, bi, :], True, True, [("Btok", ti), ("xw", bi)],
                        [("ps", 5)])
                self.tt(Ft(3)[:Pt].rearrange("p (h d) -> p h d", d=64),
                        self.PS[4][:Pt, :].rearrange("p (h d) -> p h d", d=64), bc3(v(4)), ALU.mult, [("ps", 4), smk],
                        [Fk(3)])
                self.tt(Ft(3)[:Pt], self.PS[3][:Pt, :], Ft(3)[:Pt], ALU.add, [("ps", 3), Fk(3)], [Fk(3)])
                self.stt(Ft(3)[:Pt], Ft(3)[:Pt], 0.5, Ft(2)[:Pt], ALU.mult, ALU.mult, [Fk(3), Fk(2)], [Fk(3)])
                kss, ss = self.stcol()
                jk_, jap = self.jkcol()
                self.act(jap[:Pt].broadcast_to([Pt, 512]), Ft(3)[:Pt], AF.Square, [Fk(3)], [jk_, kss], accum_out=ss[:Pt])
                krs, rs = self.rstd_from_ss(ss, kss, 512, Pt)
                if DBG["pool_u"]:
                    self.act(t["utok"][:Pt], Ft(3)[:Pt], AF.Copy, [Fk(3), krs], ["utok"], scale=rs[:Pt])
                else:
                    self.ts(t["utok"][:Pt], Ft(3)[:Pt], rs[:Pt], None, ALU.mult, None, [Fk(3), krs], ["utok"])
                pb = self.psb(7)[:, 0:4 * Pt].rearrange("p (m t) -> p m t", t=Pt)
                for m in range(4):
                    self.tr(pb[:, m, :], t["utok"][:Pt, m * 128:(m + 1) * 128], t["ident_bf"][:Pt, :Pt],
                            ["utok", "ident_bf"], [("ps", 7)])
                for m in range(4):
                    c = g * 4 + m
                    self.act(H[:, 1 + c // 8, c % 8, tsl(ti)], pb[:, m, :], AF.Identity, [("ps", 7), "gssd_col"],
                             [("H", 1 + c // 8, c % 8)], scale=t["gssd_col"][:, c:c + 1])
                Sg = t["S"][:, g * 512:(g + 1) * 512]
                self.tt(Sg.rearrange("p (h d) -> p h d", d=64), Sg.rearrange("p (h d) -> p h d", d=64),
                        sm[:, ti, 7, g * 8:(g + 1) * 8].unsqueeze(2).broadcast_to([128, 8, 64]), ALU.mult,
                        [("S", g), smk], [("S", g)], eng=("gpsimd" if DBG["pool_s"] else "vector"))
                self.tt(Sg, self.PS[5][:, :], Sg, ALU.add, [("ps", 5), ("S", g)], [("S", g)])
                self.act(t["Sb"][:, g * 512:(g + 1) * 512], Sg, AF.Copy, [("S", g)], [("Sb", g)])


_CACHE = {}


def _get_nc():
    if "nc" not in _CACHE:
        bld = Builder()
        p1 = Prog()
        bld.record(p1, None)
        plan = bld.ws.req
        p2 = Prog()
        bld.marks = []
        bld.record(p2, plan)
        if SCHED:
            p2.schedule()
        p2.emit(bld.nc)
        bld.es.close()
        _CACHE["nc"] = bld.nc
        _CACHE["marks"] = bld.marks
    return _CACHE["nc"]


def _consts():
    bf = ml_dtypes.bfloat16
    i = np.arange(128)
    half = 32
    inv = (10000.0 ** (-np.arange(half, dtype=np.float32) / half)).astype(np.float32)
    posp = np.arange(TP, dtype=np.float32)
    angp = posp[:, None] * inv[None, :]
    poss = (PAST + np.arange(TS)).astype(np.float32)
    angs = poss[:, None] * inv[None, :]
    to_tiles = lambda a: np.ascontiguousarray(a.reshape(16, 128, half).transpose(1, 0, 2)).astype(np.float32)
    return {
        "ident_bf": np.eye(128, dtype=np.float32).astype(bf), "ident_f": np.eye(128, dtype=np.float32),
        "ones_bf": np.ones((128, 128), np.float32).astype(bf), "ones_f": np.ones((128, 128), np.float32),
        "uincl": (i[:, None] <= i[None, :]).astype(np.float32),
        "ustrict": (i[:, None] > i[None, :]).astype(np.float32),
        "causal_bf": (i[:, None] <= i[None, :]).astype(np.float32).astype(bf),
        "cosP": to_tiles(np.cos(angp)), "sinP": to_tiles(np.sin(angp)),
        "cosS": np.cos(angs).astype(np.float32), "sinS": np.sin(angs).astype(np.float32),
    }


def kernel(x_prompt, x_sample, cache_k, cache_v, state_conv, state_ssm, c_prompt, c_sample,
           w_ada, b_ada, g_mix, g_ffn, w_in, lambda_q1, lambda_k1, lambda_q2, lambda_k2, g_subln,
           conv_w, conv_b, dt_bias, a_log, d_skip, g_ssd, w_pa, w_pb, w_out, w_gu, w_down, g_final):
    f = lambda a: np.ascontiguousarray(np.asarray(a, dtype=np.float32))
    nc = _get_nc()
    col = lambda v, nchunk: f(np.asarray(v, np.float32).reshape(nchunk, 128).T)
    shared = {
        "w_ada": f(w_ada[0]), "b_ada_col": col(b_ada[0], 48), "gmix_col": col(g_mix[0], 8), "gffn_col": col(g_ffn[0], 8),
        "w_in": f(w_in[0]),
        "lam4": f(np.concatenate([np.asarray(lambda_q1[0]), np.asarray(lambda_k1[0]), np.asarray(lambda_q2[0]),
                                  np.asarray(lambda_k2[0])])),
        "gsub_col": f(np.asarray(g_subln[0]).reshape(128, 1)),
        "convw_col": f(np.asarray(conv_w[0]).reshape(4, 24, 128).transpose(2, 1, 0)),
        "convb_col": col(conv_b[0], 24),
        "dt_bias": f(dt_bias[0]), "a_log": f(a_log[0]), "d_skip": f(d_skip[0]), "gssd_col": col(g_ssd[0], 16),
        "w_pa": f(w_pa[0]), "w_pb": f(w_pb[0]), "w_out": f(w_out[0]), "w_gu": f(w_gu[0]), "w_down": f(w_down[0]),
        "g_final": f(g_final),
    }
    shared.update(_consts())
    xp = np.asarray(x_prompt, np.float32); xs = np.asarray(x_sample, np.float32)
    ck = np.asarray(cache_k, np.float32)[0].reshape(-1, PAST, D)
    cv = np.asarray(cache_v, np.float32)[0].reshape(-1, PAST, D)
    sc = np.asarray(state_conv, np.float32)[0]
    ssm = np.asarray(state_ssm, np.float32)[0].reshape(-1, 2048, 128)
    cp = np.asarray(c_prompt, np.float32); cs = np.asarray(c_sample, np.float32)
    in_maps = []
    for c in range(NCORES):
        sl = slice(c * NSEQ, (c + 1) * NSEQ)
        call = np.concatenate([cp[sl], cs[sl]], axis=0)
        m = dict(shared)
        m.update({
            "xp": f(xp[sl]), "xs": f(xs[sl]), "ck": f(ck[sl]), "cv": f(cv[sl]),
            "scT": f(sc[sl].reshape(NSEQ, 3, 24, 128).transpose(0, 3, 2, 1)),
            "ssmT": f(ssm[sl].transpose(0, 2, 1)),
            "cT": f(call.reshape(8, 8, 128).transpose(2, 1, 0)),
        })
        in_maps.append(m)
    if "ncores" in LIM:
        in_maps = in_maps[:LIM["ncores"]]
    res = run_bass_kernel_spmd(nc, in_maps, core_ids=list(range(len(in_maps))))
    R = res.results
    if "ncores" in LIM:
        _CACHE["R"] = R
        R = [R[i % len(R)] for i in range(NCORES)]
    cat = lambda n: np.concatenate([np.asarray(r[n]) for r in R], axis=0)
    y_p = cat("y_p"); y_s = cat("y_s")
    k_p = cat("k_p").reshape(1, 32, TP, 16, 64); v_p = cat("v_p").reshape(1, 32, TP, 8, 128)
    k_s = cat("k_s").reshape(1, 32, TS, 16, 64); v_s = cat("v_s").reshape(1, 32, TS, 8, 128)
    unconv = lambda a: np.ascontiguousarray(a.transpose(0, 3, 2, 1).reshape(1, 32, 3, 3072))
    unssm = lambda a: np.ascontiguousarray(a.transpose(0, 2, 1).reshape(1, 32, 32, 64, 128))
    out = (y_p, y_s, k_p, v_p, unconv(cat("conv_p")), unssm(cat("ssm_p")), k_s, v_s, unconv(cat("conv_s")),
           unssm(cat("ssm_s")))
    return tuple(np.ascontiguousarray(o, dtype=np.float32) for o in out)
```
